# Optimizing a Trainium2 kernel written in Bass

```python
import math
import jax, jax.numpy as jnp
from jax import lax
import numpy as np

D_MODEL = 1024
BATCH = 8
SEQ = 2048
DEPTH = 4
DEC_BATCH = 128
DEC_SEQ = 8
PAST_LEN = 16384
PAGE_SIZE = 128

PLE_DIM = 256
BRANCH_W = 256
N_BRANCH = 4
GLA_HEADS = 4
GLA_DK = 32
GLA_DV = 64
GLA_RANK = 16
GLA_TAU = 16.0
GDN_HEADS = 4
GDN_DK = 64
GDN_DV = 64
GDN_CONV = 4
CM_GROUPS = 4
CM_CHUNK = 128
SC_WIDTH = 3
LINEAR_CHUNK = 64
D_FF = ((8 * D_MODEL // 3 + 255) // 256) * 256
EPS = 1e-6
IN_SIZES = (GLA_HEADS * GLA_DK, GLA_HEADS * GLA_DK, GLA_HEADS * GLA_DV, GLA_HEADS * GLA_DV, GLA_RANK,
            GDN_HEADS * GDN_DK, GDN_HEADS * GDN_DK, GDN_HEADS * GDN_DV, GDN_HEADS * GDN_DV, GDN_HEADS, GDN_HEADS,
            BRANCH_W, BRANCH_W,
            BRANCH_W, BRANCH_W, BRANCH_W)
IN_WIDTH = sum(IN_SIZES)

kernel_name = 'hybrid_gla_gdn_chunkmlp_shortconv_step'


def _rmsnorm(x, g):
    xf = x.astype(jnp.float32)
    y = xf * lax.rsqrt(jnp.mean(xf * xf, axis=-1, keepdims=True) + EPS)
    return (y * g.astype(jnp.float32)).astype(x.dtype)


def _layernorm(x, g, b):
    xf = x.astype(jnp.float32)
    mu = jnp.mean(xf, axis=-1, keepdims=True)
    var = jnp.mean(jnp.square(xf - mu), axis=-1, keepdims=True)
    y = (xf - mu) * lax.rsqrt(var + EPS)
    return (y * g.astype(jnp.float32) + b.astype(jnp.float32)).astype(x.dtype)


def _l2norm(x):
    return x * lax.rsqrt(jnp.sum(x * x, axis=-1, keepdims=True) + EPS)


def _causal_dwconv(x, buf, w):
    width = w.shape[0]
    T = x.shape[1]
    xp = jnp.concatenate([buf.astype(x.dtype), x], axis=1)
    y = w[0] * xp[:, 0:T]
    for j in range(1, width):
        y = y + w[j] * xp[:, j:j + T]
    return y, xp[:, xp.shape[1] - (width - 1):]


def _split_cols(proj):
    idx = []
    acc = 0
    for s in IN_SIZES[:-1]:
        acc += s
        idx.append(acc)
    return jnp.split(proj, idx, axis=-1)


def _chunk_len(T):
    return math.gcd(T, LINEAR_CHUNK)


def _gla_chunked(q, k, v, log_a, S0):
    B, T, H, K = q.shape
    L = _chunk_len(T)
    N = T // L
    q, k, v, log_a = [a.reshape(B, N, L, H, a.shape[-1]) for a in (q, k, v, log_a)]
    b = jnp.cumsum(log_a, axis=2)
    b_last = b[:, :, -1:]
    q_d = q * jnp.exp(b)
    k_d = k * jnp.exp(-b)
    k_e = k * jnp.exp(b_last - b)
    mask = jnp.tril(jnp.ones((L, L), dtype=bool))
    A = jnp.where(mask, jnp.einsum('bnlhk,bnmhk->bnhlm', q_d, k_d), 0.0)
    o_intra = jnp.einsum('bnhlm,bnmhv->bnlhv', A, v)
    dec = jnp.exp(b[:, :, -1])
    dS = jnp.einsum('bnlhk,bnlhv->bnhkv', k_e, v)

    def step(S, inp):
        qd, dc, ds = inp
        o = jnp.einsum('blhk,bhkv->blhv', qd, S)
        return dc[..., None] * S + ds, o

    S_fin, o_inter = lax.scan(step, S0, (jnp.moveaxis(q_d, 1, 0), jnp.moveaxis(dec, 1, 0), jnp.moveaxis(dS, 1, 0)))
    o = o_intra + jnp.moveaxis(o_inter, 0, 1)
    return o.reshape(B, T, H, v.shape[-1]), S_fin


def _gdn_chunked(q, k, v, g, beta, S0):
    B, T, H, K = q.shape
    V = v.shape[-1]
    L = _chunk_len(T)
    N = T // L
    qh = jnp.transpose(q.reshape(B, N, L, H, K), (0, 1, 3, 2, 4))
    kh = jnp.transpose(k.reshape(B, N, L, H, K), (0, 1, 3, 2, 4))
    vh = jnp.transpose(v.reshape(B, N, L, H, V), (0, 1, 3, 2, 4))
    Gh = jnp.cumsum(jnp.moveaxis(g.reshape(B, N, L, H), -1, 2), axis=-1)
    bh = jnp.moveaxis(beta.reshape(B, N, L, H), -1, 2)
    incl = jnp.tril(jnp.ones((L, L), dtype=bool))
    strict = jnp.tril(jnp.ones((L, L), dtype=bool), -1)
    diff = Gh[..., :, None] - Gh[..., None, :]
    decay = jnp.where(incl, jnp.exp(jnp.where(incl, diff, 0.0)), 0.0)
    kk = jnp.einsum('bnhlk,bnhmk->bnhlm', kh, kh)
    M = jnp.eye(L, dtype=q.dtype) + jnp.where(strict, bh[..., :, None] * decay * kk, 0.0)
    U = lax.linalg.triangular_solve(M, bh[..., None] * vh, left_side=True, lower=True, unit_diagonal=True)
    W = lax.linalg.triangular_solve(M, (bh * jnp.exp(Gh))[..., None] * kh, left_side=True, lower=True, unit_diagonal=True)
    qk = jnp.einsum('bnhlk,bnhmk->bnhlm', qh, kh) * decay
    q_g = qh * jnp.exp(Gh)[..., None]
    k_end = kh * jnp.exp(Gh[..., -1:] - Gh)[..., None]
    dec_end = jnp.exp(Gh[..., -1])

    def step(S, inp):
        Wn, Un, qkn, qgn, ken, dn = inp
        u = Un - jnp.einsum('bhlk,bhkv->bhlv', Wn, S)
        o = jnp.einsum('bhlk,bhkv->bhlv', qgn, S) + jnp.einsum('bhlm,bhmv->bhlv', qkn, u)
        S = dn[..., None, None] * S + jnp.einsum('bhlk,bhlv->bhkv', ken, u)
        return S, o

    mv = lambda a: jnp.moveaxis(a, 1, 0)
    S_fin, o = lax.scan(step, S0, (mv(W), mv(U), mv(qk), mv(q_g), mv(k_end), mv(dec_end)))
    o = jnp.transpose(o, (1, 0, 3, 2, 4)).reshape(B, T, H, V)
    return o, S_fin


def _chunk_mlp(u, v, ln_g, ln_b, ws, bs):
    B, T, C = v.shape
    vn = _layernorm(v, ln_g, ln_b)
    pad = (-T) % CM_CHUNK
    N = (T + pad) // CM_CHUNK
    vp = jnp.pad(vn, ((0, 0), (0, pad), (0, 0))).reshape(B, N, CM_CHUNK, CM_GROUPS, C // CM_GROUPS)
    wm = jnp.where(jnp.tril(jnp.ones((CM_CHUNK, CM_CHUNK), dtype=bool)), ws, 0.0)
    s = jnp.einsum('gts,bnsgc->bntgc', wm, vp) + jnp.transpose(bs)[:, :, None]
    s = s.reshape(B, N * CM_CHUNK, C)[:, :T]
    return u * s, vn


def _layer(h, pe, s_gla, s_gdn, b_gdn, b_sc, lp):
    f32 = jnp.float32
    B, T, _ = h.shape
    xn = _rmsnorm(h, lp['norm_mix'])
    (g_q, g_k, g_v, g_r, g_a, d_q, d_k, d_v, d_z, d_a, d_b,
     c_u, c_v, s_h, s_b, s_c) = _split_cols(xn @ lp['w_in'])
    q = g_q.reshape(B, T, GLA_HEADS, GLA_DK).astype(f32) * (GLA_DK ** -0.5)
    k = g_k.reshape(B, T, GLA_HEADS, GLA_DK).astype(f32)
    v = g_v.reshape(B, T, GLA_HEADS, GLA_DV).astype(f32)
    log_a = jax.nn.log_sigmoid((g_a @ lp['gla_wa2'] + lp['gla_ba']).astype(f32)).reshape(B, T, GLA_HEADS, GLA_DK) / GLA_TAU
    o, s_gla_new = _gla_chunked(q, k, v, log_a, s_gla.astype(f32))
    o_gla = (_rmsnorm(o, lp['gla_norm']).reshape(B, T, BRANCH_W) * jax.nn.silu(g_r.astype(f32))).astype(h.dtype)
    qkv, b_gdn_new = _causal_dwconv(jnp.concatenate([d_q, d_k, d_v], axis=-1), b_gdn, lp['gdn_conv_w'])
    qkv = jax.nn.silu(qkv.astype(f32))
    cq, ck, cv = jnp.split(qkv, [GDN_HEADS * GDN_DK, 2 * GDN_HEADS * GDN_DK], axis=-1)
    q = _l2norm(cq.reshape(B, T, GDN_HEADS, GDN_DK)) * (GDN_DK ** -0.5)
    k = _l2norm(ck.reshape(B, T, GDN_HEADS, GDN_DK))
    v = cv.reshape(B, T, GDN_HEADS, GDN_DV)
    g = -jnp.exp(lp['gdn_a_log'].astype(f32)) * jax.nn.softplus(d_a.astype(f32) + lp['gdn_dt_bias'].astype(f32))
    beta = jax.nn.sigmoid(d_b.astype(f32))
    o, s_gdn_new = _gdn_chunked(q, k, v, g, beta, s_gdn.astype(f32))
    o_gdn = (_rmsnorm(o, lp['gdn_norm']).reshape(B, T, BRANCH_W) * jax.nn.silu(d_z.astype(f32))).astype(h.dtype)
    o_cm, vn = _chunk_mlp(jax.nn.gelu(c_u), jax.nn.gelu(c_v), lp['cm_ln_g'], lp['cm_ln_b'], lp['cm_ws'], lp['cm_bs'])
    y_sc, b_sc_new = _causal_dwconv(s_c * s_h, b_sc, lp['sc_conv_w'])
    o_sc = s_b * y_sc
    branches = jnp.stack([o_gla, o_gdn, o_cm.astype(h.dtype), o_sc], axis=2)
    gates = jax.nn.sigmoid(xn @ lp['w_gate']).reshape(B, T, N_BRANCH, D_MODEL)
    merged = jnp.sum(jnp.einsum('btgc,gcd->btgd', branches, lp['w_branch']) * gates, axis=2)
    h = h + merged @ lp['w_o']
    xf = _rmsnorm(h, lp['norm_ffn'])
    h = h + (jax.nn.silu(xf @ lp['w_ffn_gate']) * (xf @ lp['w_ffn_up'])) @ lp['w_ffn_down']
    pg = jax.nn.sigmoid(_rmsnorm(h, lp['norm_ple']) @ lp['w_ple_gate'])
    h = h + pg * (pe @ lp['w_ple'])
    return h, s_gla_new.astype(h.dtype), s_gdn_new.astype(h.dtype), b_gdn_new, b_sc_new, vn


def setup_inputs(seed: int = 0) -> dict:
    key = jax.random.key(seed)
    ks = jax.random.split(key, 40)
    f32 = jnp.float32

    def nrm(k, shape, scale):
        return jax.random.normal(k, shape, f32) * scale

    def gain(k, shape):
        return 1.0 + 0.02 * jax.random.normal(k, shape, f32)

    dt = jnp.exp(jax.random.uniform(ks[20], (DEPTH, GDN_HEADS), f32, math.log(0.001), math.log(0.1)))
    return {
        'x_prompt': nrm(ks[0], (BATCH, SEQ, D_MODEL), 1.0),
        'x_sample': nrm(ks[1], (DEC_BATCH, DEC_SEQ, D_MODEL), 1.0),
        'state_gla': nrm(ks[2], (DEPTH, DEC_BATCH, GLA_HEADS, GLA_DK, GLA_DV), 0.5),
        'state_gdn': nrm(ks[3], (DEPTH, DEC_BATCH, GDN_HEADS, GDN_DK, GDN_DV), 0.1),
        'state_gdn_conv': nrm(ks[4], (DEPTH, DEC_BATCH, GDN_CONV - 1, 2 * GDN_HEADS * GDN_DK + GDN_HEADS * GDN_DV), 1.0),
        'state_sconv': nrm(ks[5], (DEPTH, DEC_BATCH, SC_WIDTH - 1, BRANCH_W), 1.0),
        'p_prompt': nrm(ks[6], (DEPTH, BATCH, SEQ, PLE_DIM), 1.0),
        'p_sample': nrm(ks[7], (DEPTH, DEC_BATCH, DEC_SEQ, PLE_DIM), 1.0),
        'norm_mix': gain(ks[8], (DEPTH, D_MODEL)),
        'w_in': nrm(ks[9], (DEPTH, D_MODEL, IN_WIDTH), D_MODEL ** -0.5),
        'gla_wa2': nrm(ks[10], (DEPTH, GLA_RANK, GLA_HEADS * GLA_DK), GLA_RANK ** -0.5),
        'gla_ba': nrm(ks[11], (DEPTH, GLA_HEADS * GLA_DK), 0.1),
        'gla_norm': gain(ks[12], (DEPTH, GLA_DV)),
        'gdn_conv_w': nrm(ks[13], (DEPTH, GDN_CONV, 2 * GDN_HEADS * GDN_DK + GDN_HEADS * GDN_DV), GDN_CONV ** -0.5),
        'gdn_a_log': jnp.log(jax.random.uniform(ks[14], (DEPTH, GDN_HEADS), f32, 1.0, 16.0)),
        'gdn_dt_bias': dt + jnp.log(-jnp.expm1(-dt)),
        'gdn_norm': gain(ks[15], (DEPTH, GDN_DV)),
        'cm_ln_g': gain(ks[16], (DEPTH, BRANCH_W)),
        'cm_ln_b': nrm(ks[17], (DEPTH, BRANCH_W), 0.02),
        'cm_ws': nrm(ks[18], (DEPTH, CM_GROUPS, CM_CHUNK, CM_CHUNK), CM_CHUNK ** -0.5),
        'cm_bs': 1.0 + nrm(ks[19], (DEPTH, CM_GROUPS, CM_CHUNK), 0.1),
        'sc_conv_w': nrm(ks[21], (DEPTH, SC_WIDTH, BRANCH_W), SC_WIDTH ** -0.5),
        'w_gate': nrm(ks[22], (DEPTH, D_MODEL, N_BRANCH * D_MODEL), D_MODEL ** -0.5),
        'w_branch': nrm(ks[23], (DEPTH, N_BRANCH, BRANCH_W, D_MODEL), BRANCH_W ** -0.5),
        'w_o': nrm(ks[24], (DEPTH, D_MODEL, D_MODEL), D_MODEL ** -0.5),
        'norm_ffn': gain(ks[25], (DEPTH, D_MODEL)),
        'w_ffn_gate': nrm(ks[26], (DEPTH, D_MODEL, D_FF), D_MODEL ** -0.5),
        'w_ffn_up': nrm(ks[27], (DEPTH, D_MODEL, D_FF), D_MODEL ** -0.5),
        'w_ffn_down': nrm(ks[28], (DEPTH, D_FF, D_MODEL), D_FF ** -0.5),
        'norm_ple': gain(ks[29], (DEPTH, D_MODEL)),
        'w_ple_gate': nrm(ks[30], (DEPTH, D_MODEL, D_MODEL), D_MODEL ** -0.5),
        'w_ple': nrm(ks[31], (DEPTH, PLE_DIM, D_MODEL), PLE_DIM ** -0.5),
        'norm_final': gain(ks[32], (D_MODEL,)),
    }


def reference(x_prompt, x_sample, state_gla, state_gdn, state_gdn_conv, state_sconv, p_prompt, p_sample,
              norm_mix, w_in, gla_wa2, gla_ba, gla_norm, gdn_conv_w, gdn_a_log, gdn_dt_bias, gdn_norm,
              cm_ln_g, cm_ln_b, cm_ws, cm_bs, sc_conv_w, w_gate, w_branch, w_o,
              norm_ffn, w_ffn_gate, w_ffn_up, w_ffn_down, norm_ple, w_ple_gate, w_ple, norm_final):
    dt = x_prompt.dtype
    bp = x_prompt.shape[0]
    z_gla = jnp.zeros((bp, GLA_HEADS, GLA_DK, GLA_DV), dt)
    z_gdn = jnp.zeros((bp, GDN_HEADS, GDN_DK, GDN_DV), dt)
    z_gdn_buf = jnp.zeros((bp, GDN_CONV - 1, state_gdn_conv.shape[-1]), dt)
    z_sc_buf = jnp.zeros((bp, SC_WIDTH - 1, BRANCH_W), dt)
    hp, hs = x_prompt, x_sample
    gla_p, gla_s, gdn_p, gdn_s, gc_p, gc_s, sc_p, sc_s, cv_s = [], [], [], [], [], [], [], [], []
    for i in range(DEPTH):
        lp = {'norm_mix': norm_mix[i], 'w_in': w_in[i], 'gla_wa2': gla_wa2[i], 'gla_ba': gla_ba[i],
              'gla_norm': gla_norm[i], 'gdn_conv_w': gdn_conv_w[i], 'gdn_a_log': gdn_a_log[i],
              'gdn_dt_bias': gdn_dt_bias[i], 'gdn_norm': gdn_norm[i], 'cm_ln_g': cm_ln_g[i], 'cm_ln_b': cm_ln_b[i],
              'cm_ws': cm_ws[i], 'cm_bs': cm_bs[i], 'sc_conv_w': sc_conv_w[i], 'w_gate': w_gate[i],
              'w_branch': w_branch[i], 'w_o': w_o[i], 'norm_ffn': norm_ffn[i], 'w_ffn_gate': w_ffn_gate[i],
              'w_ffn_up': w_ffn_up[i], 'w_ffn_down': w_ffn_down[i], 'norm_ple': norm_ple[i],
              'w_ple_gate': w_ple_gate[i], 'w_ple': w_ple[i]}
        hp, a, b, c, d, _ = _layer(hp, p_prompt[i], z_gla, z_gdn, z_gdn_buf, z_sc_buf, lp)
        gla_p.append(a); gdn_p.append(b); gc_p.append(c); sc_p.append(d)
        hs, a, b, c, d, e = _layer(hs, p_sample[i], state_gla[i], state_gdn[i], state_gdn_conv[i], state_sconv[i], lp)
        gla_s.append(a); gdn_s.append(b); gc_s.append(c); sc_s.append(d); cv_s.append(e)
    y_prompt = _rmsnorm(hp, norm_final)
    y_sample = _rmsnorm(hs, norm_final)
    return (y_prompt, y_sample, jnp.stack(gla_p), jnp.stack(gla_s), jnp.stack(gdn_p), jnp.stack(gdn_s),
            jnp.stack(gc_p), jnp.stack(gc_s), jnp.stack(sc_p), jnp.stack(sc_s), jnp.stack(cv_s))
```

```python
import numpy as np
import concourse.bass as bass
import concourse.mybir as mybir
from concourse.bass_utils import run_bass_kernel_spmd
from contextlib import ExitStack

F32 = mybir.dt.float32
BF16 = mybir.dt.bfloat16
AF = mybir.ActivationFunctionType
ALU = mybir.AluOpType
AX = mybir.AxisListType

D = 1024
KC = 8
DFF = 2816
FC = 22
IN_W = 3096
C_GQ, C_GK, C_GV, C_GR, C_GA = 0, 128, 256, 512, 768
C_DQ, C_DK, C_DV, C_DZ, C_DA, C_DB = 784, 1040, 1296, 1552, 1808, 1812
C_CU, C_CV, C_SH, C_SB, C_SC = 1816, 2072, 2328, 2584, 2840
EPS = 1e-6
BIG = 30000.0
NSLOT = 4
SLOT_EL = 4096
KD = 8


def make_consts():
    c = {}
    idx = np.arange(128)
    c['ident'] = np.eye(128, dtype=np.float32)
    c['ones'] = np.ones((128, 128), np.float32)
    c['blk64'] = (idx[:, None] // 64 == idx[None, :] // 64).astype(np.float32)
    for mode, L in (('P', 128), ('S', 8)):
        same = (idx[:, None] // L == idx[None, :] // L)
        le = idx[:, None] <= idx[None, :]
        c['triU' + mode] = (same & le).astype(np.float32)
        c['tot' + mode] = same.astype(np.float32)
        c['bigL' + mode] = np.where(same & (idx[:, None] >= idx[None, :]), 0.0, BIG).astype(np.float32)
        c['bigU' + mode] = np.where(same & le, 0.0, BIG).astype(np.float32)
        c['strictL' + mode] = (same & (idx[:, None] > idx[None, :])).astype(np.float32)
    c['headmask'] = np.zeros((128, 128), np.float32)
    c['headmask'][:, 0:4] = (idx[:, None] // 32 == np.arange(4)[None, :])
    c['seqmask'] = np.zeros((128, 128), np.float32)
    c['seqmask'][:, 0:16] = (idx[:, None] // 8 == np.arange(16)[None, :])
    hf = np.zeros((128, 4, 128), np.float32)
    for h in range(4):
        hf[:, h, 32 * h:32 * h + 32] = 1.0
    c['hmfree'] = hf.reshape(128, 512)
    e8 = np.zeros((128, 128), np.float32)
    e8[0:8, :] = (np.arange(8)[:, None] == idx[None, :] % 8)
    c['e8'] = e8
    names = ['ident', 'ones', 'blk64', 'triUP', 'totP', 'bigLP', 'bigUP', 'strictLP',
             'triUS', 'totS', 'bigLS', 'bigUS', 'strictLS', 'headmask', 'seqmask', 'e8', 'hmfree']
    offs = {}
    o = 0
    for n in names:
        offs[n] = (o, c[n].shape[1])
        o += c[n].shape[1]
    arr = np.concatenate([c[n] for n in names], axis=1).astype(np.float32)
    return arr, offs


CONST_ARR, CONST_OFF = make_consts()


class StopBuild(Exception):
    pass


class Prog:
    ENGS = ('pe', 'act', 'dve', 'pool', 'sp')
    max_ops = None
    log = None
    names = None

    def __init__(self, nc, es):
        self.nc = nc
        self.q = {e: [] for e in self.ENGS}
        self.sem = {e: es.enter_context(nc.semaphore('s_' + e)) for e in ('pe', 'act', 'dve')}
        self.cnt = {e: 0 for e in ('pe', 'act', 'dve')}
        self.dsem = {e: [es.enter_context(nc.semaphore('d_%s%d' % (e, i))) for i in range(KD)]
                     for e in ('pool', 'sp')}
        self.dcnt = {'pool': 0, 'sp': 0}
        self.waited = {}
        self.res = {}
        self.planning = False
        self.semname = {}
        for e in self.sem:
            self.semname[id(self.sem[e])] = e
        for e in self.dsem:
            for i, s in enumerate(self.dsem[e]):
                self.semname[id(s)] = '%s%d' % (e, i)
        self.out_events = []

    def _wait(self, eng, sem, val):
        key = (eng, id(sem))
        if self.waited.get(key, 0) >= val:
            return
        self.waited[key] = val
        self.q[eng].append(lambda e, s=sem, v=val: e.wait_ge(s, v))

    SPLIT = ('ct5', 'ct7', 'ct8', 'ct9', 'ct10')

    def _expand(self, keys):
        out = []
        for k in keys:
            if k in self.SPLIT:
                out.append((k, 0))
                out.append((k, 1))
            elif isinstance(k, tuple) and len(k) == 2 and k[0] == 'xp':
                out.extend([('xp', k[1], q) for q in range(4)])
            else:
                out.append(k)
        return out

    def op(self, eng, fn, reads=(), writes=(), is_out=False, strict=False):
        if self.planning:
            return
        reads = self._expand(reads)
        writes = self._expand(writes)
        self.nops = getattr(self, 'nops', 0) + 1
        if self.max_ops is not None and self.nops > self.max_ops:
            raise StopBuild()
        if self.log is not None:
            import sys as _s
            fr = _s._getframe(2)
            self.log.append((self.nops, eng, fr.f_code.co_name, fr.f_lineno, _s._getframe(3).f_code.co_name, _s._getframe(3).f_lineno))
        deps = []
        for k in reads:
            r = self.res.get(k)
            if r and r['w']:
                deps.append(r['w'] + ('raw',))
        for k in writes:
            r = self.res.get(k)
            if r:
                if r['w']:
                    deps.append(r['w'] + ('waw',))
                for (sid, (sem, val, src)) in r['r'].items():
                    deps.append((sem, val, src, 'war'))
        for (sem, val, src, kind) in deps:
            if src == eng:
                if eng == 'pe':
                    continue
                if eng in ('act', 'dve') and kind != 'raw' and not strict:
                    continue
            self._wait(eng, sem, val)
        if eng in ('sp', 'pool'):
            i = self.dcnt[eng]
            self.dcnt[eng] += 1
            sem = self.dsem[eng][i % KD]
            val = 16 * (i // KD + 1)
            if i >= KD:
                self._wait(eng, sem, val - 16)
            self.q[eng].append(lambda e, f=fn, s=sem, n_=self.nops: self._name(n_, f(e).then_inc(s, 16)))
        else:
            self.cnt[eng] += 1
            sem = self.sem[eng]
            val = self.cnt[eng]
            self.q[eng].append(lambda e, f=fn, s=sem, n_=self.nops: self._name(n_, f(e).then_inc(s, 1)))
        ev = (sem, val, eng)
        for k in reads:
            r = self.res.setdefault(k, {'w': None, 'r': {}})
            old = r['r'].get(id(sem))
            if old is None or old[1] < val:
                r['r'][id(sem)] = ev
        for k in writes:
            self.res[k] = {'w': ev, 'r': {}}
        if is_out:
            self.out_events.append(ev)

    def _name(self, n_, ins):
        if self.names is not None:
            try:
                self.names[ins.ins.name] = n_
            except Exception:
                pass
        return ins

    def war_guard(self, eng, keys):
        if self.planning:
            return
        for k in keys:
            r = self.res.get(k)
            if r:
                for (sem, val, src) in r['r'].values():
                    self._wait(eng, sem, val)
                if r['w'] and not r['r']:
                    self._wait(eng, r['w'][0], r['w'][1])
                self.res[k] = {'w': None, 'r': {}}

    def barrier(self):
        if self.planning:
            return
        for e in ('pe', 'act', 'dve'):
            for s in ('pe', 'act', 'dve'):
                if s != e and self.cnt[s] > 0:
                    self._wait(e, self.sem[s], self.cnt[s])

    def finish(self):
        last = {}
        for (sem, val, src) in self.out_events:
            if last.get(id(sem), (None, 0))[1] < val:
                last[id(sem)] = (sem, val)
        for (sem, val) in last.values():
            self._wait('sp', sem, val)

    def emit(self, block):
        nc = self.nc
        q = self.q

        @block.tensor
        def _(e):
            for f in q['pe']:
                f(e)

        @block.scalar
        def _(e):
            for f in q['act']:
                f(e)

        @block.vector
        def _(e):
            for f in q['dve']:
                f(e)

        @block.gpsimd
        def _(e):
            for f in q['pool']:
                f(e)

        @block.sync
        def _(e):
            for f in q['sp']:
                f(e)


def build(TP=2048, TT=512, DEPTH=4, do_sample=True, max_ops=None, log=None, names=None):
    nc = bass.Bass("TRN2", target_bir_lowering=False)
    NT = TP // TT
    L = DEPTH

    def din(name, shape):
        return nc.dram_tensor(name, list(shape), F32, kind="ExternalInput").ap()

    def dout(name, shape):
        return nc.dram_tensor(name, list(shape), F32, kind="ExternalOutput").ap()

    x_p = din("x_p", [TP, D])
    x_s = din("x_s", [128, D])
    st_gla = din("st_gla", [L, 16, 128, 64])
    st_gdn = din("st_gdn", [L, 16, 4, 64, 64])
    st_gc = din("st_gc", [L, 48, 768])
    st_sc = din("st_sc", [L, 32, 256])
    p_p = din("p_p", [L, TP, 256])
    p_s = din("p_s", [L, 128, 256])
    consts_d = din("consts", list(CONST_ARR.shape))
    W = {}
    wshapes = dict(norm_mix=[L, D], w_in=[L, D, IN_W], gla_wa2=[L, 16, 128], gla_ba=[L, 128], gla_norm=[L, 64],
                   gdn_conv_w=[L, 4, 768], gdn_a_log=[L, 4], gdn_dt_bias=[L, 4], gdn_norm=[L, 64],
                   cm_ln_g=[L, 256], cm_ln_b=[L, 256], cm_ws=[L, 4, 128, 128], cm_bs=[L, 4, 128],
                   sc_conv_w=[L, 3, 256], w_gate=[L, D, 4 * D], w_branch=[L, 4, 256, D], w_o=[L, D, D],
                   norm_ffn=[L, D], w_ffn_gate=[L, D, DFF], w_ffn_up=[L, D, DFF], w_ffn_down=[L, DFF, D],
                   norm_ple=[L, D], w_ple_gate=[L, D, D], w_ple=[L, 256, D], norm_final=[D])
    for k, s in wshapes.items():
        W[k] = din(k, s)
    y_p = dout("y_p", [TP, D])
    y_s = dout("y_s", [128, D])
    o_gla_p = dout("o_gla_p", [L, 128, 64])
    o_gla_s = dout("o_gla_s", [L, 16, 128, 64])
    o_gdn_p = dout("o_gdn_p", [L, 4, 64, 64])
    o_gdn_s = dout("o_gdn_s", [L, 16, 4, 64, 64])
    o_gc_p = dout("o_gc_p", [L, 3, 768])
    o_gc_s = dout("o_gc_s", [L, 48, 768])
    o_sc_p = dout("o_sc_p", [L, 2, 256])
    o_sc_s = dout("o_sc_s", [L, 32, 256])
    o_cv_s = dout("o_cv_s", [L, 128, 256])

    es = ExitStack()
    with es:
        P = Prog(nc, es)
        P.max_ops = max_ops
        P.log = log
        P.names = names

        def sb(name, shape, dt=F32):
            return es.enter_context(nc.sbuf_tensor(name, list(shape), dt))

        def psum(name):
            return es.enter_context(nc.psum_tensor(name, [128, 512], F32))

        NTK = max(TT, 512)
        cst = sb("cst", list(CONST_ARR.shape))
        onesb = sb("onesb", [128, 128], BF16)
        blk64b = sb("blk64b", [128, 128], BF16)
        cvec = sb("cvec", [128, 128])
        cvec2 = sb("cvec2", [128, 128])
        cvec3 = sb("cvec3", [128, 16])
        vstage = sb("vstage", [128, 128])
        alog_b = sb("alog_b", [128, L * 4])
        dtb_b = sb("dtb_b", [128, L * 4])
        nega_b = sb("nega_b", [128, L * 4])
        wa2_sb = sb("wa2_sb", [17, L * 128])
        hT = sb("hT", [128, KC, NTK])
        xn = sb("xn", [128, KC, NTK], BF16)
        ring = sb("ring", [128, NSLOT, SLOT_EL], BF16)
        rstd = sb("rstd", [128, NTK])
        lnt = sb("lnt", [128, NTK])
        Sgla_p = sb("Sgla_p", [128, L, 64])
        Sgdn_p = sb("Sgdn_p", [128, L, 2, 64])
        halo_gc = sb("halo_gc", [128, L, 6, 3])
        halo_sc = sb("halo_sc", [128, L, 2, 2])
        Sgla_s = sb("Sgla_s", [128, 16, 64])
        Sgdn_s = sb("Sgdn_s", [128, 16, 2, 64])
        sqs = sb("sqs", [128, KC, NTK], BF16)
        qT = sb("qT", [128, NTK])
        kT = sb("kT", [128, NTK])
        grs = sb("grs", [128, 2, NTK], BF16)
        gaT = sb("gaT", [17, NTK])
        xp = sb("xp", [128, 6, max(NTK + 3, 176)])
        qkv = sb("qkv", [128, 6, NTK])
        dzs = sb("dzs", [128, 2, NTK], BF16)
        cug = sb("cug", [128, 2, NTK], BF16)
        shp = sb("shp", [128, 2, max(NTK + 2, 160)])
        sbb = sb("sbb", [128, 2, NTK])
        gv_tm = sb("gv_tm", [128, NTK // 128, 256])
        vn_tm = sb("vn_tm", [128, NTK // 128, 256])
        ab_tm = sb("ab_tm", [128, NTK // 128, 8])
        brT = sb("brT", [128, 4, 2, NTK], BF16)
        tmpAB = sb("tmpAB", [128, 1024])
        tmpA = tmpAB[:, 0:512]
        tmpB = tmpAB[:, 512:1024]
        tmpC = sb("tmpC", [128, NTK])
        peT = sb("peT", [128, 2, NTK], BF16)
        pe_tm = sb("pe_tm", [128, 256])
        x_tm = tmpAB
        lng_b = sb("lng_b", [128, 256])
        lnb_b = sb("lnb_b", [128, 256])
        bsT = sb("bsT", [128, 2, 128])
        wmT = sb("wmT", [128, 4, 128])
        ws8 = sb("ws8", [8, 4, 8])
        ctall = sb("ctall", [128, 12 * 512])
        ct = [ctall[:, 512 * i:512 * i + 512] for i in range(12)]
        hid = ctall[:, 0:FC * 256].bitcast(BF16).rearrange("p (f t) -> p f t", f=FC)
        ws_tm = ct[10].rearrange("p (g s) -> p g s", g=4)
        t18 = ct[11][0:8, :].rearrange("p (g s) -> p g s", g=4)
        st_tm = ctall[0:48, 8 * 512:8 * 512 + 768]
        cs = [sb("cs%d" % i, [128, 256]) for i in range(8)]
        sm = [sb("sm%d" % i, [128, 16]) for i in range(10)]
        um = sqs[:, :, :].rearrange("p k t -> p (k t)").bitcast(F32).rearrange("p (s v) -> p s v", s=8)
        UMK = [('sqs', kc) for kc in range(KC)]
        XPK = [('xp', j_) for j_ in range(6)]

        ps = [psum("ps%d" % i) for i in range(8)]

        def C(name):
            o, w = CONST_OFF[name]
            return cst[:, o:o + w]

        def mm(out, lhsT, rhs, start, stop, reads, writes):
            P.op('pe', lambda e: e.matmul(out, lhsT=lhsT, rhs=rhs, start=start, stop=stop), reads, writes)

        def tr(out, in_, n_in_part, reads, writes):
            idn = C('ident')[0:n_in_part, 0:n_in_part]
            P.op('pe', lambda e: e.transpose(out, in_, idn), list(reads) + ['cst'], writes)

        def act(out, in_, func, reads, writes, bias=0.0, scale=1.0, accum_out=None):
            if accum_out is None:
                P.op('act', lambda e: e.activation(out=out, in_=in_, func=func, bias=bias, scale=scale), reads, writes)
            else:
                P.op('act', lambda e: e.activation(out=out, in_=in_, func=func, bias=bias, scale=scale,
                                                   accum_out=accum_out), reads, writes)

        def tt(out, in0, in1, op, reads, writes, eng='dve'):
            P.op(eng, lambda e: e.tensor_tensor(out=out, in0=in0, in1=in1, op=op), reads, writes)

        def ts(out, in0, s1, s2, op0, op1, reads, writes, accum_out=None):
            if op1 is None:
                P.op('dve', lambda e: e.tensor_scalar(out=out, in0=in0, scalar1=s1, scalar2=None, op0=op0),
                     reads, writes)
            elif accum_out is None:
                P.op('dve', lambda e: e.tensor_scalar(out=out, in0=in0, scalar1=s1, scalar2=s2, op0=op0, op1=op1),
                     reads, writes)
            else:
                P.op('dve', lambda e: e.tensor_scalar(out=out, in0=in0, scalar1=s1, scalar2=s2, op0=op0, op1=op1,
                                                      accum_out=accum_out), reads, writes)

        def stt(out, in0, scalar, in1, op0, op1, reads, writes):
            P.op('dve', lambda e: e.scalar_tensor_tensor(out=out, in0=in0, scalar=scalar, in1=in1, op0=op0, op1=op1),
                 reads, writes)

        def cp(out, in_, reads, writes, eng='dve'):
            if eng == 'act':
                P.op('act', lambda e: e.copy(out=out, in_=in_), reads, writes)
            else:
                P.op('dve', lambda e: e.tensor_copy(out=out, in_=in_), reads, writes)

        def dma(out, in_, reads, writes, eng='sp', is_out=False):
            P.op(eng, lambda e: e.dma_start(out=out, in_=in_), reads, writes, is_out=is_out)

        def memset(ap, val, writes):
            P.op('dve', lambda e: e.memset(ap, val), [], writes)

        def recip(out, in_, reads, writes):
            P.op('dve', lambda e: e.reciprocal(out=out, in_=in_), reads, writes)

        plan = []
        ring_state = {'next_use': 0, 'next_load': 0}

        def issue_load(k):
            (wname, l, r0, nrows, c0, ncols) = plan[k]
            slot = k % NSLOT
            kc = nrows // 128
            src = W[wname][l] if l is not None else W[wname]
            P.war_guard('pool', [('ring', slot, p_) for p_ in range(8)])
            for a in range(0, kc, 4):
                b = min(kc, a + 4)
                dst = ring[:, slot, a * ncols:b * ncols].rearrange("p (k c) -> p k c", k=b - a)
                s_ap = src[r0 + a * 128:r0 + b * 128, c0:c0 + ncols].rearrange("(k p) c -> p k c", p=128)
                dma(dst, s_ap, [], [('ring', slot, a // 4)], eng='pool')

        def slab(wname, l, r0, nrows, c0, ncols):
            spec = (wname, l, r0, nrows, c0, ncols)
            kc = nrows // 128
            assert kc * ncols <= SLOT_EL, spec
            if P.planning:
                plan.append(spec)
                k = len(plan) - 1
            else:
                k = ring_state['next_use']
                assert plan[k] == spec, (plan[k], spec)
                ring_state['next_use'] += 1
                while ring_state['next_load'] < min(len(plan), k + NSLOT - 1):
                    issue_load(ring_state['next_load'])
                    ring_state['next_load'] += 1
            slot = k % NSLOT
            v = ring[:, slot, 0:kc * ncols].rearrange("p (k c) -> p k c", k=kc)
            return v, ('ring', slot)

        def rk(wk, kc):
            return (wk[0], wk[1], kc // 4)

        def setup():
            dma(cst[:], consts_d[:, :], [], ['cst'])
            cp(onesb[:], C('ones'), ['cst'], ['onesb'])
            cp(blk64b[:], C('blk64'), ['cst'], ['blk64b'])
            memset(vstage[:], 0.0, ['vstage'])
            for i, nm in enumerate(('norm_mix', 'norm_ffn', 'norm_ple')):
                dma(vstage[i * 32:i * 32 + L * 8, :], W[nm].rearrange("l (k p) -> (l k) p", p=128), [], ['vstage'])
            dma(vstage[96:104, :], W['norm_final'].rearrange("(k p) -> k p", p=128), [], ['vstage'])
            tr(ps[0][:, 0:128], vstage[:], 128, ['vstage'], ['ps0'])
            cp(cvec[:], ps[0][:, 0:128], ['ps0'], ['cvec'])
            memset(vstage[:], 0.0, ['vstage'])
            dma(vstage[0:L * 24, :], W['gdn_conv_w'].rearrange("l j (k p) -> (l j k) p", p=128), [], ['vstage'])
            dma(vstage[96:96 + L * 6, :], W['sc_conv_w'].rearrange("l j (k p) -> (l j k) p", p=128), [], ['vstage'])
            tr(ps[0][:, 0:128], vstage[:], 128, ['vstage'], ['ps0'])
            cp(cvec2[:], ps[0][:, 0:128], ['ps0'], ['cvec2'])
            memset(vstage[:], 0.0, ['vstage'])
            for half in range(2):
                dma(vstage[0:L, 64 * half:64 * half + 64], W['gla_norm'][:, :], [], ['vstage'])
                dma(vstage[8:8 + L, 64 * half:64 * half + 64], W['gdn_norm'][:, :], [], ['vstage'])
            tr(ps[0][:, 0:128], vstage[:], 128, ['vstage'], ['ps0'])
            cp(cvec3[:], ps[0][:, 0:16], ['ps0'], ['cvec3'])
            dma(wa2_sb[16:17, :], W['gla_ba'].rearrange("(o l) c -> o (l c)", o=1), [], ['wa2b'])
            dma(wa2_sb[0:16, :].rearrange("p (l c) -> p l c", l=L), W['gla_wa2'].rearrange("l r c -> r l c"), [], ['wa2'])
            o1, _w = CONST_OFF['ones']
            for q_ in range(0, NTK, 128):
                dma(gaT[16:17, q_:q_ + 128], consts_d[0:1, o1:o1 + 128], [], ['gaT1'])
            dma(alog_b[:], W['gdn_a_log'].rearrange("(o l) h -> o (l h)", o=1).to_broadcast([128, L * 4]), [], ['alog'])
            dma(dtb_b[:], W['gdn_dt_bias'].rearrange("(o l) h -> o (l h)", o=1).to_broadcast([128, L * 4]), [], ['dtb'])
            act(nega_b[:], alog_b[:], AF.Exp, ['alog'], ['nega'])
            ts(nega_b[:], nega_b[:], -1.0, None, ALU.mult, None, ['nega'], ['nega'])
            memset(Sgla_p[:], 0.0, ['Sgla_p'])
            memset(Sgdn_p[:], 0.0, ['Sgdn_p'])
            memset(halo_gc[:], 0.0, ['halo_gc'])
            memset(halo_sc[:], 0.0, ['halo_sc'])

        def gcol(which, l, kc):
            base = {'norm_mix': 0, 'norm_ffn': 32, 'norm_ple': 64}[which]
            return cvec[:, base + l * 8 + kc:base + l * 8 + kc + 1]

        def sq_h(kc, ntok, which):
            buf, key = (sqs, 'sqs') if which == 'sqs' else (xn, 'xn')
            act(buf[:, kc, 0:ntok], hT[:, kc, 0:ntok], AF.Square, [('h', kc)], [(key, kc)])

        def rmsnorm_to_xn(ntok, gsel, which='sqs'):
            buf, key = (sqs, 'sqs') if which == 'sqs' else (xn, 'xn')
            for kc in range(KC):
                mm(ps[0][:, 0:ntok], onesb[:], buf[:, kc, 0:ntok], kc == 0, kc == KC - 1,
                   ['onesb', (key, kc)], ['ps0'])
            act(lnt[:, 0:ntok], ps[0][:, 0:ntok], AF.Ln, ['ps0'], ['lnt'], bias=eps_col[:, 0:1], scale=1.0 / D)
            act(rstd[:, 0:ntok], lnt[:, 0:ntok], AF.Exp, ['lnt'], ['rstd'], scale=-0.5)
            for kc in range(KC):
                stt(xn[:, kc, 0:ntok], hT[:, kc, 0:ntok], gsel(kc), rstd[:, 0:ntok], ALU.mult, ALU.mult,
                    [('h', kc), 'rstd', 'cvec'], [('xn', kc)])

        eps_col = sb("eps_col", [128, 4])

        def proj_fm(psb, pskey, wv, wkey, c0, ncols, src, srckey, nkc, ntok, prow=0):
            for kc in range(nkc):
                mm(psb[prow:prow + ncols, 0:ntok], wv[:, kc, c0:c0 + ncols], src[:, kc, 0:ntok], kc == 0, kc == nkc - 1,
                   [rk(wkey, kc), (srckey, kc)], [pskey])

        def layer(l, tile):
            mode = tile['mode']
            ntok = tile['ntok']
            nseq = tile['nseq']
            Ls = ntok // nseq
            nsub = ntok // 128
            xn_r = [('xn', kc) for kc in range(KC)]

            def v3(ap2d):
                return ap2d.rearrange("p (s t) -> p s t", s=nseq)

            dma(lng_b[:], W['cm_ln_g'][l:l + 1, :].to_broadcast([128, 256]), [], ['lng'])
            dma(lnb_b[:], W['cm_ln_b'][l:l + 1, :].to_broadcast([128, 256]), [], ['lnb'])
            if mode == 'P':
                for g in range(4):
                    h2 = g % 2
                    dma(bsT[64 * h2:64 * h2 + 64, g // 2, :], W['cm_bs'][l, g:g + 1, :].to_broadcast([64, 128]), [], ['bsT'])
                dma(ws_tm, W['cm_ws'][l].rearrange("g t s -> t g s"), [], ['ct10'])
                for g in range(4):
                    tr(ps[4][:, 128 * g:128 * g + 128], ws_tm[:, g, :], 128, ['ct10'], ['ps4'])
                tt(wmT[:],
                   ps[4][:, :].rearrange("p (g t) -> p g t", g=4),
                   C('triUP').unsqueeze(1).to_broadcast([128, 4, 128]), ALU.mult, ['ps4', 'cst'], ['wmT'])
            else:
                for g in range(4):
                    h2 = g % 2
                    dma(bsT[64 * h2:64 * h2 + 64, g // 2, :].rearrange("p (s t) -> p s t", s=16),
                        W['cm_bs'][l, g:g + 1, 0:8].unsqueeze(1).to_broadcast([64, 16, 8]), [], ['bsT'])
                dma(ws8[:], W['cm_ws'][l, :, 0:8, 0:8].rearrange("g t s -> t g s"), [], ['ws8'])
                for g in range(4):
                    mm(ps[4][0:8, 128 * g:128 * g + 128], ws8[:, g, :], C('e8')[0:8, :], True, True, ['ws8', 'cst'], ['ps4'])
                cp(t18[:].rearrange("p g t -> p (g t)"), ps[4][0:8, :], ['ps4'], ['ct11'])
                for g in range(4):
                    mm(ps[5][:, 128 * g:128 * g + 128], C('e8')[0:8, :], t18[:, g, :], True, True, ['ct11', 'cst'], ['ps5'])
                tt(wmT[:], ps[5][:, :].rearrange("p (g t) -> p g t", g=4),
                   C('triUS').unsqueeze(1).to_broadcast([128, 4, 128]), ALU.mult, ['ps5', 'cst'], ['wmT'])
                dma(Sgla_s[:], st_gla[l].rearrange("s p v -> p s v"), [], ['Sgla_s'])
                for pair in range(2):
                    for h2 in range(2):
                        dma(Sgdn_s[64 * h2:64 * h2 + 64, :, pair, :], st_gdn[l, :, 2 * pair + h2].rearrange("s k v -> k s v"),
                            [], ['Sgdn_s'])
                dma(st_tm[:], st_gc[l], [], ['ct8', 'ct9'])
                for j in range(6):
                    tr(ps[4][:, 48 * j:48 * j + 48], st_tm[0:48, 128 * j:128 * j + 128], 48, ['ct8', 'ct9'], ['ps4'])
                cp(xp[:, :, 0:16 * 11].rearrange("p j (s t) -> p j s t", s=16)[:, :, :, 0:3],
                   ps[4][:, 0:288].rearrange("p (j s t) -> p j s t", j=6, s=16), ['ps4'], XPK)
                dma(st_tm[0:32, 0:256], st_sc[l], [], ['ct8', 'ct9'])
                for j in range(2):
                    tr(ps[4][:, 32 * j:32 * j + 32], st_tm[0:32, 128 * j:128 * j + 128], 32, ['ct8', 'ct9'], ['ps4'])
                cp(shp[:, :, 0:16 * 10].rearrange("p j (s t) -> p j s t", s=16)[:, :, :, 0:2],
                   ps[4][:, 0:64].rearrange("p (j s t) -> p j s t", j=2, s=16), ['ps4'], ['shp'])
            if mode == 'P':
                cp(xp[:, :, 0:3], halo_gc[:, l, :, :], ['halo_gc'], XPK)
                cp(shp[:, :, 0:2], halo_sc[:, l, :, :], ['halo_sc'], ['shp'])
            pe_src = (p_p[l, tile['t0']:tile['t0'] + ntok, :] if mode == 'P' else p_s[l])
            for s_ in range(nsub):
                dma(pe_tm[:], pe_src[128 * s_:128 * s_ + 128, :], [], ['pe_tm'])
                for j in range(2):
                    tr(ps[4][:, 128 * j:128 * j + 128], pe_tm[:, 128 * j:128 * j + 128], 128, ['pe_tm'], ['ps4'])
                cp(peT[:, :, 128 * s_:128 * s_ + 128], ps[4][:, 0:256].rearrange("p (j t) -> p j t", j=2), ['ps4'], ['peT'])

            xpv = xp[:, :, 0:nseq * (Ls + 3)].rearrange("p j (s t) -> p j s t", s=nseq)
            shv = shp[:, :, 0:nseq * (Ls + 2)].rearrange("p j (s t) -> p j s t", s=nseq)

            rmsnorm_to_xn(ntok, lambda kc: gcol('norm_mix', l, kc))

            wv, wk = slab('w_in', l, 0, D, 0, 512)
            proj_fm(ps[0], 'ps0', wv, wk, 0, 128, xn, 'xn', KC, ntok)
            cp(qT[:, 0:ntok], ps[0][:, 0:ntok], ['ps0'], ['qT'], eng='act')
            proj_fm(ps[1], 'ps1', wv, wk, 128, 128, xn, 'xn', KC, ntok)
            cp(kT[:, 0:ntok], ps[1][:, 0:ntok], ['ps1'], ['kT'])
            for s_ in range(nsub):
                pb = ps[2 + (s_ % 2)]
                pk = 'ps%d' % (2 + (s_ % 2))
                for kc in range(KC):
                    mm(pb[:, 0:256], xn[:, kc, 128 * s_:128 * s_ + 128], wv[:, kc, 256:512], kc == 0, kc == KC - 1,
                       [rk(wk, kc), ('xn', kc)], [pk])
                cp(gv_tm[:, s_, :], pb[:, 0:256], [pk], ['gv_tm'], eng='act')
            wv, wk = slab('w_in', l, 0, D, 512, 272)
            for j in range(2):
                pb, pk = ps[j], 'ps%d' % j
                proj_fm(pb, pk, wv, wk, 128 * j, 128, xn, 'xn', KC, ntok)
                act(grs[:, j, 0:ntok], pb[:, 0:ntok], AF.Silu, [pk], ['grs'])
            proj_fm(ps[2], 'ps2', wv, wk, 256, 16, xn, 'xn', KC, ntok)
            cp(gaT[0:16, 0:ntok], ps[2][0:16, 0:ntok], ['ps2'], ['gaT'])
            wv, wk = slab('w_in', l, 0, D, C_DQ, 512)
            for j in range(4):
                pb, pk = ps[j % 4], 'ps%d' % (j % 4)
                proj_fm(pb, pk, wv, wk, 128 * j, 128, xn, 'xn', KC, ntok)
                cp(xpv[:, j, :, 3:3 + Ls], v3(pb[:, 0:ntok]), [pk], [('xp', j)], eng=('act' if j % 2 else 'dve'))
            wv, wk = slab('w_in', l, 0, D, C_DV, 512)
            for j in range(4):
                pb, pk = ps[j % 4], 'ps%d' % (j % 4)
                proj_fm(pb, pk, wv, wk, 128 * j, 128, xn, 'xn', KC, ntok)
                if j < 2:
                    cp(xpv[:, 4 + j, :, 3:3 + Ls], v3(pb[:, 0:ntok]), [pk], [('xp', 4 + j)], eng=('act' if j % 2 else 'dve'))
                else:
                    act(dzs[:, j - 2, 0:ntok], pb[:, 0:ntok], AF.Silu, [pk], ['dzs'])
            wv, wk = slab('w_in', l, 0, D, C_DA, 8)
            for s_ in range(nsub):
                for kc in range(KC):
                    mm(ps[4][:, 8 * s_:8 * s_ + 8], xn[:, kc, 128 * s_:128 * s_ + 128], wv[:, kc, 0:8], kc == 0, kc == KC - 1,
                       [rk(wk, kc), ('xn', kc)], ['ps4'])
            cp(ab_tm[:, 0:nsub, :], ps[4][:, 0:8 * nsub].rearrange("p (s c) -> p s c", c=8), ['ps4'], ['ab_tm'])
            wv, wk = slab('w_in', l, 0, D, C_CU, 512)
            for j in range(2):
                pb, pk = ps[j], 'ps%d' % j
                proj_fm(pb, pk, wv, wk, 128 * j, 128, xn, 'xn', KC, ntok)
                gelu(cug[:, j, 0:ntok], 'cug', pb[:, 0:ntok], pk, ntok)
            for s_ in range(nsub):
                pb, pk = ps[2 + (s_ % 2)], 'ps%d' % (2 + (s_ % 2))
                for kc in range(KC):
                    mm(pb[:, 0:256], xn[:, kc, 128 * s_:128 * s_ + 128], wv[:, kc, 256:512], kc == 0, kc == KC - 1,
                       [rk(wk, kc), ('xn', kc)], [pk])
                gelu(cs[s_][:, :], 'cs%d' % s_, pb[:, 0:256], pk, 256)
            for s_ in range(nsub):
                gk_ = 'cs%d' % s_
                P.op('dve', lambda e, b=cs[s_]: e.reduce_sum(out=sm[0][:, 0:1], in_=b[:, :], axis=AX.X), [gk_], ['sm0'])
                ts(sm[0][:, 1:2], sm[0][:, 0:1], -1.0 / 256, None, ALU.mult, None, ['sm0'], ['sm0b'])
                ts(cs[4][:, :], cs[s_][:, :], sm[0][:, 1:2], None, ALU.add, None, [gk_, 'sm0b'], ['cs4'])
                tt(cs[5][:, :], cs[4][:, :], cs[4][:, :], ALU.mult, ['cs4'], ['cs5'])
                P.op('dve', lambda e: e.reduce_sum(out=sm[0][:, 2:3], in_=cs[5][:, :], axis=AX.X), ['cs5'], ['sm0c'])
                act(sm[0][:, 3:4], sm[0][:, 2:3], AF.Ln, ['sm0c'], ['sm0d'], bias=eps_col[:, 0:1], scale=1.0 / 256)
                act(sm[0][:, 4:5], sm[0][:, 3:4], AF.Exp, ['sm0d'], ['sm0e'], scale=-0.5)
                stt(cs[5][:, :], cs[4][:, :], sm[0][:, 4:5], lng_b[:, :], ALU.mult, ALU.mult, ['cs4', 'sm0e', 'lng'], ['cs5'])
                tt(vn_tm[:, s_, :], cs[5][:, :], lnb_b[:, :], ALU.add, ['cs5', 'lnb'], ['vn_tm'])
            if mode == 'S':
                dma(o_cv_s[l], vn_tm[:, 0, :], ['vn_tm'], [], is_out=True)
            wv, wk = slab('w_in', l, 0, D, C_SH, 512)
            for j in range(2):
                pb, pk = ps[j], 'ps%d' % j
                proj_fm(pb, pk, wv, wk, 128 * j, 128, xn, 'xn', KC, ntok)
                cp(tmpA[:, 0:ntok] if j == 0 else tmpB[:, 0:ntok], pb[:, 0:ntok], [pk], ['tmpA' if j == 0 else 'tmpB'],
                   eng='act')
            for j in range(2):
                pb, pk = ps[2 + j], 'ps%d' % (2 + j)
                proj_fm(pb, pk, wv, wk, 256 + 128 * j, 128, xn, 'xn', KC, ntok)
                cp(sbb[:, j, 0:ntok], pb[:, 0:ntok], [pk], ['sbb'], eng='act')
            wv, wk = slab('w_in', l, 0, D, C_SC, 256)
            for j in range(2):
                pb, pk = ps[j], 'ps%d' % j
                proj_fm(pb, pk, wv, wk, 128 * j, 128, xn, 'xn', KC, ntok)
                shsrc = tmpA if j == 0 else tmpB
                tt(shv[:, j, :, 2:2 + Ls], v3(pb[:, 0:ntok]), v3(shsrc[:, 0:ntok]), ALU.mult,
                   [pk, 'tmpA' if j == 0 else 'tmpB'], ['shp'])

            for j in range(2):
                yv = v3(tmpA[:, 0:ntok]) if j == 0 else v3(tmpB[:, 0:ntok])
                yk = 'tmpA' if j == 0 else 'tmpB'
                wcol = lambda jj: cvec2[:, 96 + l * 6 + jj * 2 + j:96 + l * 6 + jj * 2 + j + 1]
                ts(yv, shv[:, j, :, 0:Ls], wcol(0), None, ALU.mult, None, ['shp', 'cvec2'], [yk])
                for jj in (1, 2):
                    stt(yv, shv[:, j, :, jj:jj + Ls], wcol(jj), yv, ALU.mult, ALU.add, ['shp', 'cvec2', yk], [yk])
                tt(brT[:, 3, j, 0:ntok], sbb[:, j, 0:ntok], (tmpA if j == 0 else tmpB)[:, 0:ntok], ALU.mult,
                   ['sbb', yk], [('brT', 3)])
            sc_state_out(l, tile, shv, Ls)
            for j in range(6):
                wcol = lambda jj: cvec2[:, l * 24 + jj * 6 + j:l * 24 + jj * 6 + j + 1]
                cb, cbk = ((tmpC, 'tmpC'), (tmpA, 'tmpA'), (tmpB, 'tmpB'))[j % 3]
                yv = v3(cb[:, 0:ntok])
                ts(yv, xpv[:, j, :, 0:Ls], wcol(0), None, ALU.mult, None, [('xp', j), 'cvec2'], [cbk])
                for jj in (1, 2, 3):
                    stt(yv, xpv[:, j, :, jj:jj + Ls], wcol(jj), yv, ALU.mult, ALU.add, [('xp', j), 'cvec2', cbk], [cbk])
                act(qkv[:, j, 0:ntok], cb[:, 0:ntok], AF.Silu, [cbk], [('qkv', j)])
            gc_state_out(l, tile, xpv, Ls)
            for j in range(4):
                tt(sqs[:, j, 0:ntok], qkv[:, j, 0:ntok], qkv[:, j, 0:ntok], ALU.mult, [('qkv', j)], [('sqs', j)])
                mm(ps[0][:, 0:ntok], blk64b[:], sqs[:, j, 0:ntok], True, True, ['blk64b', ('sqs', j)], ['ps0'])
                act(lnt[:, 0:ntok], ps[0][:, 0:ntok], AF.Ln, ['ps0'], ['lnt'], bias=eps_col[:, 0:1], scale=1.0)
                act(rstd[:, 0:ntok], lnt[:, 0:ntok], AF.Exp, ['lnt'], ['rstd'], scale=-0.5)
                if j < 2:
                    stt(qkv[:, j, 0:ntok], qkv[:, j, 0:ntok], 0.125, rstd[:, 0:ntok], ALU.mult, ALU.mult,
                        [('qkv', j), 'rstd'], [('qkv', j)])
                else:
                    tt(qkv[:, j, 0:ntok], qkv[:, j, 0:ntok], rstd[:, 0:ntok], ALU.mult, [('qkv', j), 'rstd'], [('qkv', j)])
            for _ in gla_chunk(l, tile, 0):
                pass

            def side_work(c_):
                if c_ + 1 < nsub:
                    for _ in gla_chunk(l, tile, c_ + 1):
                        yield
                cm_chunk(l, tile, c_)
                yield

            for c in range(nsub):
                nxt = side_work(c)
                gdn_chunk(l, tile, c, tick=((lambda g_=nxt: next(g_, None)) if mode == 'P' else None))
                for _ in nxt:
                    pass
            if tile['last']:
                state_out(l, tile)
            P.barrier()

            merge(l, tile)
            ffn(l, tile)
            ple(l, tile)

        gelu_ctr = [0]

        def gelu(out, outkey, pin, pkey, n):
            gelu_ctr[0] += 1
            if gelu_ctr[0] % 2:
                xsb, xk, t2b, tk = tmpC, 'tmpC', lnt, 'lnt'
            else:
                xsb, xk, t2b, tk = tmpA, 'tmpA', tmpB, 'tmpB'
            xs = xsb[:, 0:n]
            cp(xs, pin, [pkey], [xk], eng='act')
            t2 = t2b[:, 0:n]
            tt(t2, xs, xs, ALU.mult, [xk], [tk])
            ts(t2, t2, 0.044715, 1.0, ALU.mult, ALU.add, [tk], [tk])
            tt(t2, t2, xs, ALU.mult, [tk, xk], [tk])
            act(t2, t2, AF.Sigmoid, [tk], [tk], scale=1.5957691216057308)
            tt(out, xs, t2, ALU.mult, [xk, tk], [outkey])

        def sc_state_out(l, tile, shv, Ls):
            mode = tile['mode']
            if mode == 'P':
                cp(halo_sc[:, l, :, :], shv[:, :, 0, Ls:Ls + 2], ['shp'], ['halo_sc'])
                if not tile['last']:
                    return
                for j in range(2):
                    tr(ps[4][0:2, 128 * j:128 * j + 128], shv[:, j, 0, Ls:Ls + 2], 128, ['shp'], ['ps4'])
                cp(st_tm[0:2, 0:256], ps[4][0:2, 0:256], ['ps4'], ['ct8', 'ct9'])
                dma(o_sc_p[l], st_tm[0:2, 0:256], ['ct8', 'ct9'], [], is_out=True)
            else:
                cp(cs[0][:, 0:64].rearrange("p (j s t) -> p j s t", j=2, s=16), shv[:, :, :, Ls:Ls + 2], ['shp'], ['cs0'])
                for j in range(2):
                    tr(ps[4][0:32, 128 * j:128 * j + 128], cs[0][:, 32 * j:32 * j + 32], 128, ['cs0'], ['ps4'])
                cp(st_tm[0:32, 0:256], ps[4][0:32, 0:256], ['ps4'], ['ct8', 'ct9'])
                dma(o_sc_s[l], st_tm[0:32, 0:256], ['ct8', 'ct9'], [], is_out=True)

        def gc_state_out(l, tile, xpv, Ls):
            mode = tile['mode']
            if mode == 'P':
                cp(halo_gc[:, l, :, :], xpv[:, :, 0, Ls:Ls + 3], XPK, ['halo_gc'])
                if not tile['last']:
                    return
                for j in range(6):
                    tr(ps[4 + j // 4][0:3, 128 * (j % 4):128 * (j % 4) + 128], xpv[:, j, 0, Ls:Ls + 3], 128, XPK,
                       ['ps%d' % (4 + j // 4)])
                cp(st_tm[0:3, 0:512], ps[4][0:3, 0:512], ['ps4'], ['ct8', 'ct9'])
                cp(st_tm[0:3, 512:768], ps[5][0:3, 0:256], ['ps5'], ['ct8', 'ct9'])
                dma(o_gc_p[l], st_tm[0:3, :], ['ct8', 'ct9'], [], is_out=True)
            else:
                cp(cs[1][:, 0:144].rearrange("p (j s t) -> p j s t", j=3, s=16), xpv[:, 0:3, :, Ls:Ls + 3], XPK, ['cs1'])
                cp(cs[2][:, 0:144].rearrange("p (j s t) -> p j s t", j=3, s=16), xpv[:, 3:6, :, Ls:Ls + 3], XPK, ['cs2'])
                for j in range(6):
                    srcb = cs[1] if j < 3 else cs[2]
                    srck = 'cs1' if j < 3 else 'cs2'
                    tr(ps[4 + j // 4][0:48, 128 * (j % 4):128 * (j % 4) + 128], srcb[:, 48 * (j % 3):48 * (j % 3) + 48], 128,
                       [srck], ['ps%d' % (4 + j // 4)])
                cp(st_tm[0:48, 0:512], ps[4][0:48, 0:512], ['ps4'], ['ct8', 'ct9'])
                cp(st_tm[0:48, 512:768], ps[5][0:48, 0:256], ['ps5'], ['ct8', 'ct9'])
                dma(o_gc_s[l], st_tm[0:48, :], ['ct8', 'ct9'], [], is_out=True)

        def state_out(l, tile):
            if tile['mode'] == 'P':
                dma(o_gla_p[l], Sgla_p[:, l, :], ['Sgla_p'], [], is_out=True)
                for pair in range(2):
                    for h2 in range(2):
                        dma(o_gdn_p[l, 2 * pair + h2], Sgdn_p[64 * h2:64 * h2 + 64, l, pair, :], ['Sgdn_p'], [], is_out=True)
            else:
                dma(o_gla_s[l].rearrange("s p v -> p s v"), Sgla_s[:], ['Sgla_s'], [], is_out=True)
                for pair in range(2):
                    for h2 in range(2):
                        dma(o_gdn_s[l, :, 2 * pair + h2].rearrange("s k v -> k s v"), Sgdn_s[64 * h2:64 * h2 + 64, :, pair, :],
                            ['Sgdn_s'], [], is_out=True)

        def gla_chunk(l, tile, c):
            mode = tile['mode']
            nseq = 1 if mode == 'P' else 16
            Lq = 128 // nseq
            tok = slice(128 * c, 128 * c + 128)
            triU = C('triU' + mode)

            def s3(ap):
                return ap.rearrange("p (s t) -> p s t", s=nseq)
            gcs = [xp[:, 3 + i // 2, 256 * (i % 2):256 * (i % 2) + 256] for i in range(6)]
            gct = [xp[:, i, 0:512] for i in range(3)]

            def gk(i, b):
                return ('xp', 3 + i // 2, 2 * (i % 2) + b)
            mm(ps[0][:, 0:128], gaT[:, tok], wa2_sb[:, 128 * l:128 * l + 128], True, True, ['gaT', 'gaT1', 'wa2', 'wa2b'], ['ps0'])
            e1 = gcs[0][:, 0:128]
            act(e1, ps[0][:, 0:128], AF.Exp, ['ps0'], [gk(0, 0)], scale=-1.0)
            sp_ = gcs[0][:, 128:256]
            act(sp_, e1, AF.Ln, [gk(0, 0)], [gk(0, 1)], bias=one_col[:, 0:1], scale=1.0)
            yield
            mm(ps[1][:, 0:128], sp_, triU, True, True, [gk(0, 1), 'cst'], ['ps1'])
            bT = gcs[1][:, 0:128]
            cp(bT, ps[1][:, 0:128], ['ps1'], [gk(1, 0)])
            eb = gcs[1][:, 128:256]
            act(eb, bT, AF.Exp, [gk(1, 0)], [gk(1, 1)], scale=-1.0 / 16)
            enb = gcs[2][:, 0:128]
            act(enb, bT, AF.Exp, [gk(1, 0)], [gk(2, 0)], scale=1.0 / 16)
            yield
            dif = gcs[2][:, 128:256]
            tt(s3(dif), s3(bT), s3(bT)[:, :, Lq - 1:Lq].to_broadcast([128, nseq, Lq]), ALU.subtract, [gk(1, 0)], [gk(2, 1)])
            act(dif, dif, AF.Exp, [gk(2, 1)], [gk(2, 1)], scale=1.0 / 16)
            dec = sm[1][:, 0:nseq]
            act(dec, s3(bT)[:, :, Lq - 1], AF.Exp, [gk(1, 0)], ['sm1'], scale=-1.0 / 16)
            yield
            qd = gcs[3][:, 0:128]
            stt(qd, qT[:, tok], float(32 ** -0.5), eb, ALU.mult, ALU.mult, ['qT', gk(1, 1)], [gk(3, 0)])
            kd = gcs[3][:, 128:256]
            tt(kd, kT[:, tok], enb, ALU.mult, ['kT', gk(2, 0)], [gk(3, 1)])
            ke = gcs[4][:, 0:128]
            tt(ke, kT[:, tok], dif, ALU.mult, ['kT', gk(2, 1)], [gk(4, 0)])
            qdm = gct[0]
            tt(qdm[:, :].rearrange("p (h t) -> p h t", h=4), qd.unsqueeze(1).to_broadcast([128, 4, 128]),
               C('headmask')[:, 0:4].unsqueeze(2).to_broadcast([128, 4, 128]), ALU.mult, [gk(3, 0), 'cst'], [('xp', 0)])
            yield
            for h in range(4):
                mm(ps[2][:, 128 * h:128 * h + 128], kd, qdm[:, 128 * h:128 * h + 128], True, True, [gk(3, 1), ('xp', 0)], ['ps2'])
            AT = gct[1]
            tt(AT[:, :].rearrange("p (h t) -> p h t", h=4), ps[2][:, :].rearrange("p (h t) -> p h t", h=4),
               triU.unsqueeze(1).to_broadcast([128, 4, 128]), ALU.mult, ['ps2', 'cst'], [('xp', 1)])
            yield
            tr(ps[3][:, 0:128], ke, 128, [gk(4, 0)], ['ps3'])
            kem = gct[2]
            tt(kem[:, :].rearrange("p (h t) -> p h t", h=4), ps[3][:, 0:128].unsqueeze(1).to_broadcast([128, 4, 128]),
               C('hmfree').rearrange("p (h t) -> p h t", h=4), ALU.mult, ['ps3', 'cst'], [('xp', 2)])
            yield
            Sk = 'Sgla_p' if mode == 'P' else 'Sgla_s'
            for h in range(4):
                h2, pair = h % 2, h // 2
                ob = ps[0][64 * h2:64 * h2 + 64, 128 * pair:128 * pair + 128]
                if mode == 'P':
                    mm(ob, gv_tm[:, c, 64 * h:64 * h + 64], AT[:, 128 * h:128 * h + 128], True, False, ['gv_tm', ('xp', 1)], ['ps0'])
                    mm(ob, Sgla_p[:, l, :], qdm[:, 128 * h:128 * h + 128], False, True, [Sk, ('xp', 0)], ['ps0'])
                else:
                    mm(ob, gv_tm[:, c, 64 * h:64 * h + 64], AT[:, 128 * h:128 * h + 128], True, False, ['gv_tm', ('xp', 1)], ['ps0'])
                    for s_ in range(16):
                        mm(ps[0][64 * h2:64 * h2 + 64, 128 * pair + 8 * s_:128 * pair + 8 * s_ + 8], Sgla_s[:, s_, :],
                           qdm[:, 128 * h + 8 * s_:128 * h + 8 * s_ + 8], False, s_ == 15, [Sk, ('xp', 0)], ['ps0'])
            yield
            if mode == 'P':
                for h in range(4):
                    mm(ps[1][:, 0:64], kem[:, 128 * h:128 * h + 128], gv_tm[:, c, 64 * h:64 * h + 64], h == 0, h == 3,
                       [('xp', 2), 'gv_tm'], ['ps1'])
                stt(Sgla_p[:, l, :], Sgla_p[:, l, :], dec[:, 0:1], ps[1][:, 0:64], ALU.mult, ALU.add,
                    [Sk, 'sm1', 'ps1'], [Sk])
            else:
                for half in range(2):
                    tt(um[:, :, :], gv_tm[:, c, :].unsqueeze(1).to_broadcast([128, 8, 256]),
                       C('seqmask')[:, 8 * half:8 * half + 8].unsqueeze(2).to_broadcast([128, 8, 256]), ALU.mult,
                       ['gv_tm', 'cst'], UMK)
                    for h in range(4):
                        mm(ps[1][:, :].rearrange("p (s v) -> p s v", s=8), kem[:, 128 * h:128 * h + 128],
                           um[:, :, 64 * h:64 * h + 64], h == 0, h == 3, [('xp', 2)] + UMK, ['ps1'])
                    sl = slice(8 * half, 8 * half + 8)
                    tt(Sgla_s[:, sl, :], Sgla_s[:, sl, :], dec[:, sl].unsqueeze(2).to_broadcast([128, 8, 64]), ALU.mult,
                       [Sk, 'sm1'], [Sk])
                    tt(Sgla_s[:, sl, :], Sgla_s[:, sl, :], ps[1][:, :].rearrange("p (s v) -> p s v", s=8), ALU.add,
                       [Sk, 'ps1'], [Sk])
            yield
            head_norm_gate(ps[0], 'ps0', cvec3[:, l:l + 1], grs, 'grs', 0, tok,
                           gcs[5], [gk(5, 0), gk(5, 1)], tmpA[:, 0:256], ['tmpA'], ps[3], 'ps3')
            yield

        one_col = sb("one_col", [128, 4])

        def head_norm_gate(pso, pskey, gaincol, gate, gatekey, bidx, tok, o_sb=None, ok=None, sq=None, sk=None,
                           psq=None, psqk=None):
            if o_sb is None:
                o_sb, ok, sq, sk, psq, psqk = cs[5], ['cs5'], cs[6], ['cs6'], ps[7], 'ps7'
            cp(o_sb[:, :], pso[:, 0:256], [pskey], ok, eng='act')
            tt(sq[:, :], o_sb[:, :], o_sb[:, :], ALU.mult, ok, sk)
            mm(psq[:, 0:256], C('blk64'), sq[:, :], True, True, ['cst'] + sk, [psqk])
            act(sq[:, :], psq[:, 0:256], AF.Ln, [psqk], sk, bias=eps_col[:, 0:1], scale=1.0 / 64)
            act(sq[:, :], sq[:, :], AF.Exp, sk, sk, scale=-0.5)
            stt(o_sb[:, :], o_sb[:, :], gaincol, sq[:, :], ALU.mult, ALU.mult, ok + sk + ['cvec3'], ok)
            tt(brT[:, bidx, :, tok], o_sb[:, :].rearrange("p (j t) -> p j t", j=2), gate[:, :, tok], ALU.mult,
               ok + [gatekey], [('brT', bidx)])

        def gdn_chunk(l, tile, c, tick=None):
            mode = tile['mode']
            nseq = 1 if mode == 'P' else 16
            Lq = 128 // nseq
            tok = slice(128 * c, 128 * c + 128)
            triU = C('triU' + mode)
            H4 = lambda ap: ap.rearrange("p (h t) -> p h t", h=4)
            for j in range(2):
                tr(ps[4][:, 128 * j:128 * j + 128], qkv[:, 2 + j, tok], 128, [('qkv', 2 + j)], ['ps4'])
                tr(ps[4][:, 256 + 128 * j:256 + 128 * j + 128], qkv[:, 4 + j, tok], 128, [('qkv', 4 + j)], ['ps4'])
            k_tm = cs[0]
            v_tm = cs[1]
            cp(k_tm[:, :], ps[4][:, 0:256], ['ps4'], ['cs0'], eng='act')
            cp(v_tm[:, :], ps[4][:, 256:512], ['ps4'], ['cs1'])
            g_ = sm[2]
            tt(g_[:, 0:4], ab_tm[:, c, 0:4], dtb_b[:, 4 * l:4 * l + 4], ALU.add, ['ab_tm', 'dtb'], ['sm2'])
            act(g_[:, 0:4], g_[:, 0:4], AF.Exp, ['sm2'], ['sm2'])
            act(g_[:, 0:4], g_[:, 0:4], AF.Ln, ['sm2'], ['sm2'], bias=one_col[:, 0:1], scale=1.0)
            tt(g_[:, 0:4], g_[:, 0:4], nega_b[:, 4 * l:4 * l + 4], ALU.mult, ['sm2', 'nega'], ['sm2'])
            be = sm[3]
            act(be[:, 0:4], ab_tm[:, c, 4:8], AF.Exp, ['ab_tm'], ['sm3'], scale=-1.0)
            ts(be[:, 0:4], be[:, 0:4], 1.0, None, ALU.add, None, ['sm3'], ['sm3'])
            recip(be[:, 0:4], be[:, 0:4], ['sm3'], ['sm3'])
            gbc = ct[0]
            cp(H4(gbc[:, :]), g_[:, 0:4].unsqueeze(2).to_broadcast([128, 4, 128]), ['sm2'], ['ct0'])
            for h in range(4):
                mm(ps[5][:, 128 * h:128 * h + 128], gbc[:, 128 * h:128 * h + 128], triU, True, True, ['ct0', 'cst'], ['ps5'])
            mm(ps[6][:, 0:4], triU, g_[:, 0:4], True, True, ['cst', 'sm2'], ['ps6'])
            mm(ps[6][:, 4:8], C('tot' + mode), g_[:, 0:4], True, True, ['cst', 'sm2'], ['ps6'])
            Gc = sm[4]
            cp(Gc[:, 0:8], ps[6][:, 0:8], ['ps6'], ['sm4'])
            Gb = ct[1]
            cp(Gb[:, :], ps[5][:, :], ['ps5'], ['ct1'], eng='act')
            d_ = ct[2]
            tt(H4(d_[:, :]), H4(Gb[:, :]), Gc[:, 0:4].unsqueeze(2).to_broadcast([128, 4, 128]), ALU.subtract, ['ct1', 'sm4'], ['ct2'])
            dL = ct[3]
            tt(H4(dL[:, :]), H4(d_[:, :]), C('bigL' + mode).unsqueeze(1).to_broadcast([128, 4, 128]), ALU.add, ['ct2', 'cst'], ['ct3'])
            act(dL[:, :], dL[:, :], AF.Exp, ['ct3'], ['ct3'], scale=-1.0)
            dU = ct[4]
            tt(H4(dU[:, :]), H4(d_[:, :]), C('bigU' + mode).unsqueeze(1).to_broadcast([128, 4, 128]), ALU.subtract, ['ct2', 'cst'], ['ct4'])
            act(dU[:, :], dU[:, :], AF.Exp, ['ct4'], ['ct4'])
            eG = sm[5]
            act(eG[:, 0:4], Gc[:, 0:4], AF.Exp, ['sm4'], ['sm5'])
            bw = sm[6]
            tt(bw[:, 0:4], eG[:, 0:4], be[:, 0:4], ALU.mult, ['sm5', 'sm3'], ['sm6'])
            ts(bw[:, 4:8], bw[:, 0:4], -1.0, None, ALU.mult, None, ['sm6'], ['sm6'])
            ek = sm[7]
            tt(ek[:, 0:4], Gc[:, 4:8], Gc[:, 0:4], ALU.subtract, ['sm4'], ['sm7'])
            act(ek[:, 0:4], ek[:, 0:4], AF.Exp, ['sm7'], ['sm7'])
            hm2 = C('blk64').rearrange("p (a b) -> p a b", a=2)[:, :, 0]
            kmask, qmask = ct[0], ct[2]
            for pair in range(2):
                tt(H4(kmask[:, :])[:, 2 * pair:2 * pair + 2, :], qkv[:, 2 + pair, tok].unsqueeze(1).to_broadcast([128, 2, 128]),
                   hm2.unsqueeze(2).to_broadcast([128, 2, 128]), ALU.mult, [('qkv', 2 + pair), 'cst'], ['ct0'])
                tt(H4(qmask[:, :])[:, 2 * pair:2 * pair + 2, :], qkv[:, pair, tok].unsqueeze(1).to_broadcast([128, 2, 128]),
                   hm2.unsqueeze(2).to_broadcast([128, 2, 128]), ALU.mult, [('qkv', pair), 'cst'], ['ct2'])
            for h in range(4):
                h2, pair = h % 2, h // 2
                mm(ps[6][:, 128 * h:128 * h + 128], qkv[:, 2 + pair, tok], kmask[:, 128 * h:128 * h + 128], True, True,
                   [('qkv', 2 + pair), 'ct0'], ['ps6'])
                mm(ps[7][:, 128 * h:128 * h + 128], qkv[:, 2 + pair, tok], qmask[:, 128 * h:128 * h + 128], True, True,
                   [('qkv', 2 + pair), 'ct2'], ['ps7'])
            Nm = ct[5]
            tt(H4(Nm[:, :]), H4(dL[:, :]), C('strictL' + mode).unsqueeze(1).to_broadcast([128, 4, 128]), ALU.mult, ['ct3', 'cst'], ['ct5'])
            tt(Nm[:, :], Nm[:, :], ps[6][:, :], ALU.mult, ['ct5', 'ps6'], ['ct5'])
            tt(H4(Nm[:, :]), H4(Nm[:, :]), be[:, 0:4].unsqueeze(2).to_broadcast([128, 4, 128]), ALU.mult, ['ct5', 'sm3'], ['ct5'])
            qkT = ct[6]
            tt(qkT[:, :], dU[:, :], ps[7][:, :], ALU.mult, ['ct4', 'ps7'], ['ct6'])
            RU = cs[2]
            tt(RU[:, :].rearrange("p (h v) -> p h v", h=4), v_tm[:, :].rearrange("p (h v) -> p h v", h=4),
               be[:, 0:4].unsqueeze(2).to_broadcast([128, 4, 64]), ALU.mult, ['cs1', 'sm3'], ['cs2'])
            RW = cs[3]
            tt(RW[:, :].rearrange("p (h v) -> p h v", h=4), k_tm[:, :].rearrange("p (h v) -> p h v", h=4),
               bw[:, 4:8].unsqueeze(2).to_broadcast([128, 4, 64]), ALU.mult, ['cs0', 'sm6'], ['cs3'])
            kend = cs[4]
            tt(kend[:, :].rearrange("p (h v) -> p h v", h=4), k_tm[:, :].rearrange("p (h v) -> p h v", h=4),
               ek[:, 0:4].unsqueeze(2).to_broadcast([128, 4, 64]), ALU.mult, ['cs0', 'sm7'], ['cs4'])
            for h in range(4):
                tr(ps[5][:, 128 * h:128 * h + 128], Nm[:, 128 * h:128 * h + 128], 128, ['ct5'], ['ps5'])
            NT = ct[7]
            cp(NT[:, :], ps[5][:, :], ['ps5'], ['ct7'], eng='act')
            PT = ct[8]
            ts(PT[:, :], NT[:, :], -1.0, None, ALU.mult, None, ['ct7'], ['ct8'])
            tt(H4(PT[:, :]), H4(PT[:, :]), C('ident').unsqueeze(1).to_broadcast([128, 4, 128]), ALU.add, ['ct8', 'cst'], ['ct8'])
            A_, AT_, B_, BT_ = Nm, NT, ct[9], ct[10]
            Ak, ATk, Bk, BTk = 'ct5', 'ct7', 'ct9', 'ct10'
            nlev = 6 if mode == 'P' else 2
            psets = ((ps[5], 'ps5', ps[6], 'ps6', ps[7], 'ps7'), (ps[5], 'ps5', ps[6], 'ps6', ps[7], 'ps7'))
            for lev in range(nlev):
                last = (lev == nlev - 1)
                for hf in range(2):
                    pB, pBk, pBT, pBTk, pP, pPk = psets[hf]
                    hc = slice(256 * hf, 256 * hf + 256)
                    for h in (2 * hf, 2 * hf + 1):
                        hs = slice(128 * h, 128 * h + 128)
                        mm(pB[:, hs], AT_[:, hs], A_[:, hs], True, True, [(Ak, hf), (ATk, hf)], [pBk])
                        if not last:
                            mm(pBT[:, hs], A_[:, hs], AT_[:, hs], True, True, [(Ak, hf), (ATk, hf)], [pBTk])
                    cp(B_[:, hc], pB[:, hc], [pBk], [(Bk, hf)], eng='act')
                    if not last:
                        cp(BT_[:, hc], pBT[:, hc], [pBTk], [(BTk, hf)])
                    for h in (2 * hf, 2 * hf + 1):
                        hs = slice(128 * h, 128 * h + 128)
                        mm(pP[:, hs], B_[:, hs], PT[:, hs], True, True, [(Bk, hf), ('ct8', hf)], [pPk])
                    tt(PT[:, hc], PT[:, hc], pP[:, hc], ALU.add, [('ct8', hf), pPk], [('ct8', hf)])
                    if tick is not None:
                        tick()
                A_, AT_, B_, BT_ = B_, BT_, A_, AT_
                Ak, ATk, Bk, BTk = Bk, BTk, Ak, ATk
            if tick is not None:
                tick()
            for h in range(4):
                h2, pair = h % 2, h // 2
                mm(ps[5][64 * h2:64 * h2 + 64, 128 * pair:128 * pair + 128], RW[:, 64 * h:64 * h + 64],
                   PT[:, 128 * h:128 * h + 128], True, True, ['cs3', 'ct8'], ['ps5'])
            nWT = cs[5]
            cp(nWT[:, :], ps[5][:, 0:256], ['ps5'], ['cs5'], eng='act')
            eGb = ct[9]
            act(eGb[:, :], Gb[:, :], AF.Exp, ['ct1'], ['ct9'])
            qg = cs[6]
            for pair in range(2):
                for h2 in range(2):
                    h = 2 * pair + h2
                    tt(qg[64 * h2:64 * h2 + 64, 128 * pair:128 * pair + 128], qkv[64 * h2:64 * h2 + 64, pair, tok],
                       eGb[64 * h2:64 * h2 + 64, 128 * h:128 * h + 128], ALU.mult, [('qkv', pair), 'ct9'], ['cs6'])
            nWTm, qgm = ct[5], ct[7]
            tt(nWTm[:, :].rearrange("p (a x) -> p a x", a=2), nWT[:, :].unsqueeze(1).to_broadcast([128, 2, 256]),
               hm2.unsqueeze(2).to_broadcast([128, 2, 256]), ALU.mult, ['cs5', 'cst'], ['ct5'])
            tt(qgm[:, :].rearrange("p (a x) -> p a x", a=2), qg[:, :].unsqueeze(1).to_broadcast([128, 2, 256]),
               hm2.unsqueeze(2).to_broadcast([128, 2, 256]), ALU.mult, ['cs6', 'cst'], ['ct7'])
            u_sb = cs[7]
            Sk = 'Sgdn_p' if mode == 'P' else 'Sgdn_s'
            if mode == 'P':
                for h in range(4):
                    h2, pair = h % 2, h // 2
                    hp = slice(64 * h2, 64 * h2 + 64)
                    mm(ps[6][:, 64 * h:64 * h + 64], PT[:, 128 * h:128 * h + 128], RU[:, 64 * h:64 * h + 64], True, False,
                       ['ct8', 'cs2'], ['ps6'])
                    mm(ps[6][:, 64 * h:64 * h + 64], nWTm[:, 256 * h2 + 128 * pair:256 * h2 + 128 * pair + 128], Sgdn_p[:, l, pair, :],
                       False, True, ['ct5', Sk], ['ps6'])
                cp(u_sb[:, :], ps[6][:, 0:256], ['ps6'], ['cs7'])
                if tick is not None:
                    tick()
                for h in range(4):
                    h2, pair = h % 2, h // 2
                    hp = slice(64 * h2, 64 * h2 + 64)
                    ob = ps[4][hp, 128 * pair:128 * pair + 128]
                    mm(ob, Sgdn_p[:, l, pair, :], qgm[:, 256 * h2 + 128 * pair:256 * h2 + 128 * pair + 128], True, False, [Sk, 'ct7'], ['ps4'])
                    mm(ob, u_sb[:, 64 * h:64 * h + 64], qkT[:, 128 * h:128 * h + 128], False, True, ['cs7', 'ct6'], ['ps4'])
                for h in range(4):
                    h2, pair = h % 2, h // 2
                    hp = slice(64 * h2, 64 * h2 + 64)
                    mm(ps[7][hp, 64 * pair:64 * pair + 64], kend[:, 64 * h:64 * h + 64], u_sb[:, 64 * h:64 * h + 64], True, True,
                       ['cs4', 'cs7'], ['ps7'])
                for pair in range(2):
                    for h2 in range(2):
                        h = 2 * pair + h2
                        hp = slice(64 * h2, 64 * h2 + 64)
                        stt(Sgdn_p[hp, l, pair, :], Sgdn_p[hp, l, pair, :], eGb[hp, 128 * h + 127:128 * h + 128],
                            ps[7][hp, 64 * pair:64 * pair + 64], ALU.mult, ALU.add, [Sk, 'ct9', 'ps7'], [Sk])
            else:
                for h in range(4):
                    h2, pair = h % 2, h // 2
                    hp = slice(64 * h2, 64 * h2 + 64)
                    for s_ in range(16):
                        o_ = 256 * h2 + 128 * pair + 8 * s_
                        mm(ps[6][hp, 128 * pair + 8 * s_:128 * pair + 8 * s_ + 8], Sgdn_s[:, s_, pair, :],
                           nWTm[:, o_:o_ + 8], True, True, [Sk, 'ct5'], ['ps6'])
                cp(ct[10][:, 0:256], ps[6][:, 0:256], ['ps6'], ['ct10'])
                for pair in range(2):
                    tr(ps[7][:, 128 * pair:128 * pair + 128], ct[10][:, 128 * pair:128 * pair + 128], 128, ['ct10'], ['ps7'])
                for h in range(4):
                    mm(ps[6][:, 256 + 64 * h:256 + 64 * h + 64], PT[:, 128 * h:128 * h + 128], RU[:, 64 * h:64 * h + 64], True, True,
                       ['ct8', 'cs2'], ['ps6'])
                cp(u_sb[:, :], ps[6][:, 256:512], ['ps6'], ['cs7'])
                tt(u_sb[:, :], u_sb[:, :], ps[7][:, 0:256], ALU.add, ['cs7', 'ps7'], ['cs7'])
                for h in range(4):
                    h2, pair = h % 2, h // 2
                    hp = slice(64 * h2, 64 * h2 + 64)
                    ob = ps[4][hp, 128 * pair:128 * pair + 128]
                    mm(ob, u_sb[:, 64 * h:64 * h + 64], qkT[:, 128 * h:128 * h + 128], True, False, ['cs7', 'ct6'], ['ps4'])
                    for s_ in range(16):
                        o_ = 256 * h2 + 128 * pair + 8 * s_
                        mm(ps[4][hp, 128 * pair + 8 * s_:128 * pair + 8 * s_ + 8], Sgdn_s[:, s_, pair, :],
                           qgm[:, o_:o_ + 8], False, s_ == 15, [Sk, 'ct7'], ['ps4'])
                for half in range(2):
                    sl = slice(8 * half, 8 * half + 8)
                    tt(um[:, :, :], u_sb[:, :].unsqueeze(1).to_broadcast([128, 8, 256]),
                       C('seqmask')[:, 8 * half:8 * half + 8].unsqueeze(2).to_broadcast([128, 8, 256]), ALU.mult,
                       ['cs7', 'cst'], UMK)
                    for pair in range(2):
                        pb, pk = (ps[5], 'ps5') if pair == 0 else (ps[7], 'ps7')
                        for h2 in range(2):
                            h = 2 * pair + h2
                            hp = slice(64 * h2, 64 * h2 + 64)
                            mm(pb[hp, :].rearrange("p (s v) -> p s v", s=8), kend[:, 64 * h:64 * h + 64],
                               um[:, :, 64 * h:64 * h + 64], True, True, ['cs4'] + UMK, [pk])
                        for h2 in range(2):
                            h = 2 * pair + h2
                            hp = slice(64 * h2, 64 * h2 + 64)
                            dnv = eGb[hp, 128 * h:128 * h + 128].rearrange("p (s t) -> p s t", s=16)[:, sl, 7:8]
                            tt(Sgdn_s[hp, sl, pair, :], Sgdn_s[hp, sl, pair, :], dnv.to_broadcast([64, 8, 64]), ALU.mult,
                               [Sk, 'ct9'], [Sk])
                            tt(Sgdn_s[hp, sl, pair, :], Sgdn_s[hp, sl, pair, :], pb[hp, :].rearrange("p (s v) -> p s v", s=8),
                               ALU.add, [Sk, pk], [Sk])
            head_norm_gate(ps[4], 'ps4', cvec3[:, 8 + l:8 + l + 1], dzs, 'dzs', 1, tok)

        def cm_chunk(l, tile, c):
            tok = slice(128 * c, 128 * c + 128)
            for g in range(4):
                h2, pair = g % 2, g // 2
                mm(ps[1][64 * h2:64 * h2 + 64, 128 * pair:128 * pair + 128], vn_tm[:, c, 64 * g:64 * g + 64], wmT[:, g, :], True, True,
                   ['vn_tm', 'wmT'], ['ps1'])
            s_sb = tmpB[:, 0:256]
            P.op('dve', lambda e: e.tensor_tensor(out=s_sb, in0=ps[1][:, 0:256],
                                                  in1=bsT[:, :, :].rearrange("p j t -> p (j t)"), op=ALU.add),
                 ['ps1', 'bsT'], ['tmpB'], strict=True)
            tt(brT[:, 2, :, tok], cug[:, :, tok], s_sb.rearrange("p (j t) -> p j t", j=2), ALU.mult, ['cug', 'tmpB'],
               [('brT', 2)])

        def merge(l, tile):
            ntok = tile['ntok']
            for g in range(4):
                wb_lo, wbk_lo = slab_wbranch(l, g, 0)
                wg0, wgk0 = slab('w_gate', l, 0, D, 1024 * g, 512)
                half_groups(l, g, 0, wb_lo, wbk_lo, wg0, wgk0, ntok)
                wb_hi, wbk_hi = slab_wbranch(l, g, 1)
                wg1, wgk1 = slab('w_gate', l, 0, D, 1024 * g + 512, 512)
                half_groups(l, g, 1, wb_hi, wbk_hi, wg1, wgk1, ntok)
            for og in range(KC):
                cp(sqs[:, og, 0:ntok], ct[og][:, 0:ntok], ['ct%d' % og], [('sqs', og)], eng=('act' if og % 2 else 'dve'))
            for half in range(2):
                wv, wk = slab('w_o', l, 0, D, 512 * half, 512)
                for j in range(4):
                    og = 4 * half + j
                    pb, pk = ps[og % 2], 'ps%d' % (og % 2)
                    proj_fm(pb, pk, wv, wk, 128 * j, 128, sqs, 'sqs', KC, ntok)
                    tt(hT[:, og, 0:ntok], hT[:, og, 0:ntok], pb[:, 0:ntok], ALU.add, [('h', og), pk], [('h', og)])
                    sq_h(og, ntok, 'xn')

        def slab_wbranch(l, g, half):
            spec_rows = 256
            v, k = slab_rows('w_branch', (l, g), spec_rows, 512 * half, 512)
            return v, k

        def slab_rows(wname, idx, nrows, c0, ncols):
            spec = (wname, idx, 0, nrows, c0, ncols)
            return slab(*spec)

        def half_groups(l, g, half, wb, wbk, wg, wgk, ntok):
            for j in range(4):
                og = 4 * half + j
                pg, pgk = ps[0 + 2 * (j % 2)], 'ps%d' % (0 + 2 * (j % 2))
                pbr, pbk = ps[1 + 2 * (j % 2)], 'ps%d' % (1 + 2 * (j % 2))
                proj_fm(pg, pgk, wg, wgk, 128 * j, 128, xn, 'xn', KC, ntok)
                for kc in range(2):
                    mm(pbr[:, 0:ntok], wb[:, kc, 128 * j:128 * j + 128], brT[:, g, kc, 0:ntok], kc == 0, kc == 1,
                       [rk(wbk, kc), ('brT', g)], [pbk])
                sg = tmpA if j % 2 == 0 else tmpB
                sgk = 'tmpA' if j % 2 == 0 else 'tmpB'
                act(sg[:, 0:ntok], pg[:, 0:ntok], AF.Sigmoid, [pgk], [sgk])
                if g == 0:
                    tt(ct[og][:, 0:ntok], sg[:, 0:ntok], pbr[:, 0:ntok], ALU.mult, [sgk, pbk], ['ct%d' % og])
                else:
                    tt(sg[:, 0:ntok], sg[:, 0:ntok], pbr[:, 0:ntok], ALU.mult, [sgk, pbk], [sgk])
                    tt(ct[og][:, 0:ntok], ct[og][:, 0:ntok], sg[:, 0:ntok], ALU.add, ['ct%d' % og, sgk], ['ct%d' % og])

        def ffn(l, tile):
            ntok = tile['ntok']
            rmsnorm_to_xn(ntok, lambda kc: gcol('norm_ffn', l, kc), 'xn')
            nslab = (DFF + 511) // 512
            for s_ in range(nslab):
                c0 = 512 * s_
                ncols = min(512, DFF - c0)
                wgv, wgk = slab('w_ffn_gate', l, 0, D, c0, ncols)
                wuv, wuk = slab('w_ffn_up', l, 0, D, c0, ncols)
                for j in range(ncols // 128):
                    fg = 4 * s_ + j
                    pg, pgk = ps[0 + 2 * (j % 2)], 'ps%d' % (0 + 2 * (j % 2))
                    pu, puk = ps[1 + 2 * (j % 2)], 'ps%d' % (1 + 2 * (j % 2))
                    proj_fm(pg, pgk, wgv, wgk, 128 * j, 128, xn, 'xn', KC, ntok)
                    proj_fm(pu, puk, wuv, wuk, 128 * j, 128, xn, 'xn', KC, ntok)
                    sg = tmpA if j % 2 == 0 else tmpB
                    sgk = 'tmpA' if j % 2 == 0 else 'tmpB'
                    act(sg[:, 0:ntok], pg[:, 0:ntok], AF.Silu, [pgk], [sgk])
                    tt(hid[:, fg, 0:ntok], sg[:, 0:ntok], pu[:, 0:ntok], ALU.mult, [sgk, puk], ['ct%d' % (fg // 2)])
            for og in range(KC):
                wv, wk = slab('w_ffn_down', l, 0, DFF, 128 * og, 128)
                pb, pk = ps[og % 2], 'ps%d' % (og % 2)
                for fc in range(FC):
                    mm(pb[:, 0:ntok], wv[:, fc, :], hid[:, fc, 0:ntok], fc == 0, fc == FC - 1, [rk(wk, fc), 'ct%d' % (fc // 2)], [pk])
                tt(hT[:, og, 0:ntok], hT[:, og, 0:ntok], pb[:, 0:ntok], ALU.add, [('h', og), pk], [('h', og)])
                sq_h(og, ntok, 'sqs')

        def ple(l, tile):
            ntok = tile['ntok']
            rmsnorm_to_xn(ntok, lambda kc: gcol('norm_ple', l, kc))
            for half in range(2):
                wgv, wgk = slab('w_ple_gate', l, 0, D, 512 * half, 512)
                wpv, wpk = slab('w_ple', l, 0, 256, 512 * half, 512)
                for j in range(4):
                    og = 4 * half + j
                    pg, pgk = ps[0 + 2 * (j % 2)], 'ps%d' % (0 + 2 * (j % 2))
                    pp, ppk = ps[1 + 2 * (j % 2)], 'ps%d' % (1 + 2 * (j % 2))
                    proj_fm(pg, pgk, wgv, wgk, 128 * j, 128, xn, 'xn', KC, ntok)
                    for kc in range(2):
                        mm(pp[:, 0:ntok], wpv[:, kc, 128 * j:128 * j + 128], peT[:, kc, 0:ntok], kc == 0, kc == 1,
                           [rk(wpk, kc), 'peT'], [ppk])
                    sg = tmpA if j % 2 == 0 else tmpB
                    sgk = 'tmpA' if j % 2 == 0 else 'tmpB'
                    act(sg[:, 0:ntok], pg[:, 0:ntok], AF.Sigmoid, [pgk], [sgk])
                    tt(sg[:, 0:ntok], sg[:, 0:ntok], pp[:, 0:ntok], ALU.mult, [sgk, ppk], [sgk])
                    tt(hT[:, og, 0:ntok], hT[:, og, 0:ntok], sg[:, 0:ntok], ALU.add, [('h', og), sgk], [('h', og)])
                    sq_h(og, ntok, 'sqs')

        def load_tile(tile):
            ntok = tile['ntok']
            src = x_p[tile['t0']:tile['t0'] + ntok, :] if tile['mode'] == 'P' else x_s
            for s_ in range(ntok // 128):
                dma(x_tm[:], src[128 * s_:128 * s_ + 128, :], [], ['tmpA', 'tmpB'])
                for half in range(2):
                    pb, pk = ps[4 + half], 'ps%d' % (4 + half)
                    for j in range(4):
                        kc = 4 * half + j
                        tr(pb[:, 128 * j:128 * j + 128], x_tm[:, 128 * kc:128 * kc + 128], 128, ['tmpA', 'tmpB'], [pk])
                    cp(hT[:, 4 * half:4 * half + 4, 128 * s_:128 * s_ + 128], pb[:, :].rearrange("p (j t) -> p j t", j=4),
                       [pk], [('h', 4 * half + j) for j in range(4)], eng=('act' if half else 'dve'))
            for kc in range(KC):
                sq_h(kc, ntok, 'sqs')

        def store_tile(tile):
            ntok = tile['ntok']
            dst = y_p[tile['t0']:tile['t0'] + ntok, :] if tile['mode'] == 'P' else y_s
            for kc in range(KC):
                mm(ps[0][:, 0:ntok], onesb[:], sqs[:, kc, 0:ntok], kc == 0, kc == KC - 1, ['onesb', ('sqs', kc)], ['ps0'])
            act(lnt[:, 0:ntok], ps[0][:, 0:ntok], AF.Ln, ['ps0'], ['lnt'], bias=eps_col[:, 0:1], scale=1.0 / D)
            act(rstd[:, 0:ntok], lnt[:, 0:ntok], AF.Exp, ['lnt'], ['rstd'], scale=-0.5)
            for kc in range(KC):
                stt(hT[:, kc, 0:ntok], hT[:, kc, 0:ntok], cvec[:, 96 + kc:96 + kc + 1], rstd[:, 0:ntok], ALU.mult, ALU.mult,
                    [('h', kc), 'rstd', 'cvec'], [('h', kc)])
            for s_ in range(ntok // 128):
                for half in range(2):
                    pb, pk = ps[4 + half], 'ps%d' % (4 + half)
                    for j in range(4):
                        kc = 4 * half + j
                        tr(pb[:, 128 * j:128 * j + 128], hT[:, kc, 128 * s_:128 * s_ + 128], 128, [('h', kc)], [pk])
                    cp(x_tm[:, 512 * half:512 * half + 512], pb[:, :], [pk], ['tmpA', 'tmpB'], eng=('act' if half else 'dve'))
                dma(dst[128 * s_:128 * s_ + 128, :], x_tm[:], ['tmpA', 'tmpB'], [], is_out=True)

        tiles = []
        for t in range(NT):
            tiles.append(dict(mode='P', ntok=TT, nseq=1, t0=t * TT, last=(t == NT - 1)))
        if do_sample:
            tiles.append(dict(mode='S', ntok=128, nseq=16, t0=0, last=True))

        def program():
            for tile in tiles:
                load_tile(tile)
                for l in range(L):
                    layer(l, tile)
                store_tile(tile)

        P.planning = True
        program()
        P.planning = False
        memset(eps_col[:], EPS, ['eps_col'])
        memset(one_col[:], 1.0, ['one_col'])
        try:
            setup()
            program()
        except StopBuild:
            pass
        P.finish()
        block = es.enter_context(nc.Block())
        P.emit(block)
    return nc


_NC_CACHE = {}


def make_in_maps(inputs, TP, DEPTH, ncores):
    f = lambda a: np.ascontiguousarray(np.asarray(a, dtype=np.float32))
    wnames = ['norm_mix', 'w_in', 'gla_wa2', 'gla_ba', 'gla_norm', 'gdn_conv_w', 'gdn_a_log', 'gdn_dt_bias', 'gdn_norm',
              'cm_ln_g', 'cm_ln_b', 'cm_ws', 'cm_bs', 'sc_conv_w', 'w_gate', 'w_branch', 'w_o', 'norm_ffn', 'w_ffn_gate',
              'w_ffn_up', 'w_ffn_down', 'norm_ple', 'w_ple_gate', 'w_ple', 'norm_final']
    wd = {k: f(inputs[k]) for k in wnames}
    maps = []
    for b in range(ncores):
        sl = slice(16 * b, 16 * b + 16)
        m = dict(wd)
        m['x_p'] = f(inputs['x_prompt'][b])
        m['x_s'] = f(inputs['x_sample'][sl]).reshape(128, D)
        m['st_gla'] = f(inputs['state_gla'][:, sl]).reshape(DEPTH, 16, 128, 64)
        m['st_gdn'] = f(inputs['state_gdn'][:, sl])
        m['st_gc'] = f(inputs['state_gdn_conv'][:, sl]).reshape(DEPTH, 48, 768)
        m['st_sc'] = f(inputs['state_sconv'][:, sl]).reshape(DEPTH, 32, 256)
        m['p_p'] = f(inputs['p_prompt'][:, b])
        m['p_s'] = f(inputs['p_sample'][:, sl]).reshape(DEPTH, 128, 256)
        m['consts'] = CONST_ARR
        maps.append(m)
    return maps


def gather(results, TP, DEPTH, ncores):
    R = results
    cat = lambda k, ax: np.concatenate([np.asarray(r[k], dtype=np.float32) for r in R], axis=ax)
    y_prompt = np.stack([np.asarray(r['y_p'], np.float32) for r in R], 0)
    y_sample = cat('y_s', 0).reshape(ncores * 16, 8, D)
    gla_p = np.stack([np.asarray(r['o_gla_p'], np.float32).reshape(DEPTH, 4, 32, 64) for r in R], 1)
    gla_s = np.concatenate([np.asarray(r['o_gla_s'], np.float32).reshape(DEPTH, 16, 4, 32, 64) for r in R], 1)
    gdn_p = np.stack([np.asarray(r['o_gdn_p'], np.float32) for r in R], 1)
    gdn_s = np.concatenate([np.asarray(r['o_gdn_s'], np.float32) for r in R], 1)
    gc_p = np.stack([np.asarray(r['o_gc_p'], np.float32) for r in R], 1)
    gc_s = np.concatenate([np.asarray(r['o_gc_s'], np.float32).reshape(DEPTH, 16, 3, 768) for r in R], 1)
    sc_p = np.stack([np.asarray(r['o_sc_p'], np.float32) for r in R], 1)
    sc_s = np.concatenate([np.asarray(r['o_sc_s'], np.float32).reshape(DEPTH, 16, 2, 256) for r in R], 1)
    cv_s = np.concatenate([np.asarray(r['o_cv_s'], np.float32).reshape(DEPTH, 16, 8, 256) for r in R], 1)
    return (y_prompt, y_sample, gla_p, gla_s, gdn_p, gdn_s, gc_p, gc_s, sc_p, sc_s, cv_s)


def kernel(**inputs):
    TP = int(np.asarray(inputs['x_prompt']).shape[1])
    DEPTH = int(np.asarray(inputs['w_in']).shape[0])
    ncores = int(np.asarray(inputs['x_prompt']).shape[0])
    key = (TP, DEPTH)
    if key not in _NC_CACHE:
        _NC_CACHE[key] = build(TP=TP, TT=min(512, TP), DEPTH=DEPTH)
    nc = _NC_CACHE[key]
    maps = make_in_maps(inputs, TP, DEPTH, ncores)
    res = run_bass_kernel_spmd(nc, maps, core_ids=list(range(ncores)))
    return gather(res.results, TP, DEPTH, ncores)
```

```python
import numpy as np
import concourse.bass as bass
import concourse.mybir as mybir
from concourse.bass_utils import run_bass_kernel_spmd
from contextlib import ExitStack

F32 = mybir.dt.float32
BF16 = mybir.dt.bfloat16
AF = mybir.ActivationFunctionType
ALU = mybir.AluOpType
AX = mybir.AxisListType

D = 1024
KC = 8
DFF = 2816
FC = 22
IN_W = 3096
C_GQ, C_GK, C_GV, C_GR, C_GA = 0, 128, 256, 512, 768
C_DQ, C_DK, C_DV, C_DZ, C_DA, C_DB = 784, 1040, 1296, 1552, 1808, 1812
C_CU, C_CV, C_SH, C_SB, C_SC = 1816, 2072, 2328, 2584, 2840
EPS = 1e-6
BIG = 30000.0
NSLOT = 4
SLOT_EL = 4096
KD = 8


def make_consts():
    c = {}
    idx = np.arange(128)
    c['ident'] = np.eye(128, dtype=np.float32)
    c['ones'] = np.ones((128, 128), np.float32)
    c['blk64'] = (idx[:, None] // 64 == idx[None, :] // 64).astype(np.float32)
    for mode, L in (('P', 128), ('S', 8)):
        same = (idx[:, None] // L == idx[None, :] // L)
        le = idx[:, None] <= idx[None, :]
        c['triU' + mode] = (same & le).astype(np.float32)
        c['tot' + mode] = same.astype(np.float32)
        c['bigL' + mode] = np.where(same & (idx[:, None] >= idx[None, :]), 0.0, BIG).astype(np.float32)
        c['bigU' + mode] = np.where(same & le, 0.0, BIG).astype(np.float32)
        c['strictL' + mode] = (same & (idx[:, None] > idx[None, :])).astype(np.float32)
    c['headmask'] = np.zeros((128, 128), np.float32)
    c['headmask'][:, 0:4] = (idx[:, None] // 32 == np.arange(4)[None, :])
    c['seqmask'] = np.zeros((128, 128), np.float32)
    c['seqmask'][:, 0:16] = (idx[:, None] // 8 == np.arange(16)[None, :])
    hf = np.zeros((128, 4, 128), np.float32)
    for h in range(4):
        hf[:, h, 32 * h:32 * h + 32] = 1.0
    c['hmfree'] = hf.reshape(128, 512)
    e8 = np.zeros((128, 128), np.float32)
    e8[0:8, :] = (np.arange(8)[:, None] == idx[None, :] % 8)
    c['e8'] = e8
    names = ['ident', 'ones', 'blk64', 'triUP', 'totP', 'bigLP', 'bigUP', 'strictLP',
             'triUS', 'totS', 'bigLS', 'bigUS', 'strictLS', 'headmask', 'seqmask', 'e8', 'hmfree']
    offs = {}
    o = 0
    for n in names:
        offs[n] = (o, c[n].shape[1])
        o += c[n].shape[1]
    arr = np.concatenate([c[n] for n in names], axis=1).astype(np.float32)
    return arr, offs


CONST_ARR, CONST_OFF = make_consts()


class StopBuild(Exception):
    pass


class Prog:
    ENGS = ('pe', 'act', 'dve', 'pool', 'sp')
    max_ops = None
    log = None
    names = None

    def __init__(self, nc, es):
        self.nc = nc
        self.q = {e: [] for e in self.ENGS}
        self.sem = {e: es.enter_context(nc.semaphore('s_' + e)) for e in ('pe', 'act', 'dve')}
        self.cnt = {e: 0 for e in ('pe', 'act', 'dve')}
        self.dsem = {e: [es.enter_context(nc.semaphore('d_%s%d' % (e, i))) for i in range(KD)]
                     for e in ('pool', 'sp')}
        self.dcnt = {'pool': 0, 'sp': 0}
        self.waited = {}
        self.res = {}
        self.planning = False
        self.semname = {}
        for e in self.sem:
            self.semname[id(self.sem[e])] = e
        for e in self.dsem:
            for i, s in enumerate(self.dsem[e]):
                self.semname[id(s)] = '%s%d' % (e, i)
        self.out_events = []

    def _wait(self, eng, sem, val):
        key = (eng, id(sem))
        if self.waited.get(key, 0) >= val:
            return
        self.waited[key] = val
        self.q[eng].append(lambda e, s=sem, v=val: e.wait_ge(s, v))

    SPLIT = ('ct5', 'ct7', 'ct8', 'ct9', 'ct10')

    def _expand(self, keys):
        out = []
        for k in keys:
            if k in self.SPLIT:
                out.append((k, 0))
                out.append((k, 1))
            elif isinstance(k, tuple) and len(k) == 2 and k[0] == 'xp':
                out.extend([('xp', k[1], q) for q in range(4)])
            else:
                out.append(k)
        return out

    def op(self, eng, fn, reads=(), writes=(), is_out=False, strict=False):
        if self.planning:
            return
        reads = self._expand(reads)
        writes = self._expand(writes)
        self.nops = getattr(self, 'nops', 0) + 1
        if self.max_ops is not None and self.nops > self.max_ops:
            raise StopBuild()
        if self.log is not None:
            import sys as _s
            fr = _s._getframe(2)
            self.log.append((self.nops, eng, fr.f_code.co_name, fr.f_lineno, _s._getframe(3).f_code.co_name, _s._getframe(3).f_lineno))
        deps = []
        for k in reads:
            r = self.res.get(k)
            if r and r['w']:
                deps.append(r['w'] + ('raw',))
        for k in writes:
            r = self.res.get(k)
            if r:
                if r['w']:
                    deps.append(r['w'] + ('waw',))
                for (sid, (sem, val, src)) in r['r'].items():
                    deps.append((sem, val, src, 'war'))
        for (sem, val, src, kind) in deps:
            if src == eng:
                if eng == 'pe':
                    continue
                if eng in ('act', 'dve') and kind != 'raw' and not strict:
                    continue
            self._wait(eng, sem, val)
        if eng in ('sp', 'pool'):
            i = self.dcnt[eng]
            self.dcnt[eng] += 1
            sem = self.dsem[eng][i % KD]
            val = 16 * (i // KD + 1)
            if i >= KD:
                self._wait(eng, sem, val - 16)
            self.q[eng].append(lambda e, f=fn, s=sem, n_=self.nops: self._name(n_, f(e).then_inc(s, 16)))
        else:
            self.cnt[eng] += 1
            sem = self.sem[eng]
            val = self.cnt[eng]
            self.q[eng].append(lambda e, f=fn, s=sem, n_=self.nops: self._name(n_, f(e).then_inc(s, 1)))
        ev = (sem, val, eng)
        for k in reads:
            r = self.res.setdefault(k, {'w': None, 'r': {}})
            old = r['r'].get(id(sem))
            if old is None or old[1] < val:
                r['r'][id(sem)] = ev
        for k in writes:
            self.res[k] = {'w': ev, 'r': {}}
        if is_out:
            self.out_events.append(ev)

    def _name(self, n_, ins):
        if self.names is not None:
            try:
                self.names[ins.ins.name] = n_
            except Exception:
                pass
        return ins

    def war_guard(self, eng, keys):
        if self.planning:
            return
        for k in keys:
            r = self.res.get(k)
            if r:
                for (sem, val, src) in r['r'].values():
                    self._wait(eng, sem, val)
                if r['w'] and not r['r']:
                    self._wait(eng, r['w'][0], r['w'][1])
                self.res[k] = {'w': None, 'r': {}}

    def barrier(self):
        if self.planning:
            return
        for e in ('pe', 'act', 'dve'):
            for s in ('pe', 'act', 'dve'):
                if s != e and self.cnt[s] > 0:
                    self._wait(e, self.sem[s], self.cnt[s])

    def finish(self):
        last = {}
        for (sem, val, src) in self.out_events:
            if last.get(id(sem), (None, 0))[1] < val:
                last[id(sem)] = (sem, val)
        for (sem, val) in last.values():
            self._wait('sp', sem, val)

    def emit(self, block):
        nc = self.nc
        q = self.q

        @block.tensor
        def _(e):
            for f in q['pe']:
                f(e)

        @block.scalar
        def _(e):
            for f in q['act']:
                f(e)

        @block.vector
        def _(e):
            for f in q['dve']:
                f(e)

        @block.gpsimd
        def _(e):
            for f in q['pool']:
                f(e)

        @block.sync
        def _(e):
            for f in q['sp']:
                f(e)


def build(TP=2048, TT=512, DEPTH=4, do_sample=True, max_ops=None, log=None, names=None):
    nc = bass.Bass("TRN2", target_bir_lowering=False)
    NT = TP // TT
    L = DEPTH

    def din(name, shape):
        return nc.dram_tensor(name, list(shape), F32, kind="ExternalInput").ap()

    def dout(name, shape):
        return nc.dram_tensor(name, list(shape), F32, kind="ExternalOutput").ap()

    x_p = din("x_p", [TP, D])
    x_s = din("x_s", [128, D])
    st_gla = din("st_gla", [L, 16, 128, 64])
    st_gdn = din("st_gdn", [L, 16, 4, 64, 64])
    st_gc = din("st_gc", [L, 48, 768])
    st_sc = din("st_sc", [L, 32, 256])
    p_p = din("p_p", [L, TP, 256])
    p_s = din("p_s", [L, 128, 256])
    consts_d = din("consts", list(CONST_ARR.shape))
    W = {}
    wshapes = dict(norm_mix=[L, D], w_in=[L, D, IN_W], gla_wa2=[L, 16, 128], gla_ba=[L, 128], gla_norm=[L, 64],
                   gdn_conv_w=[L, 4, 768], gdn_a_log=[L, 4], gdn_dt_bias=[L, 4], gdn_norm=[L, 64],
                   cm_ln_g=[L, 256], cm_ln_b=[L, 256], cm_ws=[L, 4, 128, 128], cm_bs=[L, 4, 128],
                   sc_conv_w=[L, 3, 256], w_gate=[L, D, 4 * D], w_branch=[L, 4, 256, D], w_o=[L, D, D],
                   norm_ffn=[L, D], w_ffn_gate=[L, D, DFF], w_ffn_up=[L, D, DFF], w_ffn_down=[L, DFF, D],
                   norm_ple=[L, D], w_ple_gate=[L, D, D], w_ple=[L, 256, D], norm_final=[D])
    for k, s in wshapes.items():
        W[k] = din(k, s)
    y_p = dout("y_p", [TP, D])
    y_s = dout("y_s", [128, D])
    o_gla_p = dout("o_gla_p", [L, 128, 64])
    o_gla_s = dout("o_gla_s", [L, 16, 128, 64])
    o_gdn_p = dout("o_gdn_p", [L, 4, 64, 64])
    o_gdn_s = dout("o_gdn_s", [L, 16, 4, 64, 64])
    o_gc_p = dout("o_gc_p", [L, 3, 768])
    o_gc_s = dout("o_gc_s", [L, 48, 768])
    o_sc_p = dout("o_sc_p", [L, 2, 256])
    o_sc_s = dout("o_sc_s", [L, 32, 256])
    o_cv_s = dout("o_cv_s", [L, 128, 256])

    es = ExitStack()
    with es:
        P = Prog(nc, es)
        P.max_ops = max_ops
        P.log = log
        P.names = names

        def sb(name, shape, dt=F32):
            return es.enter_context(nc.sbuf_tensor(name, list(shape), dt))

        def psum(name):
            return es.enter_context(nc.psum_tensor(name, [128, 512], F32))

        NTK = max(TT, 512)
        cst = sb("cst", list(CONST_ARR.shape))
        onesb = sb("onesb", [128, 128], BF16)
        blk64b = sb("blk64b", [128, 128], BF16)
        cvec = sb("cvec", [128, 128])
        cvec2 = sb("cvec2", [128, 128])
        cvec3 = sb("cvec3", [128, 16])
        vstage = sb("vstage", [128, 128])
        alog_b = sb("alog_b", [128, L * 4])
        dtb_b = sb("dtb_b", [128, L * 4])
        nega_b = sb("nega_b", [128, L * 4])
        wa2_sb = sb("wa2_sb", [17, L * 128])
        hT = sb("hT", [128, KC, NTK])
        xn = sb("xn", [128, KC, NTK], BF16)
        ring = sb("ring", [128, NSLOT, SLOT_EL], BF16)
        rstd = sb("rstd", [128, NTK])
        lnt = sb("lnt", [128, NTK])
        Sgla_p = sb("Sgla_p", [128, L, 64])
        Sgdn_p = sb("Sgdn_p", [128, L, 2, 64])
        halo_gc = sb("halo_gc", [128, L, 6, 3])
        halo_sc = sb("halo_sc", [128, L, 2, 2])
        Sgla_s = sb("Sgla_s", [128, 16, 64])
        Sgdn_s = sb("Sgdn_s", [128, 16, 2, 64])
        sqs = sb("sqs", [128, KC, NTK], BF16)
        qT = sb("qT", [128, NTK])
        kT = sb("kT", [128, NTK])
        grs = sb("grs", [128, 2, NTK], BF16)
        gaT = sb("gaT", [17, NTK])
        xp = sb("xp", [128, 6, max(NTK + 3, 176)])
        qkv = sb("qkv", [128, 6, NTK])
        dzs = sb("dzs", [128, 2, NTK], BF16)
        cug = sb("cug", [128, 2, NTK], BF16)
        shp = sb("shp", [128, 2, max(NTK + 2, 160)])
        sbb = sb("sbb", [128, 2, NTK])
        gv_tm = sb("gv_tm", [128, NTK // 128, 256])
        vn_tm = sb("vn_tm", [128, NTK // 128, 256])
        ab_tm = sb("ab_tm", [128, NTK // 128, 8])
        brT = sb("brT", [128, 4, 2, NTK], BF16)
        tmpAB = sb("tmpAB", [128, 1024])
        tmpA = tmpAB[:, 0:512]
        tmpB = tmpAB[:, 512:1024]
        tmpC = sb("tmpC", [128, NTK])
        peT = sb("peT", [128, 2, NTK], BF16)
        pe_tm = sb("pe_tm", [128, 256])
        x_tm = tmpAB
        lng_b = sb("lng_b", [128, 256])
        lnb_b = sb("lnb_b", [128, 256])
        bsT = sb("bsT", [128, 2, 128])
        wmT = sb("wmT", [128, 4, 128])
        ws8 = sb("ws8", [8, 4, 8])
        ctall = sb("ctall", [128, 12 * 512])
        ct = [ctall[:, 512 * i:512 * i + 512] for i in range(12)]
        hid = ctall[:, 0:FC * 256].bitcast(BF16).rearrange("p (f t) -> p f t", f=FC)
        ws_tm = ct[10].rearrange("p (g s) -> p g s", g=4)
        t18 = ct[11][0:8, :].rearrange("p (g s) -> p g s", g=4)
        st_tm = ctall[0:48, 8 * 512:8 * 512 + 768]
        cs = [sb("cs%d" % i, [128, 256]) for i in range(8)]
        sm = [sb("sm%d" % i, [128, 16]) for i in range(10)]
        um = sqs[:, :, :].rearrange("p k t -> p (k t)").bitcast(F32).rearrange("p (s v) -> p s v", s=8)
        UMK = [('sqs', kc) for kc in range(KC)]
        XPK = [('xp', j_) for j_ in range(6)]

        ps = [psum("ps%d" % i) for i in range(8)]

        def C(name):
            o, w = CONST_OFF[name]
            return cst[:, o:o + w]

        def mm(out, lhsT, rhs, start, stop, reads, writes):
            P.op('pe', lambda e: e.matmul(out, lhsT=lhsT, rhs=rhs, start=start, stop=stop), reads, writes)

        def tr(out, in_, n_in_part, reads, writes):
            idn = C('ident')[0:n_in_part, 0:n_in_part]
            P.op('pe', lambda e: e.transpose(out, in_, idn), list(reads) + ['cst'], writes)

        def act(out, in_, func, reads, writes, bias=0.0, scale=1.0, accum_out=None):
            if accum_out is None:
                P.op('act', lambda e: e.activation(out=out, in_=in_, func=func, bias=bias, scale=scale), reads, writes)
            else:
                P.op('act', lambda e: e.activation(out=out, in_=in_, func=func, bias=bias, scale=scale,
                                                   accum_out=accum_out), reads, writes)

        def tt(out, in0, in1, op, reads, writes, eng='dve'):
            P.op(eng, lambda e: e.tensor_tensor(out=out, in0=in0, in1=in1, op=op), reads, writes)

        def ts(out, in0, s1, s2, op0, op1, reads, writes, accum_out=None):
            if op1 is None:
                P.op('dve', lambda e: e.tensor_scalar(out=out, in0=in0, scalar1=s1, scalar2=None, op0=op0),
                     reads, writes)
            elif accum_out is None:
                P.op('dve', lambda e: e.tensor_scalar(out=out, in0=in0, scalar1=s1, scalar2=s2, op0=op0, op1=op1),
                     reads, writes)
            else:
                P.op('dve', lambda e: e.tensor_scalar(out=out, in0=in0, scalar1=s1, scalar2=s2, op0=op0, op1=op1,
                                                      accum_out=accum_out), reads, writes)

        def stt(out, in0, scalar, in1, op0, op1, reads, writes):
            P.op('dve', lambda e: e.scalar_tensor_tensor(out=out, in0=in0, scalar=scalar, in1=in1, op0=op0, op1=op1),
                 reads, writes)

        def cp(out, in_, reads, writes, eng='dve'):
            if eng == 'act':
                P.op('act', lambda e: e.copy(out=out, in_=in_), reads, writes)
            else:
                P.op('dve', lambda e: e.tensor_copy(out=out, in_=in_), reads, writes)

        def dma(out, in_, reads, writes, eng='sp', is_out=False):
            P.op(eng, lambda e: e.dma_start(out=out, in_=in_), reads, writes, is_out=is_out)

        def memset(ap, val, writes):
            P.op('dve', lambda e: e.memset(ap, val), [], writes)

        def recip(out, in_, reads, writes):
            P.op('dve', lambda e: e.reciprocal(out=out, in_=in_), reads, writes)

        plan = []
        ring_state = {'next_use': 0, 'next_load': 0}

        def issue_load(k):
            (wname, l, r0, nrows, c0, ncols) = plan[k]
            slot = k % NSLOT
            kc = nrows // 128
            src = W[wname][l] if l is not None else W[wname]
            P.war_guard('pool', [('ring', slot, p_) for p_ in range(8)])
            for a in range(0, kc, 4):
                b = min(kc, a + 4)
                dst = ring[:, slot, a * ncols:b * ncols].rearrange("p (k c) -> p k c", k=b - a)
                s_ap = src[r0 + a * 128:r0 + b * 128, c0:c0 + ncols].rearrange("(k p) c -> p k c", p=128)
                dma(dst, s_ap, [], [('ring', slot, a // 4)], eng='pool')

        def slab(wname, l, r0, nrows, c0, ncols):
            spec = (wname, l, r0, nrows, c0, ncols)
            kc = nrows // 128
            assert kc * ncols <= SLOT_EL, spec
            if P.planning:
                plan.append(spec)
                k = len(plan) - 1
            else:
                k = ring_state['next_use']
                assert plan[k] == spec, (plan[k], spec)
                ring_state['next_use'] += 1
                while ring_state['next_load'] < min(len(plan), k + NSLOT - 1):
                    issue_load(ring_state['next_load'])
                    ring_state['next_load'] += 1
            slot = k % NSLOT
            v = ring[:, slot, 0:kc * ncols].rearrange("p (k c) -> p k c", k=kc)
            return v, ('ring', slot)

        def rk(wk, kc):
            return (wk[0], wk[1], kc // 4)

        def setup():
            dma(cst[:], consts_d[:, :], [], ['cst'])
            cp(onesb[:], C('ones'), ['cst'], ['onesb'])
            cp(blk64b[:], C('blk64'), ['cst'], ['blk64b'])
            memset(vstage[:], 0.0, ['vstage'])
            for i, nm in enumerate(('norm_mix', 'norm_ffn', 'norm_ple')):
                dma(vstage[i * 32:i * 32 + L * 8, :], W[nm].rearrange("l (k p) -> (l k) p", p=128), [], ['vstage'])
            dma(vstage[96:104, :], W['norm_final'].rearrange("(k p) -> k p", p=128), [], ['vstage'])
            tr(ps[0][:, 0:128], vstage[:], 128, ['vstage'], ['ps0'])
            cp(cvec[:], ps[0][:, 0:128], ['ps0'], ['cvec'])
            memset(vstage[:], 0.0, ['vstage'])
            dma(vstage[0:L * 24, :], W['gdn_conv_w'].rearrange("l j (k p) -> (l j k) p", p=128), [], ['vstage'])
            dma(vstage[96:96 + L * 6, :], W['sc_conv_w'].rearrange("l j (k p) -> (l j k) p", p=128), [], ['vstage'])
            tr(ps[0][:, 0:128], vstage[:], 128, ['vstage'], ['ps0'])
            cp(cvec2[:], ps[0][:, 0:128], ['ps0'], ['cvec2'])
            memset(vstage[:], 0.0, ['vstage'])
            for half in range(2):
                dma(vstage[0:L, 64 * half:64 * half + 64], W['gla_norm'][:, :], [], ['vstage'])
                dma(vstage[8:8 + L, 64 * half:64 * half + 64], W['gdn_norm'][:, :], [], ['vstage'])
            tr(ps[0][:, 0:128], vstage[:], 128, ['vstage'], ['ps0'])
            cp(cvec3[:], ps[0][:, 0:16], ['ps0'], ['cvec3'])
            dma(wa2_sb[16:17, :], W['gla_ba'].rearrange("(o l) c -> o (l c)", o=1), [], ['wa2b'])
            dma(wa2_sb[0:16, :].rearrange("p (l c) -> p l c", l=L), W['gla_wa2'].rearrange("l r c -> r l c"), [], ['wa2'])
            o1, _w = CONST_OFF['ones']
            for q_ in range(0, NTK, 128):
                dma(gaT[16:17, q_:q_ + 128], consts_d[0:1, o1:o1 + 128], [], ['gaT1'])
            dma(alog_b[:], W['gdn_a_log'].rearrange("(o l) h -> o (l h)", o=1).to_broadcast([128, L * 4]), [], ['alog'])
            dma(dtb_b[:], W['gdn_dt_bias'].rearrange("(o l) h -> o (l h)", o=1).to_broadcast([128, L * 4]), [], ['dtb'])
            act(nega_b[:], alog_b[:], AF.Exp, ['alog'], ['nega'])
            ts(nega_b[:], nega_b[:], -1.0, None, ALU.mult, None, ['nega'], ['nega'])
            memset(Sgla_p[:], 0.0, ['Sgla_p'])
            memset(Sgdn_p[:], 0.0, ['Sgdn_p'])
            memset(halo_gc[:], 0.0, ['halo_gc'])
            memset(halo_sc[:], 0.0, ['halo_sc'])

        def gcol(which, l, kc):
            base = {'norm_mix': 0, 'norm_ffn': 32, 'norm_ple': 64}[which]
            return cvec[:, base + l * 8 + kc:base + l * 8 + kc + 1]

        def sq_h(kc, ntok, which, eng='act'):
            buf, key = (sqs, 'sqs') if which == 'sqs' else (xn, 'xn')
            if eng == 'act':
                act(buf[:, kc, 0:ntok], hT[:, kc, 0:ntok], AF.Square, [('h', kc)], [(key, kc)])
            else:
                tt(buf[:, kc, 0:ntok], hT[:, kc, 0:ntok], hT[:, kc, 0:ntok], ALU.mult, [('h', kc)], [(key, kc)])

        def rmsnorm_to_xn(ntok, gsel, which='sqs'):
            buf, key = (sqs, 'sqs') if which == 'sqs' else (xn, 'xn')
            for kc in range(KC):
                mm(ps[0][:, 0:ntok], onesb[:], buf[:, kc, 0:ntok], kc == 0, kc == KC - 1,
                   ['onesb', (key, kc)], ['ps0'])
            act(lnt[:, 0:ntok], ps[0][:, 0:ntok], AF.Ln, ['ps0'], ['lnt'], bias=eps_col[:, 0:1], scale=1.0 / D)
            act(rstd[:, 0:ntok], lnt[:, 0:ntok], AF.Exp, ['lnt'], ['rstd'], scale=-0.5)
            for kc in range(KC):
                stt(xn[:, kc, 0:ntok], hT[:, kc, 0:ntok], gsel(kc), rstd[:, 0:ntok], ALU.mult, ALU.mult,
                    [('h', kc), 'rstd', 'cvec'], [('xn', kc)])

        eps_col = sb("eps_col", [128, 4])

        def proj_fm(psb, pskey, wv, wkey, c0, ncols, src, srckey, nkc, ntok, prow=0):
            for kc in range(nkc):
                mm(psb[prow:prow + ncols, 0:ntok], wv[:, kc, c0:c0 + ncols], src[:, kc, 0:ntok], kc == 0, kc == nkc - 1,
                   [rk(wkey, kc), (srckey, kc)], [pskey])

        def layer(l, tile):
            mode = tile['mode']
            ntok = tile['ntok']
            nseq = tile['nseq']
            Ls = ntok // nseq
            nsub = ntok // 128
            xn_r = [('xn', kc) for kc in range(KC)]

            def v3(ap2d):
                return ap2d.rearrange("p (s t) -> p s t", s=nseq)

            dma(lng_b[:], W['cm_ln_g'][l:l + 1, :].to_broadcast([128, 256]), [], ['lng'])
            dma(lnb_b[:], W['cm_ln_b'][l:l + 1, :].to_broadcast([128, 256]), [], ['lnb'])
            if mode == 'P':
                for g in range(4):
                    h2 = g % 2
                    dma(bsT[64 * h2:64 * h2 + 64, g // 2, :], W['cm_bs'][l, g:g + 1, :].to_broadcast([64, 128]), [], ['bsT'])
                dma(ws_tm, W['cm_ws'][l].rearrange("g t s -> t g s"), [], ['ct10'])
                for g in range(4):
                    tr(ps[4][:, 128 * g:128 * g + 128], ws_tm[:, g, :], 128, ['ct10'], ['ps4'])
                tt(wmT[:],
                   ps[4][:, :].rearrange("p (g t) -> p g t", g=4),
                   C('triUP').unsqueeze(1).to_broadcast([128, 4, 128]), ALU.mult, ['ps4', 'cst'], ['wmT'])
            else:
                for g in range(4):
                    h2 = g % 2
                    dma(bsT[64 * h2:64 * h2 + 64, g // 2, :].rearrange("p (s t) -> p s t", s=16),
                        W['cm_bs'][l, g:g + 1, 0:8].unsqueeze(1).to_broadcast([64, 16, 8]), [], ['bsT'])
                dma(ws8[:], W['cm_ws'][l, :, 0:8, 0:8].rearrange("g t s -> t g s"), [], ['ws8'])
                for g in range(4):
                    mm(ps[4][0:8, 128 * g:128 * g + 128], ws8[:, g, :], C('e8')[0:8, :], True, True, ['ws8', 'cst'], ['ps4'])
                cp(t18[:].rearrange("p g t -> p (g t)"), ps[4][0:8, :], ['ps4'], ['ct11'])
                for g in range(4):
                    mm(ps[5][:, 128 * g:128 * g + 128], C('e8')[0:8, :], t18[:, g, :], True, True, ['ct11', 'cst'], ['ps5'])
                tt(wmT[:], ps[5][:, :].rearrange("p (g t) -> p g t", g=4),
                   C('triUS').unsqueeze(1).to_broadcast([128, 4, 128]), ALU.mult, ['ps5', 'cst'], ['wmT'])
                dma(Sgla_s[:], st_gla[l].rearrange("s p v -> p s v"), [], ['Sgla_s'])
                for pair in range(2):
                    for h2 in range(2):
                        dma(Sgdn_s[64 * h2:64 * h2 + 64, :, pair, :], st_gdn[l, :, 2 * pair + h2].rearrange("s k v -> k s v"),
                            [], ['Sgdn_s'])
                dma(st_tm[:], st_gc[l], [], ['ct8', 'ct9'])
                for j in range(6):
                    tr(ps[4][:, 48 * j:48 * j + 48], st_tm[0:48, 128 * j:128 * j + 128], 48, ['ct8', 'ct9'], ['ps4'])
                cp(xp[:, :, 0:16 * 11].rearrange("p j (s t) -> p j s t", s=16)[:, :, :, 0:3],
                   ps[4][:, 0:288].rearrange("p (j s t) -> p j s t", j=6, s=16), ['ps4'], XPK)
                dma(st_tm[0:32, 0:256], st_sc[l], [], ['ct8', 'ct9'])
                for j in range(2):
                    tr(ps[4][:, 32 * j:32 * j + 32], st_tm[0:32, 128 * j:128 * j + 128], 32, ['ct8', 'ct9'], ['ps4'])
                cp(shp[:, :, 0:16 * 10].rearrange("p j (s t) -> p j s t", s=16)[:, :, :, 0:2],
                   ps[4][:, 0:64].rearrange("p (j s t) -> p j s t", j=2, s=16), ['ps4'], ['shp'])
            if mode == 'P':
                cp(xp[:, :, 0:3], halo_gc[:, l, :, :], ['halo_gc'], XPK)
                cp(shp[:, :, 0:2], halo_sc[:, l, :, :], ['halo_sc'], ['shp'])
            pe_src = (p_p[l, tile['t0']:tile['t0'] + ntok, :] if mode == 'P' else p_s[l])
            for s_ in range(nsub):
                dma(pe_tm[:], pe_src[128 * s_:128 * s_ + 128, :], [], ['pe_tm'])
                for j in range(2):
                    tr(ps[4][:, 128 * j:128 * j + 128], pe_tm[:, 128 * j:128 * j + 128], 128, ['pe_tm'], ['ps4'])
                cp(peT[:, :, 128 * s_:128 * s_ + 128], ps[4][:, 0:256].rearrange("p (j t) -> p j t", j=2), ['ps4'], ['peT'])

            xpv = xp[:, :, 0:nseq * (Ls + 3)].rearrange("p j (s t) -> p j s t", s=nseq)
            shv = shp[:, :, 0:nseq * (Ls + 2)].rearrange("p j (s t) -> p j s t", s=nseq)

            rmsnorm_to_xn(ntok, lambda kc: gcol('norm_mix', l, kc))

            wv, wk = slab('w_in', l, 0, D, 0, 512)
            proj_fm(ps[0], 'ps0', wv, wk, 0, 128, xn, 'xn', KC, ntok)
            cp(qT[:, 0:ntok], ps[0][:, 0:ntok], ['ps0'], ['qT'], eng='act')
            proj_fm(ps[1], 'ps1', wv, wk, 128, 128, xn, 'xn', KC, ntok)
            cp(kT[:, 0:ntok], ps[1][:, 0:ntok], ['ps1'], ['kT'])
            for s_ in range(nsub):
                pb = ps[2 + (s_ % 2)]
                pk = 'ps%d' % (2 + (s_ % 2))
                for kc in range(KC):
                    mm(pb[:, 0:256], xn[:, kc, 128 * s_:128 * s_ + 128], wv[:, kc, 256:512], kc == 0, kc == KC - 1,
                       [rk(wk, kc), ('xn', kc)], [pk])
                cp(gv_tm[:, s_, :], pb[:, 0:256], [pk], ['gv_tm'], eng='act')
            wv, wk = slab('w_in', l, 0, D, 512, 272)
            for j in range(2):
                pb, pk = ps[j], 'ps%d' % j
                proj_fm(pb, pk, wv, wk, 128 * j, 128, xn, 'xn', KC, ntok)
                act(grs[:, j, 0:ntok], pb[:, 0:ntok], AF.Silu, [pk], ['grs'])
            proj_fm(ps[2], 'ps2', wv, wk, 256, 16, xn, 'xn', KC, ntok)
            cp(gaT[0:16, 0:ntok], ps[2][0:16, 0:ntok], ['ps2'], ['gaT'])
            wv, wk = slab('w_in', l, 0, D, C_DQ, 512)
            for j in range(4):
                pb, pk = ps[j % 4], 'ps%d' % (j % 4)
                proj_fm(pb, pk, wv, wk, 128 * j, 128, xn, 'xn', KC, ntok)
                cp(xpv[:, j, :, 3:3 + Ls], v3(pb[:, 0:ntok]), [pk], [('xp', j)], eng=('act' if j % 2 else 'dve'))
            wv, wk = slab('w_in', l, 0, D, C_DV, 512)
            for j in range(4):
                pb, pk = ps[j % 4], 'ps%d' % (j % 4)
                proj_fm(pb, pk, wv, wk, 128 * j, 128, xn, 'xn', KC, ntok)
                if j < 2:
                    cp(xpv[:, 4 + j, :, 3:3 + Ls], v3(pb[:, 0:ntok]), [pk], [('xp', 4 + j)], eng=('act' if j % 2 else 'dve'))
                else:
                    act(dzs[:, j - 2, 0:ntok], pb[:, 0:ntok], AF.Silu, [pk], ['dzs'])
            wv, wk = slab('w_in', l, 0, D, C_DA, 8)
            for s_ in range(nsub):
                for kc in range(KC):
                    mm(ps[4][:, 8 * s_:8 * s_ + 8], xn[:, kc, 128 * s_:128 * s_ + 128], wv[:, kc, 0:8], kc == 0, kc == KC - 1,
                       [rk(wk, kc), ('xn', kc)], ['ps4'])
            cp(ab_tm[:, 0:nsub, :], ps[4][:, 0:8 * nsub].rearrange("p (s c) -> p s c", c=8), ['ps4'], ['ab_tm'])
            wv, wk = slab('w_in', l, 0, D, C_CU, 512)
            for j in range(2):
                pb, pk = ps[j], 'ps%d' % j
                proj_fm(pb, pk, wv, wk, 128 * j, 128, xn, 'xn', KC, ntok)
                gelu(cug[:, j, 0:ntok], 'cug', pb[:, 0:ntok], pk, ntok)
            for s_ in range(nsub):
                pb, pk = ps[2 + (s_ % 2)], 'ps%d' % (2 + (s_ % 2))
                for kc in range(KC):
                    mm(pb[:, 0:256], xn[:, kc, 128 * s_:128 * s_ + 128], wv[:, kc, 256:512], kc == 0, kc == KC - 1,
                       [rk(wk, kc), ('xn', kc)], [pk])
                gelu(cs[s_][:, :], 'cs%d' % s_, pb[:, 0:256], pk, 256)
            for s_ in range(nsub):
                gk_ = 'cs%d' % s_
                P.op('dve', lambda e, b=cs[s_]: e.reduce_sum(out=sm[0][:, 0:1], in_=b[:, :], axis=AX.X), [gk_], ['sm0'])
                ts(sm[0][:, 1:2], sm[0][:, 0:1], -1.0 / 256, None, ALU.mult, None, ['sm0'], ['sm0b'])
                ts(cs[4][:, :], cs[s_][:, :], sm[0][:, 1:2], None, ALU.add, None, [gk_, 'sm0b'], ['cs4'])
                tt(cs[5][:, :], cs[4][:, :], cs[4][:, :], ALU.mult, ['cs4'], ['cs5'])
                P.op('dve', lambda e: e.reduce_sum(out=sm[0][:, 2:3], in_=cs[5][:, :], axis=AX.X), ['cs5'], ['sm0c'])
                act(sm[0][:, 3:4], sm[0][:, 2:3], AF.Ln, ['sm0c'], ['sm0d'], bias=eps_col[:, 0:1], scale=1.0 / 256)
                act(sm[0][:, 4:5], sm[0][:, 3:4], AF.Exp, ['sm0d'], ['sm0e'], scale=-0.5)
                stt(cs[5][:, :], cs[4][:, :], sm[0][:, 4:5], lng_b[:, :], ALU.mult, ALU.mult, ['cs4', 'sm0e', 'lng'], ['cs5'])
                tt(vn_tm[:, s_, :], cs[5][:, :], lnb_b[:, :], ALU.add, ['cs5', 'lnb'], ['vn_tm'])
            if mode == 'S':
                dma(o_cv_s[l], vn_tm[:, 0, :], ['vn_tm'], [], is_out=True)
            wv, wk = slab('w_in', l, 0, D, C_SH, 512)
            for j in range(2):
                pb, pk = ps[j], 'ps%d' % j
                proj_fm(pb, pk, wv, wk, 128 * j, 128, xn, 'xn', KC, ntok)
                cp(tmpA[:, 0:ntok] if j == 0 else tmpB[:, 0:ntok], pb[:, 0:ntok], [pk], ['tmpA' if j == 0 else 'tmpB'],
                   eng='act')
            for j in range(2):
                pb, pk = ps[2 + j], 'ps%d' % (2 + j)
                proj_fm(pb, pk, wv, wk, 256 + 128 * j, 128, xn, 'xn', KC, ntok)
                cp(sbb[:, j, 0:ntok], pb[:, 0:ntok], [pk], ['sbb'], eng='act')
            wv, wk = slab('w_in', l, 0, D, C_SC, 256)
            for j in range(2):
                pb, pk = ps[j], 'ps%d' % j
                proj_fm(pb, pk, wv, wk, 128 * j, 128, xn, 'xn', KC, ntok)
                shsrc = tmpA if j == 0 else tmpB
                tt(shv[:, j, :, 2:2 + Ls], v3(pb[:, 0:ntok]), v3(shsrc[:, 0:ntok]), ALU.mult,
                   [pk, 'tmpA' if j == 0 else 'tmpB'], ['shp'])

            for j in range(2):
                yv = v3(tmpA[:, 0:ntok]) if j == 0 else v3(tmpB[:, 0:ntok])
                yk = 'tmpA' if j == 0 else 'tmpB'
                wcol = lambda jj: cvec2[:, 96 + l * 6 + jj * 2 + j:96 + l * 6 + jj * 2 + j + 1]
                ts(yv, shv[:, j, :, 0:Ls], wcol(0), None, ALU.mult, None, ['shp', 'cvec2'], [yk])
                for jj in (1, 2):
                    stt(yv, shv[:, j, :, jj:jj + Ls], wcol(jj), yv, ALU.mult, ALU.add, ['shp', 'cvec2', yk], [yk])
                tt(brT[:, 3, j, 0:ntok], sbb[:, j, 0:ntok], (tmpA if j == 0 else tmpB)[:, 0:ntok], ALU.mult,
                   ['sbb', yk], [('brT', 3)])
            sc_state_out(l, tile, shv, Ls)
            for j in range(6):
                wcol = lambda jj: cvec2[:, l * 24 + jj * 6 + j:l * 24 + jj * 6 + j + 1]
                cb, cbk = ((tmpC, 'tmpC'), (tmpA, 'tmpA'), (tmpB, 'tmpB'))[j % 3]
                yv = v3(cb[:, 0:ntok])
                ts(yv, xpv[:, j, :, 0:Ls], wcol(0), None, ALU.mult, None, [('xp', j), 'cvec2'], [cbk])
                for jj in (1, 2, 3):
                    stt(yv, xpv[:, j, :, jj:jj + Ls], wcol(jj), yv, ALU.mult, ALU.add, [('xp', j), 'cvec2', cbk], [cbk])
                act(qkv[:, j, 0:ntok], cb[:, 0:ntok], AF.Silu, [cbk], [('qkv', j)])
            gc_state_out(l, tile, xpv, Ls)
            for j in range(4):
                tt(sqs[:, j, 0:ntok], qkv[:, j, 0:ntok], qkv[:, j, 0:ntok], ALU.mult, [('qkv', j)], [('sqs', j)])
                mm(ps[0][:, 0:ntok], blk64b[:], sqs[:, j, 0:ntok], True, True, ['blk64b', ('sqs', j)], ['ps0'])
                act(lnt[:, 0:ntok], ps[0][:, 0:ntok], AF.Ln, ['ps0'], ['lnt'], bias=eps_col[:, 0:1], scale=1.0)
                act(rstd[:, 0:ntok], lnt[:, 0:ntok], AF.Exp, ['lnt'], ['rstd'], scale=-0.5)
                if j < 2:
                    stt(qkv[:, j, 0:ntok], qkv[:, j, 0:ntok], 0.125, rstd[:, 0:ntok], ALU.mult, ALU.mult,
                        [('qkv', j), 'rstd'], [('qkv', j)])
                else:
                    tt(qkv[:, j, 0:ntok], qkv[:, j, 0:ntok], rstd[:, 0:ntok], ALU.mult, [('qkv', j), 'rstd'], [('qkv', j)])
            for _ in gla_chunk(l, tile, 0):
                pass

            def side_work(c_):
                if c_ + 1 < nsub:
                    for _ in gla_chunk(l, tile, c_ + 1):
                        yield
                cm_chunk(l, tile, c_)
                yield

            for c in range(nsub):
                nxt = side_work(c)
                gdn_chunk(l, tile, c, tick=((lambda g_=nxt: next(g_, None)) if mode == 'P' else None))
                for _ in nxt:
                    pass
            if tile['last']:
                state_out(l, tile)
            P.barrier()

            merge(l, tile)
            ffn(l, tile)
            ple(l, tile)

        gelu_ctr = [0]

        def gelu(out, outkey, pin, pkey, n):
            gelu_ctr[0] += 1
            if gelu_ctr[0] % 2:
                xsb, xk, t2b, tk = tmpC, 'tmpC', lnt, 'lnt'
            else:
                xsb, xk, t2b, tk = tmpA, 'tmpA', tmpB, 'tmpB'
            xs = xsb[:, 0:n]
            cp(xs, pin, [pkey], [xk], eng='act')
            t2 = t2b[:, 0:n]
            tt(t2, xs, xs, ALU.mult, [xk], [tk])
            ts(t2, t2, 0.044715, 1.0, ALU.mult, ALU.add, [tk], [tk])
            tt(t2, t2, xs, ALU.mult, [tk, xk], [tk])
            act(t2, t2, AF.Sigmoid, [tk], [tk], scale=1.5957691216057308)
            tt(out, xs, t2, ALU.mult, [xk, tk], [outkey])

        def sc_state_out(l, tile, shv, Ls):
            mode = tile['mode']
            if mode == 'P':
                cp(halo_sc[:, l, :, :], shv[:, :, 0, Ls:Ls + 2], ['shp'], ['halo_sc'])
                if not tile['last']:
                    return
                for j in range(2):
                    tr(ps[4][0:2, 128 * j:128 * j + 128], shv[:, j, 0, Ls:Ls + 2], 128, ['shp'], ['ps4'])
                cp(st_tm[0:2, 0:256], ps[4][0:2, 0:256], ['ps4'], ['ct8', 'ct9'])
                dma(o_sc_p[l], st_tm[0:2, 0:256], ['ct8', 'ct9'], [], is_out=True)
            else:
                cp(cs[0][:, 0:64].rearrange("p (j s t) -> p j s t", j=2, s=16), shv[:, :, :, Ls:Ls + 2], ['shp'], ['cs0'])
                for j in range(2):
                    tr(ps[4][0:32, 128 * j:128 * j + 128], cs[0][:, 32 * j:32 * j + 32], 128, ['cs0'], ['ps4'])
                cp(st_tm[0:32, 0:256], ps[4][0:32, 0:256], ['ps4'], ['ct8', 'ct9'])
                dma(o_sc_s[l], st_tm[0:32, 0:256], ['ct8', 'ct9'], [], is_out=True)

        def gc_state_out(l, tile, xpv, Ls):
            mode = tile['mode']
            if mode == 'P':
                cp(halo_gc[:, l, :, :], xpv[:, :, 0, Ls:Ls + 3], XPK, ['halo_gc'])
                if not tile['last']:
                    return
                for j in range(6):
                    tr(ps[4 + j // 4][0:3, 128 * (j % 4):128 * (j % 4) + 128], xpv[:, j, 0, Ls:Ls + 3], 128, XPK,
                       ['ps%d' % (4 + j // 4)])
                cp(st_tm[0:3, 0:512], ps[4][0:3, 0:512], ['ps4'], ['ct8', 'ct9'])
                cp(st_tm[0:3, 512:768], ps[5][0:3, 0:256], ['ps5'], ['ct8', 'ct9'])
                dma(o_gc_p[l], st_tm[0:3, :], ['ct8', 'ct9'], [], is_out=True)
            else:
                cp(cs[1][:, 0:144].rearrange("p (j s t) -> p j s t", j=3, s=16), xpv[:, 0:3, :, Ls:Ls + 3], XPK, ['cs1'])
                cp(cs[2][:, 0:144].rearrange("p (j s t) -> p j s t", j=3, s=16), xpv[:, 3:6, :, Ls:Ls + 3], XPK, ['cs2'])
                for j in range(6):
                    srcb = cs[1] if j < 3 else cs[2]
                    srck = 'cs1' if j < 3 else 'cs2'
                    tr(ps[4 + j // 4][0:48, 128 * (j % 4):128 * (j % 4) + 128], srcb[:, 48 * (j % 3):48 * (j % 3) + 48], 128,
                       [srck], ['ps%d' % (4 + j // 4)])
                cp(st_tm[0:48, 0:512], ps[4][0:48, 0:512], ['ps4'], ['ct8', 'ct9'])
                cp(st_tm[0:48, 512:768], ps[5][0:48, 0:256], ['ps5'], ['ct8', 'ct9'])
                dma(o_gc_s[l], st_tm[0:48, :], ['ct8', 'ct9'], [], is_out=True)

        def state_out(l, tile):
            if tile['mode'] == 'P':
                dma(o_gla_p[l], Sgla_p[:, l, :], ['Sgla_p'], [], is_out=True)
                for pair in range(2):
                    for h2 in range(2):
                        dma(o_gdn_p[l, 2 * pair + h2], Sgdn_p[64 * h2:64 * h2 + 64, l, pair, :], ['Sgdn_p'], [], is_out=True)
            else:
                dma(o_gla_s[l].rearrange("s p v -> p s v"), Sgla_s[:], ['Sgla_s'], [], is_out=True)
                for pair in range(2):
                    for h2 in range(2):
                        dma(o_gdn_s[l, :, 2 * pair + h2].rearrange("s k v -> k s v"), Sgdn_s[64 * h2:64 * h2 + 64, :, pair, :],
                            ['Sgdn_s'], [], is_out=True)

        def gla_chunk(l, tile, c):
            mode = tile['mode']
            nseq = 1 if mode == 'P' else 16
            Lq = 128 // nseq
            tok = slice(128 * c, 128 * c + 128)
            triU = C('triU' + mode)

            def s3(ap):
                return ap.rearrange("p (s t) -> p s t", s=nseq)
            gcs = [xp[:, 3 + i // 2, 256 * (i % 2):256 * (i % 2) + 256] for i in range(6)]
            gct = [xp[:, i, 0:512] for i in range(3)]

            def gk(i, b):
                return ('xp', 3 + i // 2, 2 * (i % 2) + b)
            mm(ps[0][:, 0:128], gaT[:, tok], wa2_sb[:, 128 * l:128 * l + 128], True, True, ['gaT', 'gaT1', 'wa2', 'wa2b'], ['ps0'])
            e1 = gcs[0][:, 0:128]
            act(e1, ps[0][:, 0:128], AF.Exp, ['ps0'], [gk(0, 0)], scale=-1.0)
            sp_ = gcs[0][:, 128:256]
            act(sp_, e1, AF.Ln, [gk(0, 0)], [gk(0, 1)], bias=one_col[:, 0:1], scale=1.0)
            yield
            mm(ps[1][:, 0:128], sp_, triU, True, True, [gk(0, 1), 'cst'], ['ps1'])
            bT = gcs[1][:, 0:128]
            cp(bT, ps[1][:, 0:128], ['ps1'], [gk(1, 0)])
            eb = gcs[1][:, 128:256]
            act(eb, bT, AF.Exp, [gk(1, 0)], [gk(1, 1)], scale=-1.0 / 16)
            enb = gcs[2][:, 0:128]
            act(enb, bT, AF.Exp, [gk(1, 0)], [gk(2, 0)], scale=1.0 / 16)
            yield
            dif = gcs[2][:, 128:256]
            tt(s3(dif), s3(bT), s3(bT)[:, :, Lq - 1:Lq].to_broadcast([128, nseq, Lq]), ALU.subtract, [gk(1, 0)], [gk(2, 1)])
            act(dif, dif, AF.Exp, [gk(2, 1)], [gk(2, 1)], scale=1.0 / 16)
            dec = sm[1][:, 0:nseq]
            act(dec, s3(bT)[:, :, Lq - 1], AF.Exp, [gk(1, 0)], ['sm1'], scale=-1.0 / 16)
            yield
            qd = gcs[3][:, 0:128]
            stt(qd, qT[:, tok], float(32 ** -0.5), eb, ALU.mult, ALU.mult, ['qT', gk(1, 1)], [gk(3, 0)])
            kd = gcs[3][:, 128:256]
            tt(kd, kT[:, tok], enb, ALU.mult, ['kT', gk(2, 0)], [gk(3, 1)])
            ke = gcs[4][:, 0:128]
            tt(ke, kT[:, tok], dif, ALU.mult, ['kT', gk(2, 1)], [gk(4, 0)])
            qdm = gct[0]
            tt(qdm[:, :].rearrange("p (h t) -> p h t", h=4), qd.unsqueeze(1).to_broadcast([128, 4, 128]),
               C('headmask')[:, 0:4].unsqueeze(2).to_broadcast([128, 4, 128]), ALU.mult, [gk(3, 0), 'cst'], [('xp', 0)])
            yield
            for h in range(4):
                mm(ps[2][:, 128 * h:128 * h + 128], kd, qdm[:, 128 * h:128 * h + 128], True, True, [gk(3, 1), ('xp', 0)], ['ps2'])
            AT = gct[1]
            tt(AT[:, :].rearrange("p (h t) -> p h t", h=4), ps[2][:, :].rearrange("p (h t) -> p h t", h=4),
               triU.unsqueeze(1).to_broadcast([128, 4, 128]), ALU.mult, ['ps2', 'cst'], [('xp', 1)])
            yield
            tr(ps[3][:, 0:128], ke, 128, [gk(4, 0)], ['ps3'])
            kem = gct[2]
            tt(kem[:, :].rearrange("p (h t) -> p h t", h=4), ps[3][:, 0:128].unsqueeze(1).to_broadcast([128, 4, 128]),
               C('hmfree').rearrange("p (h t) -> p h t", h=4), ALU.mult, ['ps3', 'cst'], [('xp', 2)])
            yield
            Sk = 'Sgla_p' if mode == 'P' else 'Sgla_s'
            for h in range(4):
                h2, pair = h % 2, h // 2
                ob = ps[0][64 * h2:64 * h2 + 64, 128 * pair:128 * pair + 128]
                if mode == 'P':
                    mm(ob, gv_tm[:, c, 64 * h:64 * h + 64], AT[:, 128 * h:128 * h + 128], True, False, ['gv_tm', ('xp', 1)], ['ps0'])
                    mm(ob, Sgla_p[:, l, :], qdm[:, 128 * h:128 * h + 128], False, True, [Sk, ('xp', 0)], ['ps0'])
                else:
                    mm(ob, gv_tm[:, c, 64 * h:64 * h + 64], AT[:, 128 * h:128 * h + 128], True, False, ['gv_tm', ('xp', 1)], ['ps0'])
                    for s_ in range(16):
                        mm(ps[0][64 * h2:64 * h2 + 64, 128 * pair + 8 * s_:128 * pair + 8 * s_ + 8], Sgla_s[:, s_, :],
                           qdm[:, 128 * h + 8 * s_:128 * h + 8 * s_ + 8], False, s_ == 15, [Sk, ('xp', 0)], ['ps0'])
            yield
            if mode == 'P':
                for h in range(4):
                    mm(ps[1][:, 0:64], kem[:, 128 * h:128 * h + 128], gv_tm[:, c, 64 * h:64 * h + 64], h == 0, h == 3,
                       [('xp', 2), 'gv_tm'], ['ps1'])
                stt(Sgla_p[:, l, :], Sgla_p[:, l, :], dec[:, 0:1], ps[1][:, 0:64], ALU.mult, ALU.add,
                    [Sk, 'sm1', 'ps1'], [Sk])
            else:
                for half in range(2):
                    tt(um[:, :, :], gv_tm[:, c, :].unsqueeze(1).to_broadcast([128, 8, 256]),
                       C('seqmask')[:, 8 * half:8 * half + 8].unsqueeze(2).to_broadcast([128, 8, 256]), ALU.mult,
                       ['gv_tm', 'cst'], UMK)
                    for h in range(4):
                        mm(ps[1][:, :].rearrange("p (s v) -> p s v", s=8), kem[:, 128 * h:128 * h + 128],
                           um[:, :, 64 * h:64 * h + 64], h == 0, h == 3, [('xp', 2)] + UMK, ['ps1'])
                    sl = slice(8 * half, 8 * half + 8)
                    tt(Sgla_s[:, sl, :], Sgla_s[:, sl, :], dec[:, sl].unsqueeze(2).to_broadcast([128, 8, 64]), ALU.mult,
                       [Sk, 'sm1'], [Sk])
                    tt(Sgla_s[:, sl, :], Sgla_s[:, sl, :], ps[1][:, :].rearrange("p (s v) -> p s v", s=8), ALU.add,
                       [Sk, 'ps1'], [Sk])
            yield
            head_norm_gate(ps[0], 'ps0', cvec3[:, l:l + 1], grs, 'grs', 0, tok,
                           gcs[5], [gk(5, 0), gk(5, 1)], tmpA[:, 0:256], ['tmpA'], ps[3], 'ps3')
            yield

        one_col = sb("one_col", [128, 4])

        def head_norm_gate(pso, pskey, gaincol, gate, gatekey, bidx, tok, o_sb=None, ok=None, sq=None, sk=None,
                           psq=None, psqk=None):
            if o_sb is None:
                o_sb, ok, sq, sk, psq, psqk = cs[5], ['cs5'], cs[6], ['cs6'], ps[7], 'ps7'
            cp(o_sb[:, :], pso[:, 0:256], [pskey], ok, eng='act')
            tt(sq[:, :], o_sb[:, :], o_sb[:, :], ALU.mult, ok, sk)
            mm(psq[:, 0:256], C('blk64'), sq[:, :], True, True, ['cst'] + sk, [psqk])
            act(sq[:, :], psq[:, 0:256], AF.Ln, [psqk], sk, bias=eps_col[:, 0:1], scale=1.0 / 64)
            act(sq[:, :], sq[:, :], AF.Exp, sk, sk, scale=-0.5)
            stt(o_sb[:, :], o_sb[:, :], gaincol, sq[:, :], ALU.mult, ALU.mult, ok + sk + ['cvec3'], ok)
            tt(brT[:, bidx, :, tok], o_sb[:, :].rearrange("p (j t) -> p j t", j=2), gate[:, :, tok], ALU.mult,
               ok + [gatekey], [('brT', bidx)])

        def gdn_chunk(l, tile, c, tick=None):
            mode = tile['mode']
            nseq = 1 if mode == 'P' else 16
            Lq = 128 // nseq
            tok = slice(128 * c, 128 * c + 128)
            triU = C('triU' + mode)
            H4 = lambda ap: ap.rearrange("p (h t) -> p h t", h=4)
            for j in range(2):
                tr(ps[4][:, 128 * j:128 * j + 128], qkv[:, 2 + j, tok], 128, [('qkv', 2 + j)], ['ps4'])
                tr(ps[4][:, 256 + 128 * j:256 + 128 * j + 128], qkv[:, 4 + j, tok], 128, [('qkv', 4 + j)], ['ps4'])
            k_tm = cs[0]
            v_tm = cs[1]
            cp(k_tm[:, :], ps[4][:, 0:256], ['ps4'], ['cs0'], eng='act')
            cp(v_tm[:, :], ps[4][:, 256:512], ['ps4'], ['cs1'])
            g_ = sm[2]
            tt(g_[:, 0:4], ab_tm[:, c, 0:4], dtb_b[:, 4 * l:4 * l + 4], ALU.add, ['ab_tm', 'dtb'], ['sm2'])
            act(g_[:, 0:4], g_[:, 0:4], AF.Exp, ['sm2'], ['sm2'])
            act(g_[:, 0:4], g_[:, 0:4], AF.Ln, ['sm2'], ['sm2'], bias=one_col[:, 0:1], scale=1.0)
            tt(g_[:, 0:4], g_[:, 0:4], nega_b[:, 4 * l:4 * l + 4], ALU.mult, ['sm2', 'nega'], ['sm2'])
            be = sm[3]
            act(be[:, 0:4], ab_tm[:, c, 4:8], AF.Exp, ['ab_tm'], ['sm3'], scale=-1.0)
            ts(be[:, 0:4], be[:, 0:4], 1.0, None, ALU.add, None, ['sm3'], ['sm3'])
            recip(be[:, 0:4], be[:, 0:4], ['sm3'], ['sm3'])
            gbc = ct[0]
            cp(H4(gbc[:, :]), g_[:, 0:4].unsqueeze(2).to_broadcast([128, 4, 128]), ['sm2'], ['ct0'])
            for h in range(4):
                mm(ps[5][:, 128 * h:128 * h + 128], gbc[:, 128 * h:128 * h + 128], triU, True, True, ['ct0', 'cst'], ['ps5'])
            mm(ps[6][:, 0:4], triU, g_[:, 0:4], True, True, ['cst', 'sm2'], ['ps6'])
            mm(ps[6][:, 4:8], C('tot' + mode), g_[:, 0:4], True, True, ['cst', 'sm2'], ['ps6'])
            Gc = sm[4]
            cp(Gc[:, 0:8], ps[6][:, 0:8], ['ps6'], ['sm4'])
            Gb = ct[1]
            cp(Gb[:, :], ps[5][:, :], ['ps5'], ['ct1'], eng='act')
            d_ = ct[2]
            tt(H4(d_[:, :]), H4(Gb[:, :]), Gc[:, 0:4].unsqueeze(2).to_broadcast([128, 4, 128]), ALU.subtract, ['ct1', 'sm4'], ['ct2'])
            dL = ct[3]
            tt(H4(dL[:, :]), H4(d_[:, :]), C('bigL' + mode).unsqueeze(1).to_broadcast([128, 4, 128]), ALU.add, ['ct2', 'cst'], ['ct3'])
            act(dL[:, :], dL[:, :], AF.Exp, ['ct3'], ['ct3'], scale=-1.0)
            dU = ct[4]
            tt(H4(dU[:, :]), H4(d_[:, :]), C('bigU' + mode).unsqueeze(1).to_broadcast([128, 4, 128]), ALU.subtract, ['ct2', 'cst'], ['ct4'])
            act(dU[:, :], dU[:, :], AF.Exp, ['ct4'], ['ct4'])
            eG = sm[5]
            act(eG[:, 0:4], Gc[:, 0:4], AF.Exp, ['sm4'], ['sm5'])
            bw = sm[6]
            tt(bw[:, 0:4], eG[:, 0:4], be[:, 0:4], ALU.mult, ['sm5', 'sm3'], ['sm6'])
            ts(bw[:, 4:8], bw[:, 0:4], -1.0, None, ALU.mult, None, ['sm6'], ['sm6'])
            ek = sm[7]
            tt(ek[:, 0:4], Gc[:, 4:8], Gc[:, 0:4], ALU.subtract, ['sm4'], ['sm7'])
            act(ek[:, 0:4], ek[:, 0:4], AF.Exp, ['sm7'], ['sm7'])
            hm2 = C('blk64').rearrange("p (a b) -> p a b", a=2)[:, :, 0]
            kmask, qmask = ct[0], ct[2]
            for pair in range(2):
                tt(H4(kmask[:, :])[:, 2 * pair:2 * pair + 2, :], qkv[:, 2 + pair, tok].unsqueeze(1).to_broadcast([128, 2, 128]),
                   hm2.unsqueeze(2).to_broadcast([128, 2, 128]), ALU.mult, [('qkv', 2 + pair), 'cst'], ['ct0'])
                tt(H4(qmask[:, :])[:, 2 * pair:2 * pair + 2, :], qkv[:, pair, tok].unsqueeze(1).to_broadcast([128, 2, 128]),
                   hm2.unsqueeze(2).to_broadcast([128, 2, 128]), ALU.mult, [('qkv', pair), 'cst'], ['ct2'])
            for h in range(4):
                h2, pair = h % 2, h // 2
                mm(ps[6][:, 128 * h:128 * h + 128], qkv[:, 2 + pair, tok], kmask[:, 128 * h:128 * h + 128], True, True,
                   [('qkv', 2 + pair), 'ct0'], ['ps6'])
                mm(ps[7][:, 128 * h:128 * h + 128], qkv[:, 2 + pair, tok], qmask[:, 128 * h:128 * h + 128], True, True,
                   [('qkv', 2 + pair), 'ct2'], ['ps7'])
            Nm = ct[5]
            tt(H4(Nm[:, :]), H4(dL[:, :]), C('strictL' + mode).unsqueeze(1).to_broadcast([128, 4, 128]), ALU.mult, ['ct3', 'cst'], ['ct5'])
            tt(Nm[:, :], Nm[:, :], ps[6][:, :], ALU.mult, ['ct5', 'ps6'], ['ct5'])
            tt(H4(Nm[:, :]), H4(Nm[:, :]), be[:, 0:4].unsqueeze(2).to_broadcast([128, 4, 128]), ALU.mult, ['ct5', 'sm3'], ['ct5'])
            qkT = ct[6]
            tt(qkT[:, :], dU[:, :], ps[7][:, :], ALU.mult, ['ct4', 'ps7'], ['ct6'])
            RU = cs[2]
            tt(RU[:, :].rearrange("p (h v) -> p h v", h=4), v_tm[:, :].rearrange("p (h v) -> p h v", h=4),
               be[:, 0:4].unsqueeze(2).to_broadcast([128, 4, 64]), ALU.mult, ['cs1', 'sm3'], ['cs2'])
            RW = cs[3]
            tt(RW[:, :].rearrange("p (h v) -> p h v", h=4), k_tm[:, :].rearrange("p (h v) -> p h v", h=4),
               bw[:, 4:8].unsqueeze(2).to_broadcast([128, 4, 64]), ALU.mult, ['cs0', 'sm6'], ['cs3'])
            kend = cs[4]
            tt(kend[:, :].rearrange("p (h v) -> p h v", h=4), k_tm[:, :].rearrange("p (h v) -> p h v", h=4),
               ek[:, 0:4].unsqueeze(2).to_broadcast([128, 4, 64]), ALU.mult, ['cs0', 'sm7'], ['cs4'])
            for h in range(4):
                tr(ps[5][:, 128 * h:128 * h + 128], Nm[:, 128 * h:128 * h + 128], 128, ['ct5'], ['ps5'])
            NT = ct[7]
            cp(NT[:, :], ps[5][:, :], ['ps5'], ['ct7'], eng='act')
            PT = ct[8]
            ts(PT[:, :], NT[:, :], -1.0, None, ALU.mult, None, ['ct7'], ['ct8'])
            tt(H4(PT[:, :]), H4(PT[:, :]), C('ident').unsqueeze(1).to_broadcast([128, 4, 128]), ALU.add, ['ct8', 'cst'], ['ct8'])
            A_, AT_, B_, BT_ = Nm, NT, ct[9], ct[10]
            Ak, ATk, Bk, BTk = 'ct5', 'ct7', 'ct9', 'ct10'
            nlev = 6 if mode == 'P' else 2
            psets = ((ps[5], 'ps5', ps[6], 'ps6', ps[7], 'ps7'), (ps[5], 'ps5', ps[6], 'ps6', ps[7], 'ps7'))
            for lev in range(nlev):
                last = (lev == nlev - 1)
                for hf in range(2):
                    pB, pBk, pBT, pBTk, pP, pPk = psets[hf]
                    hc = slice(256 * hf, 256 * hf + 256)
                    for h in (2 * hf, 2 * hf + 1):
                        hs = slice(128 * h, 128 * h + 128)
                        mm(pB[:, hs], AT_[:, hs], A_[:, hs], True, True, [(Ak, hf), (ATk, hf)], [pBk])
                        if not last:
                            mm(pBT[:, hs], A_[:, hs], AT_[:, hs], True, True, [(Ak, hf), (ATk, hf)], [pBTk])
                    cp(B_[:, hc], pB[:, hc], [pBk], [(Bk, hf)], eng='act')
                    if not last:
                        cp(BT_[:, hc], pBT[:, hc], [pBTk], [(BTk, hf)])
                    for h in (2 * hf, 2 * hf + 1):
                        hs = slice(128 * h, 128 * h + 128)
                        mm(pP[:, hs], B_[:, hs], PT[:, hs], True, True, [(Bk, hf), ('ct8', hf)], [pPk])
                    tt(PT[:, hc], PT[:, hc], pP[:, hc], ALU.add, [('ct8', hf), pPk], [('ct8', hf)])
                    if tick is not None:
                        tick()
                A_, AT_, B_, BT_ = B_, BT_, A_, AT_
                Ak, ATk, Bk, BTk = Bk, BTk, Ak, ATk
            if tick is not None:
                tick()
            for h in range(4):
                h2, pair = h % 2, h // 2
                mm(ps[5][64 * h2:64 * h2 + 64, 128 * pair:128 * pair + 128], RW[:, 64 * h:64 * h + 64],
                   PT[:, 128 * h:128 * h + 128], True, True, ['cs3', 'ct8'], ['ps5'])
            nWT = cs[5]
            cp(nWT[:, :], ps[5][:, 0:256], ['ps5'], ['cs5'], eng='act')
            eGb = ct[9]
            act(eGb[:, :], Gb[:, :], AF.Exp, ['ct1'], ['ct9'])
            qg = cs[6]
            for pair in range(2):
                for h2 in range(2):
                    h = 2 * pair + h2
                    tt(qg[64 * h2:64 * h2 + 64, 128 * pair:128 * pair + 128], qkv[64 * h2:64 * h2 + 64, pair, tok],
                       eGb[64 * h2:64 * h2 + 64, 128 * h:128 * h + 128], ALU.mult, [('qkv', pair), 'ct9'], ['cs6'])
            nWTm, qgm = ct[5], ct[7]
            tt(nWTm[:, :].rearrange("p (a x) -> p a x", a=2), nWT[:, :].unsqueeze(1).to_broadcast([128, 2, 256]),
               hm2.unsqueeze(2).to_broadcast([128, 2, 256]), ALU.mult, ['cs5', 'cst'], ['ct5'])
            tt(qgm[:, :].rearrange("p (a x) -> p a x", a=2), qg[:, :].unsqueeze(1).to_broadcast([128, 2, 256]),
               hm2.unsqueeze(2).to_broadcast([128, 2, 256]), ALU.mult, ['cs6', 'cst'], ['ct7'])
            u_sb = cs[7]
            Sk = 'Sgdn_p' if mode == 'P' else 'Sgdn_s'
            if mode == 'P':
                for h in range(4):
                    h2, pair = h % 2, h // 2
                    hp = slice(64 * h2, 64 * h2 + 64)
                    mm(ps[6][:, 64 * h:64 * h + 64], PT[:, 128 * h:128 * h + 128], RU[:, 64 * h:64 * h + 64], True, False,
                       ['ct8', 'cs2'], ['ps6'])
                    mm(ps[6][:, 64 * h:64 * h + 64], nWTm[:, 256 * h2 + 128 * pair:256 * h2 + 128 * pair + 128], Sgdn_p[:, l, pair, :],
                       False, True, ['ct5', Sk], ['ps6'])
                cp(u_sb[:, :], ps[6][:, 0:256], ['ps6'], ['cs7'])
                if tick is not None:
                    tick()
                for h in range(4):
                    h2, pair = h % 2, h // 2
                    hp = slice(64 * h2, 64 * h2 + 64)
                    ob = ps[4][hp, 128 * pair:128 * pair + 128]
                    mm(ob, Sgdn_p[:, l, pair, :], qgm[:, 256 * h2 + 128 * pair:256 * h2 + 128 * pair + 128], True, False, [Sk, 'ct7'], ['ps4'])
                    mm(ob, u_sb[:, 64 * h:64 * h + 64], qkT[:, 128 * h:128 * h + 128], False, True, ['cs7', 'ct6'], ['ps4'])
                for h in range(4):
                    h2, pair = h % 2, h // 2
                    hp = slice(64 * h2, 64 * h2 + 64)
                    mm(ps[7][hp, 64 * pair:64 * pair + 64], kend[:, 64 * h:64 * h + 64], u_sb[:, 64 * h:64 * h + 64], True, True,
                       ['cs4', 'cs7'], ['ps7'])
                for pair in range(2):
                    for h2 in range(2):
                        h = 2 * pair + h2
                        hp = slice(64 * h2, 64 * h2 + 64)
                        stt(Sgdn_p[hp, l, pair, :], Sgdn_p[hp, l, pair, :], eGb[hp, 128 * h + 127:128 * h + 128],
                            ps[7][hp, 64 * pair:64 * pair + 64], ALU.mult, ALU.add, [Sk, 'ct9', 'ps7'], [Sk])
            else:
                for h in range(4):
                    h2, pair = h % 2, h // 2
                    hp = slice(64 * h2, 64 * h2 + 64)
                    for s_ in range(16):
                        o_ = 256 * h2 + 128 * pair + 8 * s_
                        mm(ps[6][hp, 128 * pair + 8 * s_:128 * pair + 8 * s_ + 8], Sgdn_s[:, s_, pair, :],
                           nWTm[:, o_:o_ + 8], True, True, [Sk, 'ct5'], ['ps6'])
                cp(ct[10][:, 0:256], ps[6][:, 0:256], ['ps6'], ['ct10'])
                for pair in range(2):
                    tr(ps[7][:, 128 * pair:128 * pair + 128], ct[10][:, 128 * pair:128 * pair + 128], 128, ['ct10'], ['ps7'])
                for h in range(4):
                    mm(ps[6][:, 256 + 64 * h:256 + 64 * h + 64], PT[:, 128 * h:128 * h + 128], RU[:, 64 * h:64 * h + 64], True, True,
                       ['ct8', 'cs2'], ['ps6'])
                cp(u_sb[:, :], ps[6][:, 256:512], ['ps6'], ['cs7'])
                tt(u_sb[:, :], u_sb[:, :], ps[7][:, 0:256], ALU.add, ['cs7', 'ps7'], ['cs7'])
                for h in range(4):
                    h2, pair = h % 2, h // 2
                    hp = slice(64 * h2, 64 * h2 + 64)
                    ob = ps[4][hp, 128 * pair:128 * pair + 128]
                    mm(ob, u_sb[:, 64 * h:64 * h + 64], qkT[:, 128 * h:128 * h + 128], True, False, ['cs7', 'ct6'], ['ps4'])
                    for s_ in range(16):
                        o_ = 256 * h2 + 128 * pair + 8 * s_
                        mm(ps[4][hp, 128 * pair + 8 * s_:128 * pair + 8 * s_ + 8], Sgdn_s[:, s_, pair, :],
                           qgm[:, o_:o_ + 8], False, s_ == 15, [Sk, 'ct7'], ['ps4'])
                for half in range(2):
                    sl = slice(8 * half, 8 * half + 8)
                    tt(um[:, :, :], u_sb[:, :].unsqueeze(1).to_broadcast([128, 8, 256]),
                       C('seqmask')[:, 8 * half:8 * half + 8].unsqueeze(2).to_broadcast([128, 8, 256]), ALU.mult,
                       ['cs7', 'cst'], UMK)
                    for pair in range(2):
                        pb, pk = (ps[5], 'ps5') if pair == 0 else (ps[7], 'ps7')
                        for h2 in range(2):
                            h = 2 * pair + h2
                            hp = slice(64 * h2, 64 * h2 + 64)
                            mm(pb[hp, :].rearrange("p (s v) -> p s v", s=8), kend[:, 64 * h:64 * h + 64],
                               um[:, :, 64 * h:64 * h + 64], True, True, ['cs4'] + UMK, [pk])
                        for h2 in range(2):
                            h = 2 * pair + h2
                            hp = slice(64 * h2, 64 * h2 + 64)
                            dnv = eGb[hp, 128 * h:128 * h + 128].rearrange("p (s t) -> p s t", s=16)[:, sl, 7:8]
                            tt(Sgdn_s[hp, sl, pair, :], Sgdn_s[hp, sl, pair, :], dnv.to_broadcast([64, 8, 64]), ALU.mult,
                               [Sk, 'ct9'], [Sk])
                            tt(Sgdn_s[hp, sl, pair, :], Sgdn_s[hp, sl, pair, :], pb[hp, :].rearrange("p (s v) -> p s v", s=8),
                               ALU.add, [Sk, pk], [Sk])
            head_norm_gate(ps[4], 'ps4', cvec3[:, 8 + l:8 + l + 1], dzs, 'dzs', 1, tok)

        def cm_chunk(l, tile, c):
            tok = slice(128 * c, 128 * c + 128)
            for g in range(4):
                h2, pair = g % 2, g // 2
                mm(ps[1][64 * h2:64 * h2 + 64, 128 * pair:128 * pair + 128], vn_tm[:, c, 64 * g:64 * g + 64], wmT[:, g, :], True, True,
                   ['vn_tm', 'wmT'], ['ps1'])
            s_sb = tmpB[:, 0:256]
            P.op('dve', lambda e: e.tensor_tensor(out=s_sb, in0=ps[1][:, 0:256],
                                                  in1=bsT[:, :, :].rearrange("p j t -> p (j t)"), op=ALU.add),
                 ['ps1', 'bsT'], ['tmpB'], strict=True)
            tt(brT[:, 2, :, tok], cug[:, :, tok], s_sb.rearrange("p (j t) -> p j t", j=2), ALU.mult, ['cug', 'tmpB'],
               [('brT', 2)])

        def merge(l, tile):
            ntok = tile['ntok']
            for g in range(4):
                wb_lo, wbk_lo = slab_wbranch(l, g, 0)
                wg0, wgk0 = slab('w_gate', l, 0, D, 1024 * g, 512)
                half_groups(l, g, 0, wb_lo, wbk_lo, wg0, wgk0, ntok)
                wb_hi, wbk_hi = slab_wbranch(l, g, 1)
                wg1, wgk1 = slab('w_gate', l, 0, D, 1024 * g + 512, 512)
                half_groups(l, g, 1, wb_hi, wbk_hi, wg1, wgk1, ntok)
            for og in range(KC):
                cp(sqs[:, og, 0:ntok], ct[og][:, 0:ntok], ['ct%d' % og], [('sqs', og)], eng=('act' if og % 2 else 'dve'))
            for half in range(2):
                wv, wk = slab('w_o', l, 0, D, 512 * half, 512)
                for j in range(4):
                    og = 4 * half + j
                    pb, pk = ps[og % 2], 'ps%d' % (og % 2)
                    proj_fm(pb, pk, wv, wk, 128 * j, 128, sqs, 'sqs', KC, ntok)
                    tt(hT[:, og, 0:ntok], hT[:, og, 0:ntok], pb[:, 0:ntok], ALU.add, [('h', og), pk], [('h', og)])
                    sq_h(og, ntok, 'xn')

        def slab_wbranch(l, g, half):
            spec_rows = 256
            v, k = slab_rows('w_branch', (l, g), spec_rows, 512 * half, 512)
            return v, k

        def slab_rows(wname, idx, nrows, c0, ncols):
            spec = (wname, idx, 0, nrows, c0, ncols)
            return slab(*spec)

        def half_groups(l, g, half, wb, wbk, wg, wgk, ntok):
            for j in range(4):
                og = 4 * half + j
                pg, pgk = ps[0 + 2 * (j % 2)], 'ps%d' % (0 + 2 * (j % 2))
                pbr, pbk = ps[1 + 2 * (j % 2)], 'ps%d' % (1 + 2 * (j % 2))
                proj_fm(pg, pgk, wg, wgk, 128 * j, 128, xn, 'xn', KC, ntok)
                for kc in range(2):
                    mm(pbr[:, 0:ntok], wb[:, kc, 128 * j:128 * j + 128], brT[:, g, kc, 0:ntok], kc == 0, kc == 1,
                       [rk(wbk, kc), ('brT', g)], [pbk])
                sg = tmpA if j % 2 == 0 else tmpB
                sgk = 'tmpA' if j % 2 == 0 else 'tmpB'
                act(sg[:, 0:ntok], pg[:, 0:ntok], AF.Sigmoid, [pgk], [sgk])
                if g == 0:
                    tt(ct[og][:, 0:ntok], sg[:, 0:ntok], pbr[:, 0:ntok], ALU.mult, [sgk, pbk], ['ct%d' % og])
                else:
                    tt(sg[:, 0:ntok], sg[:, 0:ntok], pbr[:, 0:ntok], ALU.mult, [sgk, pbk], [sgk])
                    tt(ct[og][:, 0:ntok], ct[og][:, 0:ntok], sg[:, 0:ntok], ALU.add, ['ct%d' % og, sgk], ['ct%d' % og])

        def ffn(l, tile):
            ntok = tile['ntok']
            rmsnorm_to_xn(ntok, lambda kc: gcol('norm_ffn', l, kc), 'xn')
            nslab = (DFF + 511) // 512
            for s_ in range(nslab):
                c0 = 512 * s_
                ncols = min(512, DFF - c0)
                wgv, wgk = slab('w_ffn_gate', l, 0, D, c0, ncols)
                wuv, wuk = slab('w_ffn_up', l, 0, D, c0, ncols)
                for j in range(ncols // 128):
                    fg = 4 * s_ + j
                    pg, pgk = ps[0 + 2 * (j % 2)], 'ps%d' % (0 + 2 * (j % 2))
                    pu, puk = ps[1 + 2 * (j % 2)], 'ps%d' % (1 + 2 * (j % 2))
                    proj_fm(pg, pgk, wgv, wgk, 128 * j, 128, xn, 'xn', KC, ntok)
                    proj_fm(pu, puk, wuv, wuk, 128 * j, 128, xn, 'xn', KC, ntok)
                    sg = tmpA if j % 2 == 0 else tmpB
                    sgk = 'tmpA' if j % 2 == 0 else 'tmpB'
                    act(sg[:, 0:ntok], pg[:, 0:ntok], AF.Silu, [pgk], [sgk])
                    tt(hid[:, fg, 0:ntok], sg[:, 0:ntok], pu[:, 0:ntok], ALU.mult, [sgk, puk], ['ct%d' % (fg // 2)])
            for og in range(KC):
                wv, wk = slab('w_ffn_down', l, 0, DFF, 128 * og, 128)
                pb, pk = ps[og % 2], 'ps%d' % (og % 2)
                for fc in range(FC):
                    mm(pb[:, 0:ntok], wv[:, fc, :], hid[:, fc, 0:ntok], fc == 0, fc == FC - 1, [rk(wk, fc), 'ct%d' % (fc // 2)], [pk])
                tt(hT[:, og, 0:ntok], hT[:, og, 0:ntok], pb[:, 0:ntok], ALU.add, [('h', og), pk], [('h', og)])
                sq_h(og, ntok, 'sqs')

        def ple(l, tile):
            ntok = tile['ntok']
            rmsnorm_to_xn(ntok, lambda kc: gcol('norm_ple', l, kc))
            for half in range(2):
                wgv, wgk = slab('w_ple_gate', l, 0, D, 512 * half, 512)
                wpv, wpk = slab('w_ple', l, 0, 256, 512 * half, 512)
                for j in range(4):
                    og = 4 * half + j
                    pg, pgk = ps[0 + 2 * (j % 2)], 'ps%d' % (0 + 2 * (j % 2))
                    pp, ppk = ps[1 + 2 * (j % 2)], 'ps%d' % (1 + 2 * (j % 2))
                    proj_fm(pg, pgk, wgv, wgk, 128 * j, 128, xn, 'xn', KC, ntok)
                    for kc in range(2):
                        mm(pp[:, 0:ntok], wpv[:, kc, 128 * j:128 * j + 128], peT[:, kc, 0:ntok], kc == 0, kc == 1,
                           [rk(wpk, kc), 'peT'], [ppk])
                    sg = tmpA if j % 2 == 0 else tmpB
                    sgk = 'tmpA' if j % 2 == 0 else 'tmpB'
                    act(sg[:, 0:ntok], pg[:, 0:ntok], AF.Sigmoid, [pgk], [sgk])
                    tt(sg[:, 0:ntok], sg[:, 0:ntok], pp[:, 0:ntok], ALU.mult, [sgk, ppk], [sgk])
                    tt(hT[:, og, 0:ntok], hT[:, og, 0:ntok], sg[:, 0:ntok], ALU.add, [('h', og), sgk], [('h', og)])
                    sq_h(og, ntok, 'sqs', eng='dve')

        def load_tile(tile):
            ntok = tile['ntok']
            src = x_p[tile['t0']:tile['t0'] + ntok, :] if tile['mode'] == 'P' else x_s
            for s_ in range(ntok // 128):
                dma(x_tm[:], src[128 * s_:128 * s_ + 128, :], [], ['tmpA', 'tmpB'])
                for half in range(2):
                    pb, pk = ps[4 + half], 'ps%d' % (4 + half)
                    for j in range(4):
                        kc = 4 * half + j
                        tr(pb[:, 128 * j:128 * j + 128], x_tm[:, 128 * kc:128 * kc + 128], 128, ['tmpA', 'tmpB'], [pk])
                    cp(hT[:, 4 * half:4 * half + 4, 128 * s_:128 * s_ + 128], pb[:, :].rearrange("p (j t) -> p j t", j=4),
                       [pk], [('h', 4 * half + j) for j in range(4)], eng=('act' if half else 'dve'))
            for kc in range(KC):
                sq_h(kc, ntok, 'sqs')

        def store_tile(tile):
            ntok = tile['ntok']
            dst = y_p[tile['t0']:tile['t0'] + ntok, :] if tile['mode'] == 'P' else y_s
            for kc in range(KC):
                mm(ps[0][:, 0:ntok], onesb[:], sqs[:, kc, 0:ntok], kc == 0, kc == KC - 1, ['onesb', ('sqs', kc)], ['ps0'])
            act(lnt[:, 0:ntok], ps[0][:, 0:ntok], AF.Ln, ['ps0'], ['lnt'], bias=eps_col[:, 0:1], scale=1.0 / D)
            act(rstd[:, 0:ntok], lnt[:, 0:ntok], AF.Exp, ['lnt'], ['rstd'], scale=-0.5)
            for kc in range(KC):
                stt(hT[:, kc, 0:ntok], hT[:, kc, 0:ntok], cvec[:, 96 + kc:96 + kc + 1], rstd[:, 0:ntok], ALU.mult, ALU.mult,
                    [('h', kc), 'rstd', 'cvec'], [('h', kc)])
            for s_ in range(ntok // 128):
                for half in range(2):
                    pb, pk = ps[4 + half], 'ps%d' % (4 + half)
                    for j in range(4):
                        kc = 4 * half + j
                        tr(pb[:, 128 * j:128 * j + 128], hT[:, kc, 128 * s_:128 * s_ + 128], 128, [('h', kc)], [pk])
                    cp(x_tm[:, 512 * half:512 * half + 512], pb[:, :], [pk], ['tmpA', 'tmpB'], eng=('act' if half else 'dve'))
                dma(dst[128 * s_:128 * s_ + 128, :], x_tm[:], ['tmpA', 'tmpB'], [], is_out=True)

        tiles = []
        for t in range(NT):
            tiles.append(dict(mode='P', ntok=TT, nseq=1, t0=t * TT, last=(t == NT - 1)))
        if do_sample:
            tiles.append(dict(mode='S', ntok=128, nseq=16, t0=0, last=True))

        def program():
            for tile in tiles:
                load_tile(tile)
                for l in range(L):
                    layer(l, tile)
                store_tile(tile)

        P.planning = True
        program()
        P.planning = False
        memset(eps_col[:], EPS, ['eps_col'])
        memset(one_col[:], 1.0, ['one_col'])
        try:
            setup()
            program()
        except StopBuild:
            pass
        P.finish()
        block = es.enter_context(nc.Block())
        P.emit(block)
    return nc


_NC_CACHE = {}


def make_in_maps(inputs, TP, DEPTH, ncores):
    f = lambda a: np.ascontiguousarray(np.asarray(a, dtype=np.float32))
    wnames = ['norm_mix', 'w_in', 'gla_wa2', 'gla_ba', 'gla_norm', 'gdn_conv_w', 'gdn_a_log', 'gdn_dt_bias', 'gdn_norm',
              'cm_ln_g', 'cm_ln_b', 'cm_ws', 'cm_bs', 'sc_conv_w', 'w_gate', 'w_branch', 'w_o', 'norm_ffn', 'w_ffn_gate',
              'w_ffn_up', 'w_ffn_down', 'norm_ple', 'w_ple_gate', 'w_ple', 'norm_final']
    wd = {k: f(inputs[k]) for k in wnames}
    maps = []
    for b in range(ncores):
        sl = slice(16 * b, 16 * b + 16)
        m = dict(wd)
        m['x_p'] = f(inputs['x_prompt'][b])
        m['x_s'] = f(inputs['x_sample'][sl]).reshape(128, D)
        m['st_gla'] = f(inputs['state_gla'][:, sl]).reshape(DEPTH, 16, 128, 64)
        m['st_gdn'] = f(inputs['state_gdn'][:, sl])
        m['st_gc'] = f(inputs['state_gdn_conv'][:, sl]).reshape(DEPTH, 48, 768)
        m['st_sc'] = f(inputs['state_sconv'][:, sl]).reshape(DEPTH, 32, 256)
        m['p_p'] = f(inputs['p_prompt'][:, b])
        m['p_s'] = f(inputs['p_sample'][:, sl]).reshape(DEPTH, 128, 256)
        m['consts'] = CONST_ARR
        maps.append(m)
    return maps


def gather(results, TP, DEPTH, ncores):
    R = results
    cat = lambda k, ax: np.concatenate([np.asarray(r[k], dtype=np.float32) for r in R], axis=ax)
    y_prompt = np.stack([np.asarray(r['y_p'], np.float32) for r in R], 0)
    y_sample = cat('y_s', 0).reshape(ncores * 16, 8, D)
    gla_p = np.stack([np.asarray(r['o_gla_p'], np.float32).reshape(DEPTH, 4, 32, 64) for r in R], 1)
    gla_s = np.concatenate([np.asarray(r['o_gla_s'], np.float32).reshape(DEPTH, 16, 4, 32, 64) for r in R], 1)
    gdn_p = np.stack([np.asarray(r['o_gdn_p'], np.float32) for r in R], 1)
    gdn_s = np.concatenate([np.asarray(r['o_gdn_s'], np.float32) for r in R], 1)
    gc_p = np.stack([np.asarray(r['o_gc_p'], np.float32) for r in R], 1)
    gc_s = np.concatenate([np.asarray(r['o_gc_s'], np.float32).reshape(DEPTH, 16, 3, 768) for r in R], 1)
    sc_p = np.stack([np.asarray(r['o_sc_p'], np.float32) for r in R], 1)
    sc_s = np.concatenate([np.asarray(r['o_sc_s'], np.float32).reshape(DEPTH, 16, 2, 256) for r in R], 1)
    cv_s = np.concatenate([np.asarray(r['o_cv_s'], np.float32).reshape(DEPTH, 16, 8, 256) for r in R], 1)
    return (y_prompt, y_sample, gla_p, gla_s, gdn_p, gdn_s, gc_p, gc_s, sc_p, sc_s, cv_s)


def kernel(**inputs):
    TP = int(np.asarray(inputs['x_prompt']).shape[1])
    DEPTH = int(np.asarray(inputs['w_in']).shape[0])
    ncores = int(np.asarray(inputs['x_prompt']).shape[0])
    key = (TP, DEPTH)
    if key not in _NC_CACHE:
        _NC_CACHE[key] = build(TP=TP, TT=min(512, TP), DEPTH=DEPTH)
    nc = _NC_CACHE[key]
    maps = make_in_maps(inputs, TP, DEPTH, ncores)
    res = run_bass_kernel_spmd(nc, maps, core_ids=list(range(ncores)))
    return gather(res.results, TP, DEPTH, ncores)
```

```python
import numpy as np
import concourse.bass as bass
import concourse.mybir as mybir
from concourse.bass_utils import run_bass_kernel_spmd
from contextlib import ExitStack

F32 = mybir.dt.float32
BF16 = mybir.dt.bfloat16
AF = mybir.ActivationFunctionType
ALU = mybir.AluOpType
AX = mybir.AxisListType

D = 1024
KC = 8
DFF = 2816
FC = 22
IN_W = 3096
C_GQ, C_GK, C_GV, C_GR, C_GA = 0, 128, 256, 512, 768
C_DQ, C_DK, C_DV, C_DZ, C_DA, C_DB = 784, 1040, 1296, 1552, 1808, 1812
C_CU, C_CV, C_SH, C_SB, C_SC = 1816, 2072, 2328, 2584, 2840
EPS = 1e-6
BIG = 30000.0
NSLOT = 4
SLOT_EL = 4096
KD = 8


def make_consts():
    c = {}
    idx = np.arange(128)
    c['ident'] = np.eye(128, dtype=np.float32)
    c['ones'] = np.ones((128, 128), np.float32)
    c['blk64'] = (idx[:, None] // 64 == idx[None, :] // 64).astype(np.float32)
    for mode, L in (('P', 128), ('S', 8)):
        same = (idx[:, None] // L == idx[None, :] // L)
        le = idx[:, None] <= idx[None, :]
        c['triU' + mode] = (same & le).astype(np.float32)
        c['tot' + mode] = same.astype(np.float32)
        c['bigL' + mode] = np.where(same & (idx[:, None] >= idx[None, :]), 0.0, BIG).astype(np.float32)
        c['bigU' + mode] = np.where(same & le, 0.0, BIG).astype(np.float32)
        c['strictL' + mode] = (same & (idx[:, None] > idx[None, :])).astype(np.float32)
    c['headmask'] = np.zeros((128, 128), np.float32)
    c['headmask'][:, 0:4] = (idx[:, None] // 32 == np.arange(4)[None, :])
    c['seqmask'] = np.zeros((128, 128), np.float32)
    c['seqmask'][:, 0:16] = (idx[:, None] // 8 == np.arange(16)[None, :])
    hf = np.zeros((128, 4, 128), np.float32)
    for h in range(4):
        hf[:, h, 32 * h:32 * h + 32] = 1.0
    c['hmfree'] = hf.reshape(128, 512)
    e8 = np.zeros((128, 128), np.float32)
    e8[0:8, :] = (np.arange(8)[:, None] == idx[None, :] % 8)
    c['e8'] = e8
    names = ['ident', 'ones', 'blk64', 'triUP', 'totP', 'bigLP', 'bigUP', 'strictLP',
             'triUS', 'totS', 'bigLS', 'bigUS', 'strictLS', 'headmask', 'seqmask', 'e8', 'hmfree']
    offs = {}
    o = 0
    for n in names:
        offs[n] = (o, c[n].shape[1])
        o += c[n].shape[1]
    arr = np.concatenate([c[n] for n in names], axis=1).astype(np.float32)
    return arr, offs


CONST_ARR, CONST_OFF = make_consts()


class StopBuild(Exception):
    pass


class Prog:
    ENGS = ('pe', 'act', 'dve', 'pool', 'sp')
    max_ops = None
    log = None
    names = None

    def __init__(self, nc, es):
        self.nc = nc
        self.q = {e: [] for e in self.ENGS}
        self.sem = {e: es.enter_context(nc.semaphore('s_' + e)) for e in ('pe', 'act', 'dve')}
        self.cnt = {e: 0 for e in ('pe', 'act', 'dve')}
        self.dsem = {e: [es.enter_context(nc.semaphore('d_%s%d' % (e, i))) for i in range(KD)]
                     for e in ('pool', 'sp')}
        self.dcnt = {'pool': 0, 'sp': 0}
        self.waited = {}
        self.res = {}
        self.planning = False
        self.semname = {}
        for e in self.sem:
            self.semname[id(self.sem[e])] = e
        for e in self.dsem:
            for i, s in enumerate(self.dsem[e]):
                self.semname[id(s)] = '%s%d' % (e, i)
        self.out_events = []

    def _wait(self, eng, sem, val):
        key = (eng, id(sem))
        if self.waited.get(key, 0) >= val:
            return
        self.waited[key] = val
        self.q[eng].append(lambda e, s=sem, v=val: e.wait_ge(s, v))

    SPLIT = ('ct5', 'ct7', 'ct8', 'ct9', 'ct10')

    def _expand(self, keys):
        out = []
        for k in keys:
            if k in self.SPLIT:
                out.append((k, 0))
                out.append((k, 1))
            elif isinstance(k, tuple) and len(k) == 2 and k[0] == 'xp':
                out.extend([('xp', k[1], q) for q in range(4)])
            else:
                out.append(k)
        return out

    def op(self, eng, fn, reads=(), writes=(), is_out=False, strict=False):
        if self.planning:
            return
        reads = self._expand(reads)
        writes = self._expand(writes)
        self.nops = getattr(self, 'nops', 0) + 1
        if self.max_ops is not None and self.nops > self.max_ops:
            raise StopBuild()
        if self.log is not None:
            import sys as _s
            fr = _s._getframe(2)
            self.log.append((self.nops, eng, fr.f_code.co_name, fr.f_lineno, _s._getframe(3).f_code.co_name, _s._getframe(3).f_lineno))
        deps = []
        for k in reads:
            r = self.res.get(k)
            if r and r['w']:
                deps.append(r['w'] + ('raw',))
        for k in writes:
            r = self.res.get(k)
            if r:
                if r['w']:
                    deps.append(r['w'] + ('waw',))
                for (sid, (sem, val, src)) in r['r'].items():
                    deps.append((sem, val, src, 'war'))
        for (sem, val, src, kind) in deps:
            if src == eng:
                if eng == 'pe':
                    continue
                if eng in ('act', 'dve') and kind != 'raw' and not strict:
                    continue
            self._wait(eng, sem, val)
        if eng in ('sp', 'pool'):
            i = self.dcnt[eng]
            self.dcnt[eng] += 1
            sem = self.dsem[eng][i % KD]
            val = 16 * (i // KD + 1)
            if i >= KD:
                self._wait(eng, sem, val - 16)
            self.q[eng].append(lambda e, f=fn, s=sem, n_=self.nops: self._name(n_, f(e).then_inc(s, 16)))
        else:
            self.cnt[eng] += 1
            sem = self.sem[eng]
            val = self.cnt[eng]
            self.q[eng].append(lambda e, f=fn, s=sem, n_=self.nops: self._name(n_, f(e).then_inc(s, 1)))
        ev = (sem, val, eng)
        for k in reads:
            r = self.res.setdefault(k, {'w': None, 'r': {}})
            old = r['r'].get(id(sem))
            if old is None or old[1] < val:
                r['r'][id(sem)] = ev
        for k in writes:
            self.res[k] = {'w': ev, 'r': {}}
        if is_out:
            self.out_events.append(ev)

    def _name(self, n_, ins):
        if self.names is not None:
            try:
                self.names[ins.ins.name] = n_
            except Exception:
                pass
        return ins

    def war_guard(self, eng, keys):
        if self.planning:
            return
        for k in keys:
            r = self.res.get(k)
            if r:
                for (sem, val, src) in r['r'].values():
                    self._wait(eng, sem, val)
                if r['w'] and not r['r']:
                    self._wait(eng, r['w'][0], r['w'][1])
                self.res[k] = {'w': None, 'r': {}}

    def barrier(self):
        if self.planning:
            return
        for e in ('pe', 'act', 'dve'):
            for s in ('pe', 'act', 'dve'):
                if s != e and self.cnt[s] > 0:
                    self._wait(e, self.sem[s], self.cnt[s])

    def finish(self):
        last = {}
        for (sem, val, src) in self.out_events:
            if last.get(id(sem), (None, 0))[1] < val:
                last[id(sem)] = (sem, val)
        for (sem, val) in last.values():
            self._wait('sp', sem, val)

    def emit(self, block):
        nc = self.nc
        q = self.q

        @block.tensor
        def _(e):
            for f in q['pe']:
                f(e)

        @block.scalar
        def _(e):
            for f in q['act']:
                f(e)

        @block.vector
        def _(e):
            for f in q['dve']:
                f(e)

        @block.gpsimd
        def _(e):
            for f in q['pool']:
                f(e)

        @block.sync
        def _(e):
            for f in q['sp']:
                f(e)


def build(TP=2048, TT=512, DEPTH=4, do_sample=True, max_ops=None, log=None, names=None):
    nc = bass.Bass("TRN2", target_bir_lowering=False)
    NT = TP // TT
    L = DEPTH

    def din(name, shape):
        return nc.dram_tensor(name, list(shape), F32, kind="ExternalInput").ap()

    def dout(name, shape):
        return nc.dram_tensor(name, list(shape), F32, kind="ExternalOutput").ap()

    x_p = din("x_p", [TP, D])
    x_s = din("x_s", [128, D])
    st_gla = din("st_gla", [L, 16, 128, 64])
    st_gdn = din("st_gdn", [L, 16, 4, 64, 64])
    st_gc = din("st_gc", [L, 48, 768])
    st_sc = din("st_sc", [L, 32, 256])
    p_p = din("p_p", [L, TP, 256])
    p_s = din("p_s", [L, 128, 256])
    consts_d = din("consts", list(CONST_ARR.shape))
    W = {}
    wshapes = dict(norm_mix=[L, D], w_in=[L, D, IN_W], gla_wa2=[L, 16, 128], gla_ba=[L, 128], gla_norm=[L, 64],
                   gdn_conv_w=[L, 4, 768], gdn_a_log=[L, 4], gdn_dt_bias=[L, 4], gdn_norm=[L, 64],
                   cm_ln_g=[L, 256], cm_ln_b=[L, 256], cm_ws=[L, 4, 128, 128], cm_bs=[L, 4, 128],
                   sc_conv_w=[L, 3, 256], w_gate=[L, D, 4 * D], w_branch=[L, 4, 256, D], w_o=[L, D, D],
                   norm_ffn=[L, D], w_ffn_gate=[L, D, DFF], w_ffn_up=[L, D, DFF], w_ffn_down=[L, DFF, D],
                   norm_ple=[L, D], w_ple_gate=[L, D, D], w_ple=[L, 256, D], norm_final=[D])
    for k, s in wshapes.items():
        W[k] = din(k, s)
    y_p = dout("y_p", [TP, D])
    y_s = dout("y_s", [128, D])
    o_gla_p = dout("o_gla_p", [L, 128, 64])
    o_gla_s = dout("o_gla_s", [L, 16, 128, 64])
    o_gdn_p = dout("o_gdn_p", [L, 4, 64, 64])
    o_gdn_s = dout("o_gdn_s", [L, 16, 4, 64, 64])
    o_gc_p = dout("o_gc_p", [L, 3, 768])
    o_gc_s = dout("o_gc_s", [L, 48, 768])
    o_sc_p = dout("o_sc_p", [L, 2, 256])
    o_sc_s = dout("o_sc_s", [L, 32, 256])
    o_cv_s = dout("o_cv_s", [L, 128, 256])

    es = ExitStack()
    with es:
        P = Prog(nc, es)
        P.max_ops = max_ops
        P.log = log
        P.names = names

        def sb(name, shape, dt=F32):
            return es.enter_context(nc.sbuf_tensor(name, list(shape), dt))

        def psum(name):
            return es.enter_context(nc.psum_tensor(name, [128, 512], F32))

        NTK = max(TT, 512)
        cst = sb("cst", list(CONST_ARR.shape))
        onesb = sb("onesb", [128, 128], BF16)
        blk64b = sb("blk64b", [128, 128], BF16)
        cvec = sb("cvec", [128, 128])
        cvec2 = sb("cvec2", [128, 128])
        cvec3 = sb("cvec3", [128, 16])
        vstage = sb("vstage", [128, 128])
        alog_b = sb("alog_b", [128, L * 4])
        dtb_b = sb("dtb_b", [128, L * 4])
        nega_b = sb("nega_b", [128, L * 4])
        wa2_sb = sb("wa2_sb", [17, L * 128])
        hT = sb("hT", [128, KC, NTK])
        xn = sb("xn", [128, KC, NTK], BF16)
        ring = sb("ring", [128, NSLOT, SLOT_EL], BF16)
        rstd = sb("rstd", [128, NTK])
        lnt = sb("lnt", [128, NTK])
        Sgla_p = sb("Sgla_p", [128, L, 64])
        Sgdn_p = sb("Sgdn_p", [128, L, 2, 64])
        halo_gc = sb("halo_gc", [128, L, 6, 3])
        halo_sc = sb("halo_sc", [128, L, 2, 2])
        Sgla_s = sb("Sgla_s", [128, 16, 64])
        Sgdn_s = sb("Sgdn_s", [128, 16, 2, 64])
        sqs = sb("sqs", [128, KC, NTK], BF16)
        qT = sb("qT", [128, NTK])
        kT = sb("kT", [128, NTK])
        grs = sb("grs", [128, 2, NTK], BF16)
        gaT = sb("gaT", [17, NTK])
        xp = sb("xp", [128, 6, max(NTK + 3, 176)])
        qkv = sb("qkv", [128, 6, NTK])
        dzs = sb("dzs", [128, 2, NTK], BF16)
        cug = sb("cug", [128, 2, NTK], BF16)
        shp = sb("shp", [128, 2, max(NTK + 2, 160)])
        sbb = sb("sbb", [128, 2, NTK])
        gv_tm = sb("gv_tm", [128, NTK // 128, 256])
        vn_tm = sb("vn_tm", [128, NTK // 128, 256])
        ab_tm = sb("ab_tm", [128, NTK // 128, 8])
        brT = sb("brT", [128, 4, 2, NTK], BF16)
        tmpAB = sb("tmpAB", [128, 1024])
        tmpA = tmpAB[:, 0:512]
        tmpB = tmpAB[:, 512:1024]
        tmpC = sb("tmpC", [128, NTK])
        peT = sb("peT", [128, 2, NTK], BF16)
        pe_tm = sb("pe_tm", [128, 256])
        x_tm = tmpAB
        lng_b = sb("lng_b", [128, 256])
        lnb_b = sb("lnb_b", [128, 256])
        bsT = sb("bsT", [128, 2, 128])
        wmT = sb("wmT", [128, 4, 128])
        ws8 = sb("ws8", [8, 4, 8])
        ctall = sb("ctall", [128, 12 * 512])
        ct = [ctall[:, 512 * i:512 * i + 512] for i in range(12)]
        hid = ctall[:, 0:FC * 256].bitcast(BF16).rearrange("p (f t) -> p f t", f=FC)
        ws_tm = ct[10].rearrange("p (g s) -> p g s", g=4)
        t18 = ct[11][0:8, :].rearrange("p (g s) -> p g s", g=4)
        st_tm = ctall[0:48, 8 * 512:8 * 512 + 768]
        cs = [sb("cs%d" % i, [128, 256]) for i in range(8)]
        sm = [sb("sm%d" % i, [128, 16]) for i in range(10)]
        um = sqs[:, :, :].rearrange("p k t -> p (k t)").bitcast(F32).rearrange("p (s v) -> p s v", s=8)
        UMK = [('sqs', kc) for kc in range(KC)]
        XPK = [('xp', j_) for j_ in range(6)]

        ps = [psum("ps%d" % i) for i in range(8)]

        def C(name):
            o, w = CONST_OFF[name]
            return cst[:, o:o + w]

        def mm(out, lhsT, rhs, start, stop, reads, writes):
            P.op('pe', lambda e: e.matmul(out, lhsT=lhsT, rhs=rhs, start=start, stop=stop), reads, writes)

        def tr(out, in_, n_in_part, reads, writes):
            idn = C('ident')[0:n_in_part, 0:n_in_part]
            P.op('pe', lambda e: e.transpose(out, in_, idn), list(reads) + ['cst'], writes)

        def act(out, in_, func, reads, writes, bias=0.0, scale=1.0, accum_out=None):
            if accum_out is None:
                P.op('act', lambda e: e.activation(out=out, in_=in_, func=func, bias=bias, scale=scale), reads, writes)
            else:
                P.op('act', lambda e: e.activation(out=out, in_=in_, func=func, bias=bias, scale=scale,
                                                   accum_out=accum_out), reads, writes)

        def tt(out, in0, in1, op, reads, writes, eng='dve'):
            P.op(eng, lambda e: e.tensor_tensor(out=out, in0=in0, in1=in1, op=op), reads, writes)

        def ts(out, in0, s1, s2, op0, op1, reads, writes, accum_out=None):
            if op1 is None:
                P.op('dve', lambda e: e.tensor_scalar(out=out, in0=in0, scalar1=s1, scalar2=None, op0=op0),
                     reads, writes)
            elif accum_out is None:
                P.op('dve', lambda e: e.tensor_scalar(out=out, in0=in0, scalar1=s1, scalar2=s2, op0=op0, op1=op1),
                     reads, writes)
            else:
                P.op('dve', lambda e: e.tensor_scalar(out=out, in0=in0, scalar1=s1, scalar2=s2, op0=op0, op1=op1,
                                                      accum_out=accum_out), reads, writes)

        def stt(out, in0, scalar, in1, op0, op1, reads, writes):
            P.op('dve', lambda e: e.scalar_tensor_tensor(out=out, in0=in0, scalar=scalar, in1=in1, op0=op0, op1=op1),
                 reads, writes)

        def cp(out, in_, reads, writes, eng='dve'):
            if eng == 'act':
                P.op('act', lambda e: e.copy(out=out, in_=in_), reads, writes)
            else:
                P.op('dve', lambda e: e.tensor_copy(out=out, in_=in_), reads, writes)

        def dma(out, in_, reads, writes, eng='sp', is_out=False):
            P.op(eng, lambda e: e.dma_start(out=out, in_=in_), reads, writes, is_out=is_out)

        def memset(ap, val, writes):
            P.op('dve', lambda e: e.memset(ap, val), [], writes)

        def recip(out, in_, reads, writes):
            P.op('dve', lambda e: e.reciprocal(out=out, in_=in_), reads, writes)

        plan = []
        ring_state = {'next_use': 0, 'next_load': 0}

        def issue_load(k):
            (wname, l, r0, nrows, c0, ncols) = plan[k]
            slot = k % NSLOT
            kc = nrows // 128
            src = W[wname][l] if l is not None else W[wname]
            P.war_guard('pool', [('ring', slot, p_) for p_ in range(8)])
            for a in range(0, kc, 4):
                b = min(kc, a + 4)
                dst = ring[:, slot, a * ncols:b * ncols].rearrange("p (k c) -> p k c", k=b - a)
                s_ap = src[r0 + a * 128:r0 + b * 128, c0:c0 + ncols].rearrange("(k p) c -> p k c", p=128)
                dma(dst, s_ap, [], [('ring', slot, a // 4)], eng='pool')

        def slab(wname, l, r0, nrows, c0, ncols):
            spec = (wname, l, r0, nrows, c0, ncols)
            kc = nrows // 128
            assert kc * ncols <= SLOT_EL, spec
            if P.planning:
                plan.append(spec)
                k = len(plan) - 1
            else:
                k = ring_state['next_use']
                assert plan[k] == spec, (plan[k], spec)
                ring_state['next_use'] += 1
                while ring_state['next_load'] < min(len(plan), k + NSLOT - 1):
                    issue_load(ring_state['next_load'])
                    ring_state['next_load'] += 1
            slot = k % NSLOT
            v = ring[:, slot, 0:kc * ncols].rearrange("p (k c) -> p k c", k=kc)
            return v, ('ring', slot)

        def rk(wk, kc):
            return (wk[0], wk[1], kc // 4)

        def setup():
            dma(cst[:], consts_d[:, :], [], ['cst'])
            cp(onesb[:], C('ones'), ['cst'], ['onesb'])
            cp(blk64b[:], C('blk64'), ['cst'], ['blk64b'])
            memset(vstage[:], 0.0, ['vstage'])
            for i, nm in enumerate(('norm_mix', 'norm_ffn', 'norm_ple')):
                dma(vstage[i * 32:i * 32 + L * 8, :], W[nm].rearrange("l (k p) -> (l k) p", p=128), [], ['vstage'])
            dma(vstage[96:104, :], W['norm_final'].rearrange("(k p) -> k p", p=128), [], ['vstage'])
            tr(ps[0][:, 0:128], vstage[:], 128, ['vstage'], ['ps0'])
            cp(cvec[:], ps[0][:, 0:128], ['ps0'], ['cvec'])
            memset(vstage[:], 0.0, ['vstage'])
            dma(vstage[0:L * 24, :], W['gdn_conv_w'].rearrange("l j (k p) -> (l j k) p", p=128), [], ['vstage'])
            dma(vstage[96:96 + L * 6, :], W['sc_conv_w'].rearrange("l j (k p) -> (l j k) p", p=128), [], ['vstage'])
            tr(ps[0][:, 0:128], vstage[:], 128, ['vstage'], ['ps0'])
            cp(cvec2[:], ps[0][:, 0:128], ['ps0'], ['cvec2'])
            memset(vstage[:], 0.0, ['vstage'])
            for half in range(2):
                dma(vstage[0:L, 64 * half:64 * half + 64], W['gla_norm'][:, :], [], ['vstage'])
                dma(vstage[8:8 + L, 64 * half:64 * half + 64], W['gdn_norm'][:, :], [], ['vstage'])
            tr(ps[0][:, 0:128], vstage[:], 128, ['vstage'], ['ps0'])
            cp(cvec3[:], ps[0][:, 0:16], ['ps0'], ['cvec3'])
            dma(wa2_sb[16:17, :], W['gla_ba'].rearrange("(o l) c -> o (l c)", o=1), [], ['wa2b'])
            dma(wa2_sb[0:16, :].rearrange("p (l c) -> p l c", l=L), W['gla_wa2'].rearrange("l r c -> r l c"), [], ['wa2'])
            o1, _w = CONST_OFF['ones']
            for q_ in range(0, NTK, 128):
                dma(gaT[16:17, q_:q_ + 128], consts_d[0:1, o1:o1 + 128], [], ['gaT1'])
            dma(alog_b[:], W['gdn_a_log'].rearrange("(o l) h -> o (l h)", o=1).to_broadcast([128, L * 4]), [], ['alog'])
            dma(dtb_b[:], W['gdn_dt_bias'].rearrange("(o l) h -> o (l h)", o=1).to_broadcast([128, L * 4]), [], ['dtb'])
            act(nega_b[:], alog_b[:], AF.Exp, ['alog'], ['nega'])
            ts(nega_b[:], nega_b[:], -1.0, None, ALU.mult, None, ['nega'], ['nega'])
            memset(Sgla_p[:], 0.0, ['Sgla_p'])
            memset(Sgdn_p[:], 0.0, ['Sgdn_p'])
            memset(halo_gc[:], 0.0, ['halo_gc'])
            memset(halo_sc[:], 0.0, ['halo_sc'])

        def gcol(which, l, kc):
            base = {'norm_mix': 0, 'norm_ffn': 32, 'norm_ple': 64}[which]
            return cvec[:, base + l * 8 + kc:base + l * 8 + kc + 1]

        def sq_h(kc, ntok, which):
            buf, key = (sqs, 'sqs') if which == 'sqs' else (xn, 'xn')
            act(buf[:, kc, 0:ntok], hT[:, kc, 0:ntok], AF.Square, [('h', kc)], [(key, kc)])

        def rmsnorm_to_xn(ntok, gsel, which='sqs'):
            buf, key = (sqs, 'sqs') if which == 'sqs' else (xn, 'xn')
            for kc in range(KC):
                mm(ps[0][:, 0:ntok], onesb[:], buf[:, kc, 0:ntok], kc == 0, kc == KC - 1,
                   ['onesb', (key, kc)], ['ps0'])
            act(lnt[:, 0:ntok], ps[0][:, 0:ntok], AF.Ln, ['ps0'], ['lnt'], bias=eps_col[:, 0:1], scale=1.0 / D)
            act(rstd[:, 0:ntok], lnt[:, 0:ntok], AF.Exp, ['lnt'], ['rstd'], scale=-0.5)
            for kc in range(KC):
                stt(xn[:, kc, 0:ntok], hT[:, kc, 0:ntok], gsel(kc), rstd[:, 0:ntok], ALU.mult, ALU.mult,
                    [('h', kc), 'rstd', 'cvec'], [('xn', kc)])

        eps_col = sb("eps_col", [128, 4])

        def proj_fm(psb, pskey, wv, wkey, c0, ncols, src, srckey, nkc, ntok, prow=0):
            for kc in range(nkc):
                mm(psb[prow:prow + ncols, 0:ntok], wv[:, kc, c0:c0 + ncols], src[:, kc, 0:ntok], kc == 0, kc == nkc - 1,
                   [rk(wkey, kc), (srckey, kc)], [pskey])

        def layer(l, tile):
            mode = tile['mode']
            ntok = tile['ntok']
            nseq = tile['nseq']
            Ls = ntok // nseq
            nsub = ntok // 128
            xn_r = [('xn', kc) for kc in range(KC)]

            def v3(ap2d):
                return ap2d.rearrange("p (s t) -> p s t", s=nseq)

            dma(lng_b[:], W['cm_ln_g'][l:l + 1, :].to_broadcast([128, 256]), [], ['lng'])
            dma(lnb_b[:], W['cm_ln_b'][l:l + 1, :].to_broadcast([128, 256]), [], ['lnb'])
            if mode == 'P':
                for g in range(4):
                    h2 = g % 2
                    dma(bsT[64 * h2:64 * h2 + 64, g // 2, :], W['cm_bs'][l, g:g + 1, :].to_broadcast([64, 128]), [], ['bsT'])
                dma(ws_tm, W['cm_ws'][l].rearrange("g t s -> t g s"), [], ['ct10'])
                for g in range(4):
                    tr(ps[4][:, 128 * g:128 * g + 128], ws_tm[:, g, :], 128, ['ct10'], ['ps4'])
                tt(wmT[:],
                   ps[4][:, :].rearrange("p (g t) -> p g t", g=4),
                   C('triUP').unsqueeze(1).to_broadcast([128, 4, 128]), ALU.mult, ['ps4', 'cst'], ['wmT'])
            else:
                for g in range(4):
                    h2 = g % 2
                    dma(bsT[64 * h2:64 * h2 + 64, g // 2, :].rearrange("p (s t) -> p s t", s=16),
                        W['cm_bs'][l, g:g + 1, 0:8].unsqueeze(1).to_broadcast([64, 16, 8]), [], ['bsT'])
                dma(ws8[:], W['cm_ws'][l, :, 0:8, 0:8].rearrange("g t s -> t g s"), [], ['ws8'])
                for g in range(4):
                    mm(ps[4][0:8, 128 * g:128 * g + 128], ws8[:, g, :], C('e8')[0:8, :], True, True, ['ws8', 'cst'], ['ps4'])
                cp(t18[:].rearrange("p g t -> p (g t)"), ps[4][0:8, :], ['ps4'], ['ct11'])
                for g in range(4):
                    mm(ps[5][:, 128 * g:128 * g + 128], C('e8')[0:8, :], t18[:, g, :], True, True, ['ct11', 'cst'], ['ps5'])
                tt(wmT[:], ps[5][:, :].rearrange("p (g t) -> p g t", g=4),
                   C('triUS').unsqueeze(1).to_broadcast([128, 4, 128]), ALU.mult, ['ps5', 'cst'], ['wmT'])
                dma(Sgla_s[:], st_gla[l].rearrange("s p v -> p s v"), [], ['Sgla_s'])
                for pair in range(2):
                    for h2 in range(2):
                        dma(Sgdn_s[64 * h2:64 * h2 + 64, :, pair, :], st_gdn[l, :, 2 * pair + h2].rearrange("s k v -> k s v"),
                            [], ['Sgdn_s'])
                dma(st_tm[:], st_gc[l], [], ['ct8', 'ct9'])
                for j in range(6):
                    tr(ps[4][:, 48 * j:48 * j + 48], st_tm[0:48, 128 * j:128 * j + 128], 48, ['ct8', 'ct9'], ['ps4'])
                cp(xp[:, :, 0:16 * 11].rearrange("p j (s t) -> p j s t", s=16)[:, :, :, 0:3],
                   ps[4][:, 0:288].rearrange("p (j s t) -> p j s t", j=6, s=16), ['ps4'], XPK)
                dma(st_tm[0:32, 0:256], st_sc[l], [], ['ct8', 'ct9'])
                for j in range(2):
                    tr(ps[4][:, 32 * j:32 * j + 32], st_tm[0:32, 128 * j:128 * j + 128], 32, ['ct8', 'ct9'], ['ps4'])
                cp(shp[:, :, 0:16 * 10].rearrange("p j (s t) -> p j s t", s=16)[:, :, :, 0:2],
                   ps[4][:, 0:64].rearrange("p (j s t) -> p j s t", j=2, s=16), ['ps4'], ['shp'])
            if mode == 'P':
                cp(xp[:, :, 0:3], halo_gc[:, l, :, :], ['halo_gc'], XPK)
                cp(shp[:, :, 0:2], halo_sc[:, l, :, :], ['halo_sc'], ['shp'])
            pe_src = (p_p[l, tile['t0']:tile['t0'] + ntok, :] if mode == 'P' else p_s[l])
            for s_ in range(nsub):
                dma(pe_tm[:], pe_src[128 * s_:128 * s_ + 128, :], [], ['pe_tm'])
                for j in range(2):
                    tr(ps[4][:, 128 * j:128 * j + 128], pe_tm[:, 128 * j:128 * j + 128], 128, ['pe_tm'], ['ps4'])
                cp(peT[:, :, 128 * s_:128 * s_ + 128], ps[4][:, 0:256].rearrange("p (j t) -> p j t", j=2), ['ps4'], ['peT'])

            xpv = xp[:, :, 0:nseq * (Ls + 3)].rearrange("p j (s t) -> p j s t", s=nseq)
            shv = shp[:, :, 0:nseq * (Ls + 2)].rearrange("p j (s t) -> p j s t", s=nseq)

            rmsnorm_to_xn(ntok, lambda kc: gcol('norm_mix', l, kc))

            wv, wk = slab('w_in', l, 0, D, 0, 512)
            proj_fm(ps[0], 'ps0', wv, wk, 0, 128, xn, 'xn', KC, ntok)
            cp(qT[:, 0:ntok], ps[0][:, 0:ntok], ['ps0'], ['qT'], eng='act')
            proj_fm(ps[1], 'ps1', wv, wk, 128, 128, xn, 'xn', KC, ntok)
            cp(kT[:, 0:ntok], ps[1][:, 0:ntok], ['ps1'], ['kT'])
            for s_ in range(nsub):
                pb = ps[2 + (s_ % 2)]
                pk = 'ps%d' % (2 + (s_ % 2))
                for kc in range(KC):
                    mm(pb[:, 0:256], xn[:, kc, 128 * s_:128 * s_ + 128], wv[:, kc, 256:512], kc == 0, kc == KC - 1,
                       [rk(wk, kc), ('xn', kc)], [pk])
                cp(gv_tm[:, s_, :], pb[:, 0:256], [pk], ['gv_tm'], eng='act')
            wv, wk = slab('w_in', l, 0, D, 512, 272)
            for j in range(2):
                pb, pk = ps[j], 'ps%d' % j
                proj_fm(pb, pk, wv, wk, 128 * j, 128, xn, 'xn', KC, ntok)
                act(grs[:, j, 0:ntok], pb[:, 0:ntok], AF.Silu, [pk], ['grs'])
            proj_fm(ps[2], 'ps2', wv, wk, 256, 16, xn, 'xn', KC, ntok)
            cp(gaT[0:16, 0:ntok], ps[2][0:16, 0:ntok], ['ps2'], ['gaT'])
            wv, wk = slab('w_in', l, 0, D, C_DQ, 512)
            for j in range(4):
                pb, pk = ps[j % 4], 'ps%d' % (j % 4)
                proj_fm(pb, pk, wv, wk, 128 * j, 128, xn, 'xn', KC, ntok)
                cp(xpv[:, j, :, 3:3 + Ls], v3(pb[:, 0:ntok]), [pk], [('xp', j)], eng=('act' if j % 2 else 'dve'))
            wv, wk = slab('w_in', l, 0, D, C_DV, 512)
            for j in range(4):
                pb, pk = ps[j % 4], 'ps%d' % (j % 4)
                proj_fm(pb, pk, wv, wk, 128 * j, 128, xn, 'xn', KC, ntok)
                if j < 2:
                    cp(xpv[:, 4 + j, :, 3:3 + Ls], v3(pb[:, 0:ntok]), [pk], [('xp', 4 + j)], eng=('act' if j % 2 else 'dve'))
                else:
                    act(dzs[:, j - 2, 0:ntok], pb[:, 0:ntok], AF.Silu, [pk], ['dzs'])
            wv, wk = slab('w_in', l, 0, D, C_DA, 8)
            for s_ in range(nsub):
                for kc in range(KC):
                    mm(ps[4][:, 8 * s_:8 * s_ + 8], xn[:, kc, 128 * s_:128 * s_ + 128], wv[:, kc, 0:8], kc == 0, kc == KC - 1,
                       [rk(wk, kc), ('xn', kc)], ['ps4'])
            cp(ab_tm[:, 0:nsub, :], ps[4][:, 0:8 * nsub].rearrange("p (s c) -> p s c", c=8), ['ps4'], ['ab_tm'])
            wv, wk = slab('w_in', l, 0, D, C_CU, 512)
            for j in range(2):
                pb, pk = ps[j], 'ps%d' % j
                proj_fm(pb, pk, wv, wk, 128 * j, 128, xn, 'xn', KC, ntok)
                gelu(cug[:, j, 0:ntok], 'cug', pb[:, 0:ntok], pk, ntok)
            for s_ in range(nsub):
                pb, pk = ps[2 + (s_ % 2)], 'ps%d' % (2 + (s_ % 2))
                for kc in range(KC):
                    mm(pb[:, 0:256], xn[:, kc, 128 * s_:128 * s_ + 128], wv[:, kc, 256:512], kc == 0, kc == KC - 1,
                       [rk(wk, kc), ('xn', kc)], [pk])
                gelu(cs[s_][:, :], 'cs%d' % s_, pb[:, 0:256], pk, 256)
            for s_ in range(nsub):
                gk_ = 'cs%d' % s_
                P.op('dve', lambda e, b=cs[s_]: e.reduce_sum(out=sm[0][:, 0:1], in_=b[:, :], axis=AX.X), [gk_], ['sm0'])
                ts(sm[0][:, 1:2], sm[0][:, 0:1], -1.0 / 256, None, ALU.mult, None, ['sm0'], ['sm0b'])
                ts(cs[4][:, :], cs[s_][:, :], sm[0][:, 1:2], None, ALU.add, None, [gk_, 'sm0b'], ['cs4'])
                tt(cs[5][:, :], cs[4][:, :], cs[4][:, :], ALU.mult, ['cs4'], ['cs5'])
                P.op('dve', lambda e: e.reduce_sum(out=sm[0][:, 2:3], in_=cs[5][:, :], axis=AX.X), ['cs5'], ['sm0c'])
                act(sm[0][:, 3:4], sm[0][:, 2:3], AF.Ln, ['sm0c'], ['sm0d'], bias=eps_col[:, 0:1], scale=1.0 / 256)
                act(sm[0][:, 4:5], sm[0][:, 3:4], AF.Exp, ['sm0d'], ['sm0e'], scale=-0.5)
                stt(cs[5][:, :], cs[4][:, :], sm[0][:, 4:5], lng_b[:, :], ALU.mult, ALU.mult, ['cs4', 'sm0e', 'lng'], ['cs5'])
                tt(vn_tm[:, s_, :], cs[5][:, :], lnb_b[:, :], ALU.add, ['cs5', 'lnb'], ['vn_tm'])
            if mode == 'S':
                dma(o_cv_s[l], vn_tm[:, 0, :], ['vn_tm'], [], is_out=True)
            wv, wk = slab('w_in', l, 0, D, C_SH, 512)
            for j in range(2):
                pb, pk = ps[j], 'ps%d' % j
                proj_fm(pb, pk, wv, wk, 128 * j, 128, xn, 'xn', KC, ntok)
                cp(tmpA[:, 0:ntok] if j == 0 else tmpB[:, 0:ntok], pb[:, 0:ntok], [pk], ['tmpA' if j == 0 else 'tmpB'],
                   eng='act')
            for j in range(2):
                pb, pk = ps[2 + j], 'ps%d' % (2 + j)
                proj_fm(pb, pk, wv, wk, 256 + 128 * j, 128, xn, 'xn', KC, ntok)
                cp(sbb[:, j, 0:ntok], pb[:, 0:ntok], [pk], ['sbb'], eng='act')
            wv, wk = slab('w_in', l, 0, D, C_SC, 256)
            for j in range(2):
                pb, pk = ps[j], 'ps%d' % j
                proj_fm(pb, pk, wv, wk, 128 * j, 128, xn, 'xn', KC, ntok)
                shsrc = tmpA if j == 0 else tmpB
                tt(shv[:, j, :, 2:2 + Ls], v3(pb[:, 0:ntok]), v3(shsrc[:, 0:ntok]), ALU.mult,
                   [pk, 'tmpA' if j == 0 else 'tmpB'], ['shp'])

            for j in range(2):
                yv = v3(tmpA[:, 0:ntok]) if j == 0 else v3(tmpB[:, 0:ntok])
                yk = 'tmpA' if j == 0 else 'tmpB'
                wcol = lambda jj: cvec2[:, 96 + l * 6 + jj * 2 + j:96 + l * 6 + jj * 2 + j + 1]
                ts(yv, shv[:, j, :, 0:Ls], wcol(0), None, ALU.mult, None, ['shp', 'cvec2'], [yk])
                for jj in (1, 2):
                    stt(yv, shv[:, j, :, jj:jj + Ls], wcol(jj), yv, ALU.mult, ALU.add, ['shp', 'cvec2', yk], [yk])
                tt(brT[:, 3, j, 0:ntok], sbb[:, j, 0:ntok], (tmpA if j == 0 else tmpB)[:, 0:ntok], ALU.mult,
                   ['sbb', yk], [('brT', 3)])
            sc_state_out(l, tile, shv, Ls)
            for j in range(6):
                wcol = lambda jj: cvec2[:, l * 24 + jj * 6 + j:l * 24 + jj * 6 + j + 1]
                cb, cbk = ((tmpC, 'tmpC'), (tmpA, 'tmpA'), (tmpB, 'tmpB'))[j % 3]
                yv = v3(cb[:, 0:ntok])
                ts(yv, xpv[:, j, :, 0:Ls], wcol(0), None, ALU.mult, None, [('xp', j), 'cvec2'], [cbk])
                for jj in (1, 2, 3):
                    stt(yv, xpv[:, j, :, jj:jj + Ls], wcol(jj), yv, ALU.mult, ALU.add, [('xp', j), 'cvec2', cbk], [cbk])
                act(qkv[:, j, 0:ntok], cb[:, 0:ntok], AF.Silu, [cbk], [('qkv', j)])
            gc_state_out(l, tile, xpv, Ls)
            for j in range(4):
                tt(sqs[:, j, 0:ntok], qkv[:, j, 0:ntok], qkv[:, j, 0:ntok], ALU.mult, [('qkv', j)], [('sqs', j)])
                mm(ps[0][:, 0:ntok], blk64b[:], sqs[:, j, 0:ntok], True, True, ['blk64b', ('sqs', j)], ['ps0'])
                act(lnt[:, 0:ntok], ps[0][:, 0:ntok], AF.Ln, ['ps0'], ['lnt'], bias=eps_col[:, 0:1], scale=1.0)
                act(rstd[:, 0:ntok], lnt[:, 0:ntok], AF.Exp, ['lnt'], ['rstd'], scale=-0.5)
                if j < 2:
                    stt(qkv[:, j, 0:ntok], qkv[:, j, 0:ntok], 0.125, rstd[:, 0:ntok], ALU.mult, ALU.mult,
                        [('qkv', j), 'rstd'], [('qkv', j)])
                else:
                    tt(qkv[:, j, 0:ntok], qkv[:, j, 0:ntok], rstd[:, 0:ntok], ALU.mult, [('qkv', j), 'rstd'], [('qkv', j)])
            for _ in gla_chunk(l, tile, 0):
                pass

            def side_work(c_):
                if c_ + 1 < nsub:
                    for _ in gla_chunk(l, tile, c_ + 1):
                        yield
                cm_chunk(l, tile, c_)
                yield

            for c in range(nsub):
                nxt = side_work(c)
                gdn_chunk(l, tile, c, tick=((lambda g_=nxt: next(g_, None)) if mode == 'P' else None))
                for _ in nxt:
                    pass
            if tile['last']:
                state_out(l, tile)
            P.barrier()

            merge(l, tile)
            ffn(l, tile)
            ple(l, tile)

        gelu_ctr = [0]

        def gelu(out, outkey, pin, pkey, n):
            gelu_ctr[0] += 1
            if gelu_ctr[0] % 2:
                xsb, xk, t2b, tk = tmpC, 'tmpC', lnt, 'lnt'
            else:
                xsb, xk, t2b, tk = tmpA, 'tmpA', tmpB, 'tmpB'
            xs = xsb[:, 0:n]
            cp(xs, pin, [pkey], [xk], eng='act')
            t2 = t2b[:, 0:n]
            tt(t2, xs, xs, ALU.mult, [xk], [tk])
            ts(t2, t2, 0.044715, 1.0, ALU.mult, ALU.add, [tk], [tk])
            tt(t2, t2, xs, ALU.mult, [tk, xk], [tk])
            act(t2, t2, AF.Sigmoid, [tk], [tk], scale=1.5957691216057308)
            tt(out, xs, t2, ALU.mult, [xk, tk], [outkey])

        def sc_state_out(l, tile, shv, Ls):
            mode = tile['mode']
            if mode == 'P':
                cp(halo_sc[:, l, :, :], shv[:, :, 0, Ls:Ls + 2], ['shp'], ['halo_sc'])
                if not tile['last']:
                    return
                for j in range(2):
                    tr(ps[4][0:2, 128 * j:128 * j + 128], shv[:, j, 0, Ls:Ls + 2], 128, ['shp'], ['ps4'])
                cp(st_tm[0:2, 0:256], ps[4][0:2, 0:256], ['ps4'], ['ct8', 'ct9'])
                dma(o_sc_p[l], st_tm[0:2, 0:256], ['ct8', 'ct9'], [], is_out=True)
            else:
                cp(cs[0][:, 0:64].rearrange("p (j s t) -> p j s t", j=2, s=16), shv[:, :, :, Ls:Ls + 2], ['shp'], ['cs0'])
                for j in range(2):
                    tr(ps[4][0:32, 128 * j:128 * j + 128], cs[0][:, 32 * j:32 * j + 32], 128, ['cs0'], ['ps4'])
                cp(st_tm[0:32, 0:256], ps[4][0:32, 0:256], ['ps4'], ['ct8', 'ct9'])
                dma(o_sc_s[l], st_tm[0:32, 0:256], ['ct8', 'ct9'], [], is_out=True)

        def gc_state_out(l, tile, xpv, Ls):
            mode = tile['mode']
            if mode == 'P':
                cp(halo_gc[:, l, :, :], xpv[:, :, 0, Ls:Ls + 3], XPK, ['halo_gc'])
                if not tile['last']:
                    return
                for j in range(6):
                    tr(ps[4 + j // 4][0:3, 128 * (j % 4):128 * (j % 4) + 128], xpv[:, j, 0, Ls:Ls + 3], 128, XPK,
                       ['ps%d' % (4 + j // 4)])
                cp(st_tm[0:3, 0:512], ps[4][0:3, 0:512], ['ps4'], ['ct8', 'ct9'])
                cp(st_tm[0:3, 512:768], ps[5][0:3, 0:256], ['ps5'], ['ct8', 'ct9'])
                dma(o_gc_p[l], st_tm[0:3, :], ['ct8', 'ct9'], [], is_out=True)
            else:
                cp(cs[1][:, 0:144].rearrange("p (j s t) -> p j s t", j=3, s=16), xpv[:, 0:3, :, Ls:Ls + 3], XPK, ['cs1'])
                cp(cs[2][:, 0:144].rearrange("p (j s t) -> p j s t", j=3, s=16), xpv[:, 3:6, :, Ls:Ls + 3], XPK, ['cs2'])
                for j in range(6):
                    srcb = cs[1] if j < 3 else cs[2]
                    srck = 'cs1' if j < 3 else 'cs2'
                    tr(ps[4 + j // 4][0:48, 128 * (j % 4):128 * (j % 4) + 128], srcb[:, 48 * (j % 3):48 * (j % 3) + 48], 128,
                       [srck], ['ps%d' % (4 + j // 4)])
                cp(st_tm[0:48, 0:512], ps[4][0:48, 0:512], ['ps4'], ['ct8', 'ct9'])
                cp(st_tm[0:48, 512:768], ps[5][0:48, 0:256], ['ps5'], ['ct8', 'ct9'])
                dma(o_gc_s[l], st_tm[0:48, :], ['ct8', 'ct9'], [], is_out=True)

        def state_out(l, tile):
            if tile['mode'] == 'P':
                dma(o_gla_p[l], Sgla_p[:, l, :], ['Sgla_p'], [], is_out=True)
                for pair in range(2):
                    for h2 in range(2):
                        dma(o_gdn_p[l, 2 * pair + h2], Sgdn_p[64 * h2:64 * h2 + 64, l, pair, :], ['Sgdn_p'], [], is_out=True)
            else:
                dma(o_gla_s[l].rearrange("s p v -> p s v"), Sgla_s[:], ['Sgla_s'], [], is_out=True)
                for pair in range(2):
                    for h2 in range(2):
                        dma(o_gdn_s[l, :, 2 * pair + h2].rearrange("s k v -> k s v"), Sgdn_s[64 * h2:64 * h2 + 64, :, pair, :],
                            ['Sgdn_s'], [], is_out=True)

        def gla_chunk(l, tile, c):
            mode = tile['mode']
            nseq = 1 if mode == 'P' else 16
            Lq = 128 // nseq
            tok = slice(128 * c, 128 * c + 128)
            triU = C('triU' + mode)

            def s3(ap):
                return ap.rearrange("p (s t) -> p s t", s=nseq)
            gcs = [xp[:, 3 + i // 2, 256 * (i % 2):256 * (i % 2) + 256] for i in range(6)]
            gct = [xp[:, i, 0:512] for i in range(3)]

            def gk(i, b):
                return ('xp', 3 + i // 2, 2 * (i % 2) + b)
            mm(ps[0][:, 0:128], gaT[:, tok], wa2_sb[:, 128 * l:128 * l + 128], True, True, ['gaT', 'gaT1', 'wa2', 'wa2b'], ['ps0'])
            e1 = gcs[0][:, 0:128]
            act(e1, ps[0][:, 0:128], AF.Exp, ['ps0'], [gk(0, 0)], scale=-1.0)
            sp_ = gcs[0][:, 128:256]
            act(sp_, e1, AF.Ln, [gk(0, 0)], [gk(0, 1)], bias=one_col[:, 0:1], scale=1.0)
            yield
            mm(ps[1][:, 0:128], sp_, triU, True, True, [gk(0, 1), 'cst'], ['ps1'])
            bT = gcs[1][:, 0:128]
            cp(bT, ps[1][:, 0:128], ['ps1'], [gk(1, 0)], eng='act')
            eb = gcs[1][:, 128:256]
            act(eb, bT, AF.Exp, [gk(1, 0)], [gk(1, 1)], scale=-1.0 / 16)
            enb = gcs[2][:, 0:128]
            act(enb, bT, AF.Exp, [gk(1, 0)], [gk(2, 0)], scale=1.0 / 16)
            yield
            dif = gcs[2][:, 128:256]
            tt(s3(dif), s3(bT), s3(bT)[:, :, Lq - 1:Lq].to_broadcast([128, nseq, Lq]), ALU.subtract, [gk(1, 0)], [gk(2, 1)])
            act(dif, dif, AF.Exp, [gk(2, 1)], [gk(2, 1)], scale=1.0 / 16)
            dec = sm[1][:, 0:nseq]
            act(dec, s3(bT)[:, :, Lq - 1], AF.Exp, [gk(1, 0)], ['sm1'], scale=-1.0 / 16)
            yield
            qd = gcs[3][:, 0:128]
            stt(qd, qT[:, tok], float(32 ** -0.5), eb, ALU.mult, ALU.mult, ['qT', gk(1, 1)], [gk(3, 0)])
            kd = gcs[3][:, 128:256]
            tt(kd, kT[:, tok], enb, ALU.mult, ['kT', gk(2, 0)], [gk(3, 1)])
            ke = gcs[4][:, 0:128]
            tt(ke, kT[:, tok], dif, ALU.mult, ['kT', gk(2, 1)], [gk(4, 0)])
            qdm = gct[0]
            tt(qdm[:, :].rearrange("p (h t) -> p h t", h=4), qd.unsqueeze(1).to_broadcast([128, 4, 128]),
               C('headmask')[:, 0:4].unsqueeze(2).to_broadcast([128, 4, 128]), ALU.mult, [gk(3, 0), 'cst'], [('xp', 0)])
            yield
            for h in range(4):
                mm(ps[2][:, 128 * h:128 * h + 128], kd, qdm[:, 128 * h:128 * h + 128], True, True, [gk(3, 1), ('xp', 0)], ['ps2'])
            AT = gct[1]
            tt(AT[:, :].rearrange("p (h t) -> p h t", h=4), ps[2][:, :].rearrange("p (h t) -> p h t", h=4),
               triU.unsqueeze(1).to_broadcast([128, 4, 128]), ALU.mult, ['ps2', 'cst'], [('xp', 1)])
            yield
            tr(ps[3][:, 0:128], ke, 128, [gk(4, 0)], ['ps3'])
            kem = gct[2]
            tt(kem[:, :].rearrange("p (h t) -> p h t", h=4), ps[3][:, 0:128].unsqueeze(1).to_broadcast([128, 4, 128]),
               C('hmfree').rearrange("p (h t) -> p h t", h=4), ALU.mult, ['ps3', 'cst'], [('xp', 2)])
            yield
            Sk = 'Sgla_p' if mode == 'P' else 'Sgla_s'
            for h in range(4):
                h2, pair = h % 2, h // 2
                ob = ps[0][64 * h2:64 * h2 + 64, 128 * pair:128 * pair + 128]
                if mode == 'P':
                    mm(ob, gv_tm[:, c, 64 * h:64 * h + 64], AT[:, 128 * h:128 * h + 128], True, False, ['gv_tm', ('xp', 1)], ['ps0'])
                    mm(ob, Sgla_p[:, l, :], qdm[:, 128 * h:128 * h + 128], False, True, [Sk, ('xp', 0)], ['ps0'])
                else:
                    mm(ob, gv_tm[:, c, 64 * h:64 * h + 64], AT[:, 128 * h:128 * h + 128], True, False, ['gv_tm', ('xp', 1)], ['ps0'])
                    for s_ in range(16):
                        mm(ps[0][64 * h2:64 * h2 + 64, 128 * pair + 8 * s_:128 * pair + 8 * s_ + 8], Sgla_s[:, s_, :],
                           qdm[:, 128 * h + 8 * s_:128 * h + 8 * s_ + 8], False, s_ == 15, [Sk, ('xp', 0)], ['ps0'])
            yield
            if mode == 'P':
                for h in range(4):
                    mm(ps[1][:, 0:64], kem[:, 128 * h:128 * h + 128], gv_tm[:, c, 64 * h:64 * h + 64], h == 0, h == 3,
                       [('xp', 2), 'gv_tm'], ['ps1'])
                stt(Sgla_p[:, l, :], Sgla_p[:, l, :], dec[:, 0:1], ps[1][:, 0:64], ALU.mult, ALU.add,
                    [Sk, 'sm1', 'ps1'], [Sk])
            else:
                for half in range(2):
                    tt(um[:, :, :], gv_tm[:, c, :].unsqueeze(1).to_broadcast([128, 8, 256]),
                       C('seqmask')[:, 8 * half:8 * half + 8].unsqueeze(2).to_broadcast([128, 8, 256]), ALU.mult,
                       ['gv_tm', 'cst'], UMK)
                    for h in range(4):
                        mm(ps[1][:, :].rearrange("p (s v) -> p s v", s=8), kem[:, 128 * h:128 * h + 128],
                           um[:, :, 64 * h:64 * h + 64], h == 0, h == 3, [('xp', 2)] + UMK, ['ps1'])
                    sl = slice(8 * half, 8 * half + 8)
                    tt(Sgla_s[:, sl, :], Sgla_s[:, sl, :], dec[:, sl].unsqueeze(2).to_broadcast([128, 8, 64]), ALU.mult,
                       [Sk, 'sm1'], [Sk])
                    tt(Sgla_s[:, sl, :], Sgla_s[:, sl, :], ps[1][:, :].rearrange("p (s v) -> p s v", s=8), ALU.add,
                       [Sk, 'ps1'], [Sk])
            yield
            head_norm_gate(ps[0], 'ps0', cvec3[:, l:l + 1], grs, 'grs', 0, tok,
                           gcs[5], [gk(5, 0), gk(5, 1)], tmpA[:, 0:256], ['tmpA'], ps[3], 'ps3')
            yield

        one_col = sb("one_col", [128, 4])

        def head_norm_gate(pso, pskey, gaincol, gate, gatekey, bidx, tok, o_sb=None, ok=None, sq=None, sk=None,
                           psq=None, psqk=None):
            if o_sb is None:
                o_sb, ok, sq, sk, psq, psqk = cs[5], ['cs5'], cs[6], ['cs6'], ps[7], 'ps7'
            cp(o_sb[:, :], pso[:, 0:256], [pskey], ok, eng='act')
            tt(sq[:, :], o_sb[:, :], o_sb[:, :], ALU.mult, ok, sk)
            mm(psq[:, 0:256], C('blk64'), sq[:, :], True, True, ['cst'] + sk, [psqk])
            act(sq[:, :], psq[:, 0:256], AF.Ln, [psqk], sk, bias=eps_col[:, 0:1], scale=1.0 / 64)
            act(sq[:, :], sq[:, :], AF.Exp, sk, sk, scale=-0.5)
            stt(o_sb[:, :], o_sb[:, :], gaincol, sq[:, :], ALU.mult, ALU.mult, ok + sk + ['cvec3'], ok)
            tt(brT[:, bidx, :, tok], o_sb[:, :].rearrange("p (j t) -> p j t", j=2), gate[:, :, tok], ALU.mult,
               ok + [gatekey], [('brT', bidx)])

        def gdn_chunk(l, tile, c, tick=None):
            mode = tile['mode']
            nseq = 1 if mode == 'P' else 16
            Lq = 128 // nseq
            tok = slice(128 * c, 128 * c + 128)
            triU = C('triU' + mode)
            H4 = lambda ap: ap.rearrange("p (h t) -> p h t", h=4)
            for j in range(2):
                tr(ps[4][:, 128 * j:128 * j + 128], qkv[:, 2 + j, tok], 128, [('qkv', 2 + j)], ['ps4'])
                tr(ps[4][:, 256 + 128 * j:256 + 128 * j + 128], qkv[:, 4 + j, tok], 128, [('qkv', 4 + j)], ['ps4'])
            k_tm = cs[0]
            v_tm = cs[1]
            cp(k_tm[:, :], ps[4][:, 0:256], ['ps4'], ['cs0'], eng='act')
            cp(v_tm[:, :], ps[4][:, 256:512], ['ps4'], ['cs1'], eng='act')
            g_ = sm[2]
            tt(g_[:, 0:4], ab_tm[:, c, 0:4], dtb_b[:, 4 * l:4 * l + 4], ALU.add, ['ab_tm', 'dtb'], ['sm2'])
            act(g_[:, 0:4], g_[:, 0:4], AF.Exp, ['sm2'], ['sm2'])
            act(g_[:, 0:4], g_[:, 0:4], AF.Ln, ['sm2'], ['sm2'], bias=one_col[:, 0:1], scale=1.0)
            tt(g_[:, 0:4], g_[:, 0:4], nega_b[:, 4 * l:4 * l + 4], ALU.mult, ['sm2', 'nega'], ['sm2'])
            be = sm[3]
            act(be[:, 0:4], ab_tm[:, c, 4:8], AF.Exp, ['ab_tm'], ['sm3'], scale=-1.0)
            ts(be[:, 0:4], be[:, 0:4], 1.0, None, ALU.add, None, ['sm3'], ['sm3'])
            recip(be[:, 0:4], be[:, 0:4], ['sm3'], ['sm3'])
            gbc = ct[0]
            cp(H4(gbc[:, :]), g_[:, 0:4].unsqueeze(2).to_broadcast([128, 4, 128]), ['sm2'], ['ct0'])
            for h in range(4):
                mm(ps[5][:, 128 * h:128 * h + 128], gbc[:, 128 * h:128 * h + 128], triU, True, True, ['ct0', 'cst'], ['ps5'])
            mm(ps[6][:, 0:4], triU, g_[:, 0:4], True, True, ['cst', 'sm2'], ['ps6'])
            mm(ps[6][:, 4:8], C('tot' + mode), g_[:, 0:4], True, True, ['cst', 'sm2'], ['ps6'])
            Gc = sm[4]
            cp(Gc[:, 0:8], ps[6][:, 0:8], ['ps6'], ['sm4'])
            Gb = ct[1]
            cp(Gb[:, :], ps[5][:, :], ['ps5'], ['ct1'], eng='act')
            d_ = ct[2]
            tt(H4(d_[:, :]), H4(Gb[:, :]), Gc[:, 0:4].unsqueeze(2).to_broadcast([128, 4, 128]), ALU.subtract, ['ct1', 'sm4'], ['ct2'])
            dL = ct[3]
            tt(H4(dL[:, :]), H4(d_[:, :]), C('bigL' + mode).unsqueeze(1).to_broadcast([128, 4, 128]), ALU.add, ['ct2', 'cst'], ['ct3'])
            act(dL[:, :], dL[:, :], AF.Exp, ['ct3'], ['ct3'], scale=-1.0)
            dU = ct[4]
            tt(H4(dU[:, :]), H4(d_[:, :]), C('bigU' + mode).unsqueeze(1).to_broadcast([128, 4, 128]), ALU.subtract, ['ct2', 'cst'], ['ct4'])
            act(dU[:, :], dU[:, :], AF.Exp, ['ct4'], ['ct4'])
            eG = sm[5]
            act(eG[:, 0:4], Gc[:, 0:4], AF.Exp, ['sm4'], ['sm5'])
            bw = sm[6]
            tt(bw[:, 0:4], eG[:, 0:4], be[:, 0:4], ALU.mult, ['sm5', 'sm3'], ['sm6'])
            ts(bw[:, 4:8], bw[:, 0:4], -1.0, None, ALU.mult, None, ['sm6'], ['sm6'])
            ek = sm[7]
            tt(ek[:, 0:4], Gc[:, 4:8], Gc[:, 0:4], ALU.subtract, ['sm4'], ['sm7'])
            act(ek[:, 0:4], ek[:, 0:4], AF.Exp, ['sm7'], ['sm7'])
            hm2 = C('blk64').rearrange("p (a b) -> p a b", a=2)[:, :, 0]
            kmask, qmask = ct[0], ct[2]
            for pair in range(2):
                tt(H4(kmask[:, :])[:, 2 * pair:2 * pair + 2, :], qkv[:, 2 + pair, tok].unsqueeze(1).to_broadcast([128, 2, 128]),
                   hm2.unsqueeze(2).to_broadcast([128, 2, 128]), ALU.mult, [('qkv', 2 + pair), 'cst'], ['ct0'])
                tt(H4(qmask[:, :])[:, 2 * pair:2 * pair + 2, :], qkv[:, pair, tok].unsqueeze(1).to_broadcast([128, 2, 128]),
                   hm2.unsqueeze(2).to_broadcast([128, 2, 128]), ALU.mult, [('qkv', pair), 'cst'], ['ct2'])
            for h in range(4):
                h2, pair = h % 2, h // 2
                mm(ps[6][:, 128 * h:128 * h + 128], qkv[:, 2 + pair, tok], kmask[:, 128 * h:128 * h + 128], True, True,
                   [('qkv', 2 + pair), 'ct0'], ['ps6'])
                mm(ps[7][:, 128 * h:128 * h + 128], qkv[:, 2 + pair, tok], qmask[:, 128 * h:128 * h + 128], True, True,
                   [('qkv', 2 + pair), 'ct2'], ['ps7'])
            Nm = ct[5]
            tt(H4(Nm[:, :]), H4(dL[:, :]), C('strictL' + mode).unsqueeze(1).to_broadcast([128, 4, 128]), ALU.mult, ['ct3', 'cst'], ['ct5'])
            tt(Nm[:, :], Nm[:, :], ps[6][:, :], ALU.mult, ['ct5', 'ps6'], ['ct5'])
            tt(H4(Nm[:, :]), H4(Nm[:, :]), be[:, 0:4].unsqueeze(2).to_broadcast([128, 4, 128]), ALU.mult, ['ct5', 'sm3'], ['ct5'])
            qkT = ct[6]
            tt(qkT[:, :], dU[:, :], ps[7][:, :], ALU.mult, ['ct4', 'ps7'], ['ct6'])
            RU = cs[2]
            tt(RU[:, :].rearrange("p (h v) -> p h v", h=4), v_tm[:, :].rearrange("p (h v) -> p h v", h=4),
               be[:, 0:4].unsqueeze(2).to_broadcast([128, 4, 64]), ALU.mult, ['cs1', 'sm3'], ['cs2'])
            RW = cs[3]
            tt(RW[:, :].rearrange("p (h v) -> p h v", h=4), k_tm[:, :].rearrange("p (h v) -> p h v", h=4),
               bw[:, 4:8].unsqueeze(2).to_broadcast([128, 4, 64]), ALU.mult, ['cs0', 'sm6'], ['cs3'])
            kend = cs[4]
            tt(kend[:, :].rearrange("p (h v) -> p h v", h=4), k_tm[:, :].rearrange("p (h v) -> p h v", h=4),
               ek[:, 0:4].unsqueeze(2).to_broadcast([128, 4, 64]), ALU.mult, ['cs0', 'sm7'], ['cs4'])
            for h in range(4):
                tr(ps[5][:, 128 * h:128 * h + 128], Nm[:, 128 * h:128 * h + 128], 128, ['ct5'], ['ps5'])
            NT = ct[7]
            cp(NT[:, :], ps[5][:, :], ['ps5'], ['ct7'], eng='act')
            PT = ct[8]
            ts(PT[:, :], NT[:, :], -1.0, None, ALU.mult, None, ['ct7'], ['ct8'])
            tt(H4(PT[:, :]), H4(PT[:, :]), C('ident').unsqueeze(1).to_broadcast([128, 4, 128]), ALU.add, ['ct8', 'cst'], ['ct8'])
            A_, AT_, B_, BT_ = Nm, NT, ct[9], ct[10]
            Ak, ATk, Bk, BTk = 'ct5', 'ct7', 'ct9', 'ct10'
            nlev = 6 if mode == 'P' else 2
            psets = ((ps[5], 'ps5', ps[6], 'ps6', ps[7], 'ps7'), (ps[5], 'ps5', ps[6], 'ps6', ps[7], 'ps7'))
            for lev in range(nlev):
                last = (lev == nlev - 1)
                for hf in range(2):
                    pB, pBk, pBT, pBTk, pP, pPk = psets[hf]
                    hc = slice(256 * hf, 256 * hf + 256)
                    for h in (2 * hf, 2 * hf + 1):
                        hs = slice(128 * h, 128 * h + 128)
                        mm(pB[:, hs], AT_[:, hs], A_[:, hs], True, True, [(Ak, hf), (ATk, hf)], [pBk])
                        if not last:
                            mm(pBT[:, hs], A_[:, hs], AT_[:, hs], True, True, [(Ak, hf), (ATk, hf)], [pBTk])
                    cp(B_[:, hc], pB[:, hc], [pBk], [(Bk, hf)], eng='act')
                    if not last:
                        cp(BT_[:, hc], pBT[:, hc], [pBTk], [(BTk, hf)], eng='act')
                    for h in (2 * hf, 2 * hf + 1):
                        hs = slice(128 * h, 128 * h + 128)
                        mm(pP[:, hs], B_[:, hs], PT[:, hs], True, True, [(Bk, hf), ('ct8', hf)], [pPk])
                    tt(PT[:, hc], PT[:, hc], pP[:, hc], ALU.add, [('ct8', hf), pPk], [('ct8', hf)])
                    if tick is not None:
                        tick()
                A_, AT_, B_, BT_ = B_, BT_, A_, AT_
                Ak, ATk, Bk, BTk = Bk, BTk, Ak, ATk
            if tick is not None:
                tick()
            for h in range(4):
                h2, pair = h % 2, h // 2
                mm(ps[5][64 * h2:64 * h2 + 64, 128 * pair:128 * pair + 128], RW[:, 64 * h:64 * h + 64],
                   PT[:, 128 * h:128 * h + 128], True, True, ['cs3', 'ct8'], ['ps5'])
            nWT = cs[5]
            cp(nWT[:, :], ps[5][:, 0:256], ['ps5'], ['cs5'], eng='act')
            eGb = ct[9]
            act(eGb[:, :], Gb[:, :], AF.Exp, ['ct1'], ['ct9'])
            qg = cs[6]
            for pair in range(2):
                for h2 in range(2):
                    h = 2 * pair + h2
                    tt(qg[64 * h2:64 * h2 + 64, 128 * pair:128 * pair + 128], qkv[64 * h2:64 * h2 + 64, pair, tok],
                       eGb[64 * h2:64 * h2 + 64, 128 * h:128 * h + 128], ALU.mult, [('qkv', pair), 'ct9'], ['cs6'])
            nWTm, qgm = ct[5], ct[7]
            tt(nWTm[:, :].rearrange("p (a x) -> p a x", a=2), nWT[:, :].unsqueeze(1).to_broadcast([128, 2, 256]),
               hm2.unsqueeze(2).to_broadcast([128, 2, 256]), ALU.mult, ['cs5', 'cst'], ['ct5'])
            tt(qgm[:, :].rearrange("p (a x) -> p a x", a=2), qg[:, :].unsqueeze(1).to_broadcast([128, 2, 256]),
               hm2.unsqueeze(2).to_broadcast([128, 2, 256]), ALU.mult, ['cs6', 'cst'], ['ct7'])
            u_sb = cs[7]
            Sk = 'Sgdn_p' if mode == 'P' else 'Sgdn_s'
            if mode == 'P':
                for h in range(4):
                    h2, pair = h % 2, h // 2
                    hp = slice(64 * h2, 64 * h2 + 64)
                    mm(ps[6][:, 64 * h:64 * h + 64], PT[:, 128 * h:128 * h + 128], RU[:, 64 * h:64 * h + 64], True, False,
                       ['ct8', 'cs2'], ['ps6'])
                    mm(ps[6][:, 64 * h:64 * h + 64], nWTm[:, 256 * h2 + 128 * pair:256 * h2 + 128 * pair + 128], Sgdn_p[:, l, pair, :],
                       False, True, ['ct5', Sk], ['ps6'])
                cp(u_sb[:, :], ps[6][:, 0:256], ['ps6'], ['cs7'], eng='act')
                if tick is not None:
                    tick()
                for h in range(4):
                    h2, pair = h % 2, h // 2
                    hp = slice(64 * h2, 64 * h2 + 64)
                    ob = ps[4][hp, 128 * pair:128 * pair + 128]
                    mm(ob, Sgdn_p[:, l, pair, :], qgm[:, 256 * h2 + 128 * pair:256 * h2 + 128 * pair + 128], True, False, [Sk, 'ct7'], ['ps4'])
                    mm(ob, u_sb[:, 64 * h:64 * h + 64], qkT[:, 128 * h:128 * h + 128], False, True, ['cs7', 'ct6'], ['ps4'])
                for h in range(4):
                    h2, pair = h % 2, h // 2
                    hp = slice(64 * h2, 64 * h2 + 64)
                    mm(ps[7][hp, 64 * pair:64 * pair + 64], kend[:, 64 * h:64 * h + 64], u_sb[:, 64 * h:64 * h + 64], True, True,
                       ['cs4', 'cs7'], ['ps7'])
                for pair in range(2):
                    for h2 in range(2):
                        h = 2 * pair + h2
                        hp = slice(64 * h2, 64 * h2 + 64)
                        stt(Sgdn_p[hp, l, pair, :], Sgdn_p[hp, l, pair, :], eGb[hp, 128 * h + 127:128 * h + 128],
                            ps[7][hp, 64 * pair:64 * pair + 64], ALU.mult, ALU.add, [Sk, 'ct9', 'ps7'], [Sk])
            else:
                for h in range(4):
                    h2, pair = h % 2, h // 2
                    hp = slice(64 * h2, 64 * h2 + 64)
                    for s_ in range(16):
                        o_ = 256 * h2 + 128 * pair + 8 * s_
                        mm(ps[6][hp, 128 * pair + 8 * s_:128 * pair + 8 * s_ + 8], Sgdn_s[:, s_, pair, :],
                           nWTm[:, o_:o_ + 8], True, True, [Sk, 'ct5'], ['ps6'])
                cp(ct[10][:, 0:256], ps[6][:, 0:256], ['ps6'], ['ct10'])
                for pair in range(2):
                    tr(ps[7][:, 128 * pair:128 * pair + 128], ct[10][:, 128 * pair:128 * pair + 128], 128, ['ct10'], ['ps7'])
                for h in range(4):
                    mm(ps[6][:, 256 + 64 * h:256 + 64 * h + 64], PT[:, 128 * h:128 * h + 128], RU[:, 64 * h:64 * h + 64], True, True,
                       ['ct8', 'cs2'], ['ps6'])
                cp(u_sb[:, :], ps[6][:, 256:512], ['ps6'], ['cs7'])
                tt(u_sb[:, :], u_sb[:, :], ps[7][:, 0:256], ALU.add, ['cs7', 'ps7'], ['cs7'])
                for h in range(4):
                    h2, pair = h % 2, h // 2
                    hp = slice(64 * h2, 64 * h2 + 64)
                    ob = ps[4][hp, 128 * pair:128 * pair + 128]
                    mm(ob, u_sb[:, 64 * h:64 * h + 64], qkT[:, 128 * h:128 * h + 128], True, False, ['cs7', 'ct6'], ['ps4'])
                    for s_ in range(16):
                        o_ = 256 * h2 + 128 * pair + 8 * s_
                        mm(ps[4][hp, 128 * pair + 8 * s_:128 * pair + 8 * s_ + 8], Sgdn_s[:, s_, pair, :],
                           qgm[:, o_:o_ + 8], False, s_ == 15, [Sk, 'ct7'], ['ps4'])
                for half in range(2):
                    sl = slice(8 * half, 8 * half + 8)
                    tt(um[:, :, :], u_sb[:, :].unsqueeze(1).to_broadcast([128, 8, 256]),
                       C('seqmask')[:, 8 * half:8 * half + 8].unsqueeze(2).to_broadcast([128, 8, 256]), ALU.mult,
                       ['cs7', 'cst'], UMK)
                    for pair in range(2):
                        pb, pk = (ps[5], 'ps5') if pair == 0 else (ps[7], 'ps7')
                        for h2 in range(2):
                            h = 2 * pair + h2
                            hp = slice(64 * h2, 64 * h2 + 64)
                            mm(pb[hp, :].rearrange("p (s v) -> p s v", s=8), kend[:, 64 * h:64 * h + 64],
                               um[:, :, 64 * h:64 * h + 64], True, True, ['cs4'] + UMK, [pk])
                        for h2 in range(2):
                            h = 2 * pair + h2
                            hp = slice(64 * h2, 64 * h2 + 64)
                            dnv = eGb[hp, 128 * h:128 * h + 128].rearrange("p (s t) -> p s t", s=16)[:, sl, 7:8]
                            tt(Sgdn_s[hp, sl, pair, :], Sgdn_s[hp, sl, pair, :], dnv.to_broadcast([64, 8, 64]), ALU.mult,
                               [Sk, 'ct9'], [Sk])
                            tt(Sgdn_s[hp, sl, pair, :], Sgdn_s[hp, sl, pair, :], pb[hp, :].rearrange("p (s v) -> p s v", s=8),
                               ALU.add, [Sk, pk], [Sk])
            head_norm_gate(ps[4], 'ps4', cvec3[:, 8 + l:8 + l + 1], dzs, 'dzs', 1, tok)

        def cm_chunk(l, tile, c):
            tok = slice(128 * c, 128 * c + 128)
            for g in range(4):
                h2, pair = g % 2, g // 2
                mm(ps[1][64 * h2:64 * h2 + 64, 128 * pair:128 * pair + 128], vn_tm[:, c, 64 * g:64 * g + 64], wmT[:, g, :], True, True,
                   ['vn_tm', 'wmT'], ['ps1'])
            s_sb = tmpB[:, 0:256]
            P.op('dve', lambda e: e.tensor_tensor(out=s_sb, in0=ps[1][:, 0:256],
                                                  in1=bsT[:, :, :].rearrange("p j t -> p (j t)"), op=ALU.add),
                 ['ps1', 'bsT'], ['tmpB'], strict=True)
            tt(brT[:, 2, :, tok], cug[:, :, tok], s_sb.rearrange("p (j t) -> p j t", j=2), ALU.mult, ['cug', 'tmpB'],
               [('brT', 2)])

        def merge(l, tile):
            ntok = tile['ntok']
            for g in range(4):
                wb_lo, wbk_lo = slab_wbranch(l, g, 0)
                wg0, wgk0 = slab('w_gate', l, 0, D, 1024 * g, 512)
                half_groups(l, g, 0, wb_lo, wbk_lo, wg0, wgk0, ntok)
                wb_hi, wbk_hi = slab_wbranch(l, g, 1)
                wg1, wgk1 = slab('w_gate', l, 0, D, 1024 * g + 512, 512)
                half_groups(l, g, 1, wb_hi, wbk_hi, wg1, wgk1, ntok)
            for og in range(KC):
                cp(sqs[:, og, 0:ntok], ct[og][:, 0:ntok], ['ct%d' % og], [('sqs', og)], eng=('act' if og % 2 else 'dve'))
            for half in range(2):
                wv, wk = slab('w_o', l, 0, D, 512 * half, 512)
                for j in range(4):
                    og = 4 * half + j
                    pb, pk = ps[og % 2], 'ps%d' % (og % 2)
                    proj_fm(pb, pk, wv, wk, 128 * j, 128, sqs, 'sqs', KC, ntok)
                    tt(hT[:, og, 0:ntok], hT[:, og, 0:ntok], pb[:, 0:ntok], ALU.add, [('h', og), pk], [('h', og)])
                    sq_h(og, ntok, 'xn')

        def slab_wbranch(l, g, half):
            spec_rows = 256
            v, k = slab_rows('w_branch', (l, g), spec_rows, 512 * half, 512)
            return v, k

        def slab_rows(wname, idx, nrows, c0, ncols):
            spec = (wname, idx, 0, nrows, c0, ncols)
            return slab(*spec)

        def half_groups(l, g, half, wb, wbk, wg, wgk, ntok):
            for j in range(4):
                og = 4 * half + j
                pg, pgk = ps[0 + 2 * (j % 2)], 'ps%d' % (0 + 2 * (j % 2))
                pbr, pbk = ps[1 + 2 * (j % 2)], 'ps%d' % (1 + 2 * (j % 2))
                proj_fm(pg, pgk, wg, wgk, 128 * j, 128, xn, 'xn', KC, ntok)
                for kc in range(2):
                    mm(pbr[:, 0:ntok], wb[:, kc, 128 * j:128 * j + 128], brT[:, g, kc, 0:ntok], kc == 0, kc == 1,
                       [rk(wbk, kc), ('brT', g)], [pbk])
                sg = tmpA if j % 2 == 0 else tmpB
                sgk = 'tmpA' if j % 2 == 0 else 'tmpB'
                act(sg[:, 0:ntok], pg[:, 0:ntok], AF.Sigmoid, [pgk], [sgk])
                if g == 0:
                    tt(ct[og][:, 0:ntok], sg[:, 0:ntok], pbr[:, 0:ntok], ALU.mult, [sgk, pbk], ['ct%d' % og])
                else:
                    tt(sg[:, 0:ntok], sg[:, 0:ntok], pbr[:, 0:ntok], ALU.mult, [sgk, pbk], [sgk])
                    tt(ct[og][:, 0:ntok], ct[og][:, 0:ntok], sg[:, 0:ntok], ALU.add, ['ct%d' % og, sgk], ['ct%d' % og])

        def ffn(l, tile):
            ntok = tile['ntok']
            rmsnorm_to_xn(ntok, lambda kc: gcol('norm_ffn', l, kc), 'xn')
            nslab = (DFF + 511) // 512
            for s_ in range(nslab):
                c0 = 512 * s_
                ncols = min(512, DFF - c0)
                wgv, wgk = slab('w_ffn_gate', l, 0, D, c0, ncols)
                wuv, wuk = slab('w_ffn_up', l, 0, D, c0, ncols)
                for j in range(ncols // 128):
                    fg = 4 * s_ + j
                    pg, pgk = ps[0 + 2 * (j % 2)], 'ps%d' % (0 + 2 * (j % 2))
                    pu, puk = ps[1 + 2 * (j % 2)], 'ps%d' % (1 + 2 * (j % 2))
                    proj_fm(pg, pgk, wgv, wgk, 128 * j, 128, xn, 'xn', KC, ntok)
                    proj_fm(pu, puk, wuv, wuk, 128 * j, 128, xn, 'xn', KC, ntok)
                    sg = tmpA if j % 2 == 0 else tmpB
                    sgk = 'tmpA' if j % 2 == 0 else 'tmpB'
                    act(sg[:, 0:ntok], pg[:, 0:ntok], AF.Silu, [pgk], [sgk])
                    tt(hid[:, fg, 0:ntok], sg[:, 0:ntok], pu[:, 0:ntok], ALU.mult, [sgk, puk], ['ct%d' % (fg // 2)])
            for og in range(KC):
                wv, wk = slab('w_ffn_down', l, 0, DFF, 128 * og, 128)
                pb, pk = ps[og % 2], 'ps%d' % (og % 2)
                for fc in range(FC):
                    mm(pb[:, 0:ntok], wv[:, fc, :], hid[:, fc, 0:ntok], fc == 0, fc == FC - 1, [rk(wk, fc), 'ct%d' % (fc // 2)], [pk])
                tt(hT[:, og, 0:ntok], hT[:, og, 0:ntok], pb[:, 0:ntok], ALU.add, [('h', og), pk], [('h', og)])
                sq_h(og, ntok, 'sqs')

        def ple(l, tile):
            ntok = tile['ntok']
            rmsnorm_to_xn(ntok, lambda kc: gcol('norm_ple', l, kc))
            for half in range(2):
                wgv, wgk = slab('w_ple_gate', l, 0, D, 512 * half, 512)
                wpv, wpk = slab('w_ple', l, 0, 256, 512 * half, 512)
                for j in range(4):
                    og = 4 * half + j
                    pg, pgk = ps[0 + 2 * (j % 2)], 'ps%d' % (0 + 2 * (j % 2))
                    pp, ppk = ps[1 + 2 * (j % 2)], 'ps%d' % (1 + 2 * (j % 2))
                    proj_fm(pg, pgk, wgv, wgk, 128 * j, 128, xn, 'xn', KC, ntok)
                    for kc in range(2):
                        mm(pp[:, 0:ntok], wpv[:, kc, 128 * j:128 * j + 128], peT[:, kc, 0:ntok], kc == 0, kc == 1,
                           [rk(wpk, kc), 'peT'], [ppk])
                    sg = tmpA if j % 2 == 0 else tmpB
                    sgk = 'tmpA' if j % 2 == 0 else 'tmpB'
                    act(sg[:, 0:ntok], pg[:, 0:ntok], AF.Sigmoid, [pgk], [sgk])
                    tt(sg[:, 0:ntok], sg[:, 0:ntok], pp[:, 0:ntok], ALU.mult, [sgk, ppk], [sgk])
                    tt(hT[:, og, 0:ntok], hT[:, og, 0:ntok], sg[:, 0:ntok], ALU.add, [('h', og), sgk], [('h', og)])
                    sq_h(og, ntok, 'sqs')

        def load_tile(tile):
            ntok = tile['ntok']
            src = x_p[tile['t0']:tile['t0'] + ntok, :] if tile['mode'] == 'P' else x_s
            for s_ in range(ntok // 128):
                dma(x_tm[:], src[128 * s_:128 * s_ + 128, :], [], ['tmpA', 'tmpB'])
                for half in range(2):
                    pb, pk = ps[4 + half], 'ps%d' % (4 + half)
                    for j in range(4):
                        kc = 4 * half + j
                        tr(pb[:, 128 * j:128 * j + 128], x_tm[:, 128 * kc:128 * kc + 128], 128, ['tmpA', 'tmpB'], [pk])
                    cp(hT[:, 4 * half:4 * half + 4, 128 * s_:128 * s_ + 128], pb[:, :].rearrange("p (j t) -> p j t", j=4),
                       [pk], [('h', 4 * half + j) for j in range(4)], eng=('act' if half else 'dve'))
            for kc in range(KC):
                sq_h(kc, ntok, 'sqs')

        def store_tile(tile):
            ntok = tile['ntok']
            dst = y_p[tile['t0']:tile['t0'] + ntok, :] if tile['mode'] == 'P' else y_s
            for kc in range(KC):
                mm(ps[0][:, 0:ntok], onesb[:], sqs[:, kc, 0:ntok], kc == 0, kc == KC - 1, ['onesb', ('sqs', kc)], ['ps0'])
            act(lnt[:, 0:ntok], ps[0][:, 0:ntok], AF.Ln, ['ps0'], ['lnt'], bias=eps_col[:, 0:1], scale=1.0 / D)
            act(rstd[:, 0:ntok], lnt[:, 0:ntok], AF.Exp, ['lnt'], ['rstd'], scale=-0.5)
            for kc in range(KC):
                stt(hT[:, kc, 0:ntok], hT[:, kc, 0:ntok], cvec[:, 96 + kc:96 + kc + 1], rstd[:, 0:ntok], ALU.mult, ALU.mult,
                    [('h', kc), 'rstd', 'cvec'], [('h', kc)])
            for s_ in range(ntok // 128):
                for half in range(2):
                    pb, pk = ps[4 + half], 'ps%d' % (4 + half)
                    for j in range(4):
                        kc = 4 * half + j
                        tr(pb[:, 128 * j:128 * j + 128], hT[:, kc, 128 * s_:128 * s_ + 128], 128, [('h', kc)], [pk])
                    cp(x_tm[:, 512 * half:512 * half + 512], pb[:, :], [pk], ['tmpA', 'tmpB'], eng=('act' if half else 'dve'))
                dma(dst[128 * s_:128 * s_ + 128, :], x_tm[:], ['tmpA', 'tmpB'], [], is_out=True)

        tiles = []
        for t in range(NT):
            tiles.append(dict(mode='P', ntok=TT, nseq=1, t0=t * TT, last=(t == NT - 1)))
        if do_sample:
            tiles.append(dict(mode='S', ntok=128, nseq=16, t0=0, last=True))

        def program():
            for tile in tiles:
                load_tile(tile)
                for l in range(L):
                    layer(l, tile)
                store_tile(tile)

        P.planning = True
        program()
        P.planning = False
        memset(eps_col[:], EPS, ['eps_col'])
        memset(one_col[:], 1.0, ['one_col'])
        try:
            setup()
            program()
        except StopBuild:
            pass
        P.finish()
        block = es.enter_context(nc.Block())
        P.emit(block)
    return nc


_NC_CACHE = {}


def make_in_maps(inputs, TP, DEPTH, ncores):
    f = lambda a: np.ascontiguousarray(np.asarray(a, dtype=np.float32))
    wnames = ['norm_mix', 'w_in', 'gla_wa2', 'gla_ba', 'gla_norm', 'gdn_conv_w', 'gdn_a_log', 'gdn_dt_bias', 'gdn_norm',
              'cm_ln_g', 'cm_ln_b', 'cm_ws', 'cm_bs', 'sc_conv_w', 'w_gate', 'w_branch', 'w_o', 'norm_ffn', 'w_ffn_gate',
              'w_ffn_up', 'w_ffn_down', 'norm_ple', 'w_ple_gate', 'w_ple', 'norm_final']
    wd = {k: f(inputs[k]) for k in wnames}
    maps = []
    for b in range(ncores):
        sl = slice(16 * b, 16 * b + 16)
        m = dict(wd)
        m['x_p'] = f(inputs['x_prompt'][b])
        m['x_s'] = f(inputs['x_sample'][sl]).reshape(128, D)
        m['st_gla'] = f(inputs['state_gla'][:, sl]).reshape(DEPTH, 16, 128, 64)
        m['st_gdn'] = f(inputs['state_gdn'][:, sl])
        m['st_gc'] = f(inputs['state_gdn_conv'][:, sl]).reshape(DEPTH, 48, 768)
        m['st_sc'] = f(inputs['state_sconv'][:, sl]).reshape(DEPTH, 32, 256)
        m['p_p'] = f(inputs['p_prompt'][:, b])
        m['p_s'] = f(inputs['p_sample'][:, sl]).reshape(DEPTH, 128, 256)
        m['consts'] = CONST_ARR
        maps.append(m)
    return maps


def gather(results, TP, DEPTH, ncores):
    R = results
    cat = lambda k, ax: np.concatenate([np.asarray(r[k], dtype=np.float32) for r in R], axis=ax)
    y_prompt = np.stack([np.asarray(r['y_p'], np.float32) for r in R], 0)
    y_sample = cat('y_s', 0).reshape(ncores * 16, 8, D)
    gla_p = np.stack([np.asarray(r['o_gla_p'], np.float32).reshape(DEPTH, 4, 32, 64) for r in R], 1)
    gla_s = np.concatenate([np.asarray(r['o_gla_s'], np.float32).reshape(DEPTH, 16, 4, 32, 64) for r in R], 1)
    gdn_p = np.stack([np.asarray(r['o_gdn_p'], np.float32) for r in R], 1)
    gdn_s = np.concatenate([np.asarray(r['o_gdn_s'], np.float32) for r in R], 1)
    gc_p = np.stack([np.asarray(r['o_gc_p'], np.float32) for r in R], 1)
    gc_s = np.concatenate([np.asarray(r['o_gc_s'], np.float32).reshape(DEPTH, 16, 3, 768) for r in R], 1)
    sc_p = np.stack([np.asarray(r['o_sc_p'], np.float32) for r in R], 1)
    sc_s = np.concatenate([np.asarray(r['o_sc_s'], np.float32).reshape(DEPTH, 16, 2, 256) for r in R], 1)
    cv_s = np.concatenate([np.asarray(r['o_cv_s'], np.float32).reshape(DEPTH, 16, 8, 256) for r in R], 1)
    return (y_prompt, y_sample, gla_p, gla_s, gdn_p, gdn_s, gc_p, gc_s, sc_p, sc_s, cv_s)


def kernel(**inputs):
    TP = int(np.asarray(inputs['x_prompt']).shape[1])
    DEPTH = int(np.asarray(inputs['w_in']).shape[0])
    ncores = int(np.asarray(inputs['x_prompt']).shape[0])
    key = (TP, DEPTH)
    if key not in _NC_CACHE:
        _NC_CACHE[key] = build(TP=TP, TT=min(512, TP), DEPTH=DEPTH)
    nc = _NC_CACHE[key]
    maps = make_in_maps(inputs, TP, DEPTH, ncores)
    res = run_bass_kernel_spmd(nc, maps, core_ids=list(range(ncores)))
    return gather(res.results, TP, DEPTH, ncores)
```

```python
import numpy as np
import concourse.bass as bass
import concourse.mybir as mybir
from concourse.bass_utils import run_bass_kernel_spmd
from contextlib import ExitStack

F32 = mybir.dt.float32
BF16 = mybir.dt.bfloat16
AF = mybir.ActivationFunctionType
ALU = mybir.AluOpType
AX = mybir.AxisListType

D = 1024
KC = 8
DFF = 2816
FC = 22
IN_W = 3096
C_GQ, C_GK, C_GV, C_GR, C_GA = 0, 128, 256, 512, 768
C_DQ, C_DK, C_DV, C_DZ, C_DA, C_DB = 784, 1040, 1296, 1552, 1808, 1812
C_CU, C_CV, C_SH, C_SB, C_SC = 1816, 2072, 2328, 2584, 2840
EPS = 1e-6
BIG = 30000.0
NSLOT = 4
SLOT_EL = 4096
KD = 8


def make_consts():
    c = {}
    idx = np.arange(128)
    c['ident'] = np.eye(128, dtype=np.float32)
    c['ones'] = np.ones((128, 128), np.float32)
    c['blk64'] = (idx[:, None] // 64 == idx[None, :] // 64).astype(np.float32)
    for mode, L in (('P', 128), ('S', 8)):
        same = (idx[:, None] // L == idx[None, :] // L)
        le = idx[:, None] <= idx[None, :]
        c['triU' + mode] = (same & le).astype(np.float32)
        c['tot' + mode] = same.astype(np.float32)
        c['bigL' + mode] = np.where(same & (idx[:, None] >= idx[None, :]), 0.0, BIG).astype(np.float32)
        c['bigU' + mode] = np.where(same & le, 0.0, BIG).astype(np.float32)
        c['strictL' + mode] = (same & (idx[:, None] > idx[None, :])).astype(np.float32)
    c['headmask'] = np.zeros((128, 128), np.float32)
    c['headmask'][:, 0:4] = (idx[:, None] // 32 == np.arange(4)[None, :])
    c['seqmask'] = np.zeros((128, 128), np.float32)
    c['seqmask'][:, 0:16] = (idx[:, None] // 8 == np.arange(16)[None, :])
    hf = np.zeros((128, 4, 128), np.float32)
    for h in range(4):
        hf[:, h, 32 * h:32 * h + 32] = 1.0
    c['hmfree'] = hf.reshape(128, 512)
    e8 = np.zeros((128, 128), np.float32)
    e8[0:8, :] = (np.arange(8)[:, None] == idx[None, :] % 8)
    c['e8'] = e8
    names = ['ident', 'ones', 'blk64', 'triUP', 'totP', 'bigLP', 'bigUP', 'strictLP',
             'triUS', 'totS', 'bigLS', 'bigUS', 'strictLS', 'headmask', 'seqmask', 'e8', 'hmfree']
    offs = {}
    o = 0
    for n in names:
        offs[n] = (o, c[n].shape[1])
        o += c[n].shape[1]
    arr = np.concatenate([c[n] for n in names], axis=1).astype(np.float32)
    return arr, offs


CONST_ARR, CONST_OFF = make_consts()


class StopBuild(Exception):
    pass


class Prog:
    ENGS = ('pe', 'act', 'dve', 'pool', 'sp')
    max_ops = None
    log = None
    names = None

    def __init__(self, nc, es):
        self.nc = nc
        self.q = {e: [] for e in self.ENGS}
        self.sem = {e: es.enter_context(nc.semaphore('s_' + e)) for e in ('pe', 'act', 'dve')}
        self.cnt = {e: 0 for e in ('pe', 'act', 'dve')}
        self.dsem = {e: [es.enter_context(nc.semaphore('d_%s%d' % (e, i))) for i in range(KD)]
                     for e in ('pool', 'sp')}
        self.dcnt = {'pool': 0, 'sp': 0}
        self.waited = {}
        self.res = {}
        self.planning = False
        self.semname = {}
        for e in self.sem:
            self.semname[id(self.sem[e])] = e
        for e in self.dsem:
            for i, s in enumerate(self.dsem[e]):
                self.semname[id(s)] = '%s%d' % (e, i)
        self.out_events = []

    def _wait(self, eng, sem, val):
        key = (eng, id(sem))
        if self.waited.get(key, 0) >= val:
            return
        self.waited[key] = val
        self.q[eng].append(lambda e, s=sem, v=val: e.wait_ge(s, v))

    SPLIT = ('ct5', 'ct7', 'ct8', 'ct9', 'ct10')

    def _expand(self, keys):
        out = []
        for k in keys:
            if k in self.SPLIT:
                out.append((k, 0))
                out.append((k, 1))
            elif isinstance(k, tuple) and len(k) == 2 and k[0] == 'xp':
                out.extend([('xp', k[1], q) for q in range(4)])
            else:
                out.append(k)
        return out

    def op(self, eng, fn, reads=(), writes=(), is_out=False, strict=False):
        if self.planning:
            return
        reads = self._expand(reads)
        writes = self._expand(writes)
        self.nops = getattr(self, 'nops', 0) + 1
        if self.max_ops is not None and self.nops > self.max_ops:
            raise StopBuild()
        if self.log is not None:
            import sys as _s
            fr = _s._getframe(2)
            self.log.append((self.nops, eng, fr.f_code.co_name, fr.f_lineno, _s._getframe(3).f_code.co_name, _s._getframe(3).f_lineno))
        deps = []
        for k in reads:
            r = self.res.get(k)
            if r and r['w']:
                deps.append(r['w'] + ('raw',))
        for k in writes:
            r = self.res.get(k)
            if r:
                if r['w']:
                    deps.append(r['w'] + ('waw',))
                for (sid, (sem, val, src)) in r['r'].items():
                    deps.append((sem, val, src, 'war'))
        for (sem, val, src, kind) in deps:
            if src == eng:
                if eng == 'pe':
                    continue
                if eng in ('act', 'dve') and kind != 'raw' and not strict:
                    continue
            self._wait(eng, sem, val)
        if eng in ('sp', 'pool'):
            i = self.dcnt[eng]
            self.dcnt[eng] += 1
            sem = self.dsem[eng][i % KD]
            val = 16 * (i // KD + 1)
            if i >= KD:
                self._wait(eng, sem, val - 16)
            self.q[eng].append(lambda e, f=fn, s=sem, n_=self.nops: self._name(n_, f(e).then_inc(s, 16)))
        else:
            self.cnt[eng] += 1
            sem = self.sem[eng]
            val = self.cnt[eng]
            self.q[eng].append(lambda e, f=fn, s=sem, n_=self.nops: self._name(n_, f(e).then_inc(s, 1)))
        ev = (sem, val, eng)
        for k in reads:
            r = self.res.setdefault(k, {'w': None, 'r': {}})
            old = r['r'].get(id(sem))
            if old is None or old[1] < val:
                r['r'][id(sem)] = ev
        for k in writes:
            self.res[k] = {'w': ev, 'r': {}}
        if is_out:
            self.out_events.append(ev)

    def _name(self, n_, ins):
        if self.names is not None:
            try:
                self.names[ins.ins.name] = n_
            except Exception:
                pass
        return ins

    def war_guard(self, eng, keys):
        if self.planning:
            return
        for k in keys:
            r = self.res.get(k)
            if r:
                for (sem, val, src) in r['r'].values():
                    self._wait(eng, sem, val)
                if r['w'] and not r['r']:
                    self._wait(eng, r['w'][0], r['w'][1])
                self.res[k] = {'w': None, 'r': {}}

    def barrier(self):
        if self.planning:
            return
        for e in ('pe', 'act', 'dve'):
            for s in ('pe', 'act', 'dve'):
                if s != e and self.cnt[s] > 0:
                    self._wait(e, self.sem[s], self.cnt[s])

    def finish(self):
        last = {}
        for (sem, val, src) in self.out_events:
            if last.get(id(sem), (None, 0))[1] < val:
                last[id(sem)] = (sem, val)
        for (sem, val) in last.values():
            self._wait('sp', sem, val)

    def emit(self, block):
        nc = self.nc
        q = self.q

        @block.tensor
        def _(e):
            for f in q['pe']:
                f(e)

        @block.scalar
        def _(e):
            for f in q['act']:
                f(e)

        @block.vector
        def _(e):
            for f in q['dve']:
                f(e)

        @block.gpsimd
        def _(e):
            for f in q['pool']:
                f(e)

        @block.sync
        def _(e):
            for f in q['sp']:
                f(e)


def build(TP=2048, TT=512, DEPTH=4, do_sample=True, max_ops=None, log=None, names=None):
    nc = bass.Bass("TRN2", target_bir_lowering=False)
    NT = TP // TT
    L = DEPTH

    def din(name, shape):
        return nc.dram_tensor(name, list(shape), F32, kind="ExternalInput").ap()

    def dout(name, shape):
        return nc.dram_tensor(name, list(shape), F32, kind="ExternalOutput").ap()

    x_p = din("x_p", [TP, D])
    x_s = din("x_s", [128, D])
    st_gla = din("st_gla", [L, 16, 128, 64])
    st_gdn = din("st_gdn", [L, 16, 4, 64, 64])
    st_gc = din("st_gc", [L, 48, 768])
    st_sc = din("st_sc", [L, 32, 256])
    p_p = din("p_p", [L, TP, 256])
    p_s = din("p_s", [L, 128, 256])
    consts_d = din("consts", list(CONST_ARR.shape))
    W = {}
    wshapes = dict(norm_mix=[L, D], w_in=[L, D, IN_W], gla_wa2=[L, 16, 128], gla_ba=[L, 128], gla_norm=[L, 64],
                   gdn_conv_w=[L, 4, 768], gdn_a_log=[L, 4], gdn_dt_bias=[L, 4], gdn_norm=[L, 64],
                   cm_ln_g=[L, 256], cm_ln_b=[L, 256], cm_ws=[L, 4, 128, 128], cm_bs=[L, 4, 128],
                   sc_conv_w=[L, 3, 256], w_gate=[L, D, 4 * D], w_branch=[L, 4, 256, D], w_o=[L, D, D],
                   norm_ffn=[L, D], w_ffn_gate=[L, D, DFF], w_ffn_up=[L, D, DFF], w_ffn_down=[L, DFF, D],
                   norm_ple=[L, D], w_ple_gate=[L, D, D], w_ple=[L, 256, D], norm_final=[D])
    for k, s in wshapes.items():
        W[k] = din(k, s)
    y_p = dout("y_p", [TP, D])
    y_s = dout("y_s", [128, D])
    o_gla_p = dout("o_gla_p", [L, 128, 64])
    o_gla_s = dout("o_gla_s", [L, 16, 128, 64])
    o_gdn_p = dout("o_gdn_p", [L, 4, 64, 64])
    o_gdn_s = dout("o_gdn_s", [L, 16, 4, 64, 64])
    o_gc_p = dout("o_gc_p", [L, 3, 768])
    o_gc_s = dout("o_gc_s", [L, 48, 768])
    o_sc_p = dout("o_sc_p", [L, 2, 256])
    o_sc_s = dout("o_sc_s", [L, 32, 256])
    o_cv_s = dout("o_cv_s", [L, 128, 256])

    es = ExitStack()
    with es:
        P = Prog(nc, es)
        P.max_ops = max_ops
        P.log = log
        P.names = names

        def sb(name, shape, dt=F32):
            return es.enter_context(nc.sbuf_tensor(name, list(shape), dt))

        def psum(name):
            return es.enter_context(nc.psum_tensor(name, [128, 512], F32))

        NTK = max(TT, 512)
        cst = sb("cst", list(CONST_ARR.shape))
        onesb = sb("onesb", [128, 128], BF16)
        blk64b = sb("blk64b", [128, 128], BF16)
        cvec = sb("cvec", [128, 128])
        cvec2 = sb("cvec2", [128, 128])
        cvec3 = sb("cvec3", [128, 16])
        vstage = sb("vstage", [128, 128])
        alog_b = sb("alog_b", [128, L * 4])
        dtb_b = sb("dtb_b", [128, L * 4])
        nega_b = sb("nega_b", [128, L * 4])
        wa2_sb = sb("wa2_sb", [17, L * 128])
        hT = sb("hT", [128, KC, NTK])
        xn = sb("xn", [128, KC, NTK], BF16)
        ring = sb("ring", [128, NSLOT, SLOT_EL], BF16)
        rstd = sb("rstd", [128, NTK])
        lnt = sb("lnt", [128, NTK])
        Sgla_p = sb("Sgla_p", [128, L, 64])
        Sgdn_p = sb("Sgdn_p", [128, L, 2, 64])
        halo_gc = sb("halo_gc", [128, L, 6, 3])
        halo_sc = sb("halo_sc", [128, L, 2, 2])
        Sgla_s = sb("Sgla_s", [128, 16, 64])
        Sgdn_s = sb("Sgdn_s", [128, 16, 2, 64])
        sqs = sb("sqs", [128, KC, NTK], BF16)
        qT = sb("qT", [128, NTK])
        kT = sb("kT", [128, NTK])
        grs = sb("grs", [128, 2, NTK], BF16)
        gaT = sb("gaT", [17, NTK])
        xp = sb("xp", [128, 6, max(NTK + 3, 176)])
        qkv = sb("qkv", [128, 6, NTK])
        dzs = sb("dzs", [128, 2, NTK], BF16)
        cug = sb("cug", [128, 2, NTK], BF16)
        shp = sb("shp", [128, 2, max(NTK + 2, 160)])
        sbb = sb("sbb", [128, 2, NTK])
        gv_tm = sb("gv_tm", [128, NTK // 128, 256])
        vn_tm = sb("vn_tm", [128, NTK // 128, 256])
        ab_tm = sb("ab_tm", [128, NTK // 128, 8])
        brT = sb("brT", [128, 4, 2, NTK], BF16)
        tmpAB = sb("tmpAB", [128, 1024])
        tmpA = tmpAB[:, 0:512]
        tmpB = tmpAB[:, 512:1024]
        tmpC = sb("tmpC", [128, NTK])
        peT = sb("peT", [128, 2, NTK], BF16)
        pe_tm = sb("pe_tm", [128, 256])
        x_tm = tmpAB
        lng_b = sb("lng_b", [128, 256])
        lnb_b = sb("lnb_b", [128, 256])
        bsT = sb("bsT", [128, 2, 128])
        wmT = sb("wmT", [128, 4, 128])
        ws8 = sb("ws8", [8, 4, 8])
        ctall = sb("ctall", [128, 12 * 512])
        ct = [ctall[:, 512 * i:512 * i + 512] for i in range(12)]
        hid = ctall[:, 0:FC * 256].bitcast(BF16).rearrange("p (f t) -> p f t", f=FC)
        ws_tm = ct[10].rearrange("p (g s) -> p g s", g=4)
        t18 = ct[11][0:8, :].rearrange("p (g s) -> p g s", g=4)
        st_tm = ctall[0:48, 8 * 512:8 * 512 + 768]
        cs = [sb("cs%d" % i, [128, 256]) for i in range(8)]
        sm = [sb("sm%d" % i, [128, 16]) for i in range(10)]
        um = sqs[:, :, :].rearrange("p k t -> p (k t)").bitcast(F32).rearrange("p (s v) -> p s v", s=8)
        UMK = [('sqs', kc) for kc in range(KC)]
        XPK = [('xp', j_) for j_ in range(6)]

        ps = [psum("ps%d" % i) for i in range(8)]

        def C(name):
            o, w = CONST_OFF[name]
            return cst[:, o:o + w]

        def mm(out, lhsT, rhs, start, stop, reads, writes):
            P.op('pe', lambda e: e.matmul(out, lhsT=lhsT, rhs=rhs, start=start, stop=stop), reads, writes)

        def tr(out, in_, n_in_part, reads, writes):
            idn = C('ident')[0:n_in_part, 0:n_in_part]
            P.op('pe', lambda e: e.transpose(out, in_, idn), list(reads) + ['cst'], writes)

        def act(out, in_, func, reads, writes, bias=0.0, scale=1.0, accum_out=None):
            if accum_out is None:
                P.op('act', lambda e: e.activation(out=out, in_=in_, func=func, bias=bias, scale=scale), reads, writes)
            else:
                P.op('act', lambda e: e.activation(out=out, in_=in_, func=func, bias=bias, scale=scale,
                                                   accum_out=accum_out), reads, writes)

        def tt(out, in0, in1, op, reads, writes, eng='dve'):
            P.op(eng, lambda e: e.tensor_tensor(out=out, in0=in0, in1=in1, op=op), reads, writes)

        def ts(out, in0, s1, s2, op0, op1, reads, writes, accum_out=None):
            if op1 is None:
                P.op('dve', lambda e: e.tensor_scalar(out=out, in0=in0, scalar1=s1, scalar2=None, op0=op0),
                     reads, writes)
            elif accum_out is None:
                P.op('dve', lambda e: e.tensor_scalar(out=out, in0=in0, scalar1=s1, scalar2=s2, op0=op0, op1=op1),
                     reads, writes)
            else:
                P.op('dve', lambda e: e.tensor_scalar(out=out, in0=in0, scalar1=s1, scalar2=s2, op0=op0, op1=op1,
                                                      accum_out=accum_out), reads, writes)

        def stt(out, in0, scalar, in1, op0, op1, reads, writes):
            P.op('dve', lambda e: e.scalar_tensor_tensor(out=out, in0=in0, scalar=scalar, in1=in1, op0=op0, op1=op1),
                 reads, writes)

        def cp(out, in_, reads, writes, eng='dve'):
            if eng == 'act':
                P.op('act', lambda e: e.copy(out=out, in_=in_), reads, writes)
            else:
                P.op('dve', lambda e: e.tensor_copy(out=out, in_=in_), reads, writes)

        def dma(out, in_, reads, writes, eng='sp', is_out=False):
            P.op(eng, lambda e: e.dma_start(out=out, in_=in_), reads, writes, is_out=is_out)

        def memset(ap, val, writes):
            P.op('dve', lambda e: e.memset(ap, val), [], writes)

        def recip(out, in_, reads, writes):
            P.op('dve', lambda e: e.reciprocal(out=out, in_=in_), reads, writes)

        plan = []
        ring_state = {'next_use': 0, 'next_load': 0}

        def issue_load(k):
            (wname, l, r0, nrows, c0, ncols) = plan[k]
            slot = k % NSLOT
            kc = nrows // 128
            src = W[wname][l] if l is not None else W[wname]
            P.war_guard('pool', [('ring', slot, p_) for p_ in range(8)])
            for a in range(0, kc, 4):
                b = min(kc, a + 4)
                dst = ring[:, slot, a * ncols:b * ncols].rearrange("p (k c) -> p k c", k=b - a)
                s_ap = src[r0 + a * 128:r0 + b * 128, c0:c0 + ncols].rearrange("(k p) c -> p k c", p=128)
                dma(dst, s_ap, [], [('ring', slot, a // 4)], eng='pool')

        def slab(wname, l, r0, nrows, c0, ncols):
            spec = (wname, l, r0, nrows, c0, ncols)
            kc = nrows // 128
            assert kc * ncols <= SLOT_EL, spec
            if P.planning:
                plan.append(spec)
                k = len(plan) - 1
            else:
                k = ring_state['next_use']
                assert plan[k] == spec, (plan[k], spec)
                ring_state['next_use'] += 1
                while ring_state['next_load'] < min(len(plan), k + NSLOT - 1):
                    issue_load(ring_state['next_load'])
                    ring_state['next_load'] += 1
            slot = k % NSLOT
            v = ring[:, slot, 0:kc * ncols].rearrange("p (k c) -> p k c", k=kc)
            return v, ('ring', slot)

        def rk(wk, kc):
            return (wk[0], wk[1], kc // 4)

        def setup():
            dma(cst[:], consts_d[:, :], [], ['cst'])
            cp(onesb[:], C('ones'), ['cst'], ['onesb'])
            cp(blk64b[:], C('blk64'), ['cst'], ['blk64b'])
            memset(vstage[:], 0.0, ['vstage'])
            for i, nm in enumerate(('norm_mix', 'norm_ffn', 'norm_ple')):
                dma(vstage[i * 32:i * 32 + L * 8, :], W[nm].rearrange("l (k p) -> (l k) p", p=128), [], ['vstage'])
            dma(vstage[96:104, :], W['norm_final'].rearrange("(k p) -> k p", p=128), [], ['vstage'])
            tr(ps[0][:, 0:128], vstage[:], 128, ['vstage'], ['ps0'])
            cp(cvec[:], ps[0][:, 0:128], ['ps0'], ['cvec'])
            memset(vstage[:], 0.0, ['vstage'])
            dma(vstage[0:L * 24, :], W['gdn_conv_w'].rearrange("l j (k p) -> (l j k) p", p=128), [], ['vstage'])
            dma(vstage[96:96 + L * 6, :], W['sc_conv_w'].rearrange("l j (k p) -> (l j k) p", p=128), [], ['vstage'])
            tr(ps[0][:, 0:128], vstage[:], 128, ['vstage'], ['ps0'])
            cp(cvec2[:], ps[0][:, 0:128], ['ps0'], ['cvec2'])
            memset(vstage[:], 0.0, ['vstage'])
            for half in range(2):
                dma(vstage[0:L, 64 * half:64 * half + 64], W['gla_norm'][:, :], [], ['vstage'])
                dma(vstage[8:8 + L, 64 * half:64 * half + 64], W['gdn_norm'][:, :], [], ['vstage'])
            tr(ps[0][:, 0:128], vstage[:], 128, ['vstage'], ['ps0'])
            cp(cvec3[:], ps[0][:, 0:16], ['ps0'], ['cvec3'])
            dma(wa2_sb[16:17, :], W['gla_ba'].rearrange("(o l) c -> o (l c)", o=1), [], ['wa2b'])
            dma(wa2_sb[0:16, :].rearrange("p (l c) -> p l c", l=L), W['gla_wa2'].rearrange("l r c -> r l c"), [], ['wa2'])
            o1, _w = CONST_OFF['ones']
            for q_ in range(0, NTK, 128):
                dma(gaT[16:17, q_:q_ + 128], consts_d[0:1, o1:o1 + 128], [], ['gaT1'])
            dma(alog_b[:], W['gdn_a_log'].rearrange("(o l) h -> o (l h)", o=1).to_broadcast([128, L * 4]), [], ['alog'])
            dma(dtb_b[:], W['gdn_dt_bias'].rearrange("(o l) h -> o (l h)", o=1).to_broadcast([128, L * 4]), [], ['dtb'])
            act(nega_b[:], alog_b[:], AF.Exp, ['alog'], ['nega'])
            ts(nega_b[:], nega_b[:], -1.0, None, ALU.mult, None, ['nega'], ['nega'])
            memset(Sgla_p[:], 0.0, ['Sgla_p'])
            memset(Sgdn_p[:], 0.0, ['Sgdn_p'])
            memset(halo_gc[:], 0.0, ['halo_gc'])
            memset(halo_sc[:], 0.0, ['halo_sc'])

        def gcol(which, l, kc):
            base = {'norm_mix': 0, 'norm_ffn': 32, 'norm_ple': 64}[which]
            return cvec[:, base + l * 8 + kc:base + l * 8 + kc + 1]

        def sq_h(kc, ntok, which):
            buf, key = (sqs, 'sqs') if which == 'sqs' else (xn, 'xn')
            act(buf[:, kc, 0:ntok], hT[:, kc, 0:ntok], AF.Square, [('h', kc)], [(key, kc)])

        def rmsnorm_to_xn(ntok, gsel, which='sqs'):
            buf, key = (sqs, 'sqs') if which == 'sqs' else (xn, 'xn')
            for kc in range(KC):
                mm(ps[0][:, 0:ntok], onesb[:], buf[:, kc, 0:ntok], kc == 0, kc == KC - 1,
                   ['onesb', (key, kc)], ['ps0'])
            act(lnt[:, 0:ntok], ps[0][:, 0:ntok], AF.Ln, ['ps0'], ['lnt'], bias=eps_col[:, 0:1], scale=1.0 / D)
            act(rstd[:, 0:ntok], lnt[:, 0:ntok], AF.Exp, ['lnt'], ['rstd'], scale=-0.5)
            for kc in range(KC):
                stt(xn[:, kc, 0:ntok], hT[:, kc, 0:ntok], gsel(kc), rstd[:, 0:ntok], ALU.mult, ALU.mult,
                    [('h', kc), 'rstd', 'cvec'], [('xn', kc)])

        eps_col = sb("eps_col", [128, 4])

        def proj_fm(psb, pskey, wv, wkey, c0, ncols, src, srckey, nkc, ntok, prow=0):
            for kc in range(nkc):
                mm(psb[prow:prow + ncols, 0:ntok], wv[:, kc, c0:c0 + ncols], src[:, kc, 0:ntok], kc == 0, kc == nkc - 1,
                   [rk(wkey, kc), (srckey, kc)], [pskey])

        def layer(l, tile):
            mode = tile['mode']
            ntok = tile['ntok']
            nseq = tile['nseq']
            Ls = ntok // nseq
            nsub = ntok // 128
            xn_r = [('xn', kc) for kc in range(KC)]

            def v3(ap2d):
                return ap2d.rearrange("p (s t) -> p s t", s=nseq)

            dma(lng_b[:], W['cm_ln_g'][l:l + 1, :].to_broadcast([128, 256]), [], ['lng'])
            dma(lnb_b[:], W['cm_ln_b'][l:l + 1, :].to_broadcast([128, 256]), [], ['lnb'])
            if mode == 'P':
                for g in range(4):
                    h2 = g % 2
                    dma(bsT[64 * h2:64 * h2 + 64, g // 2, :], W['cm_bs'][l, g:g + 1, :].to_broadcast([64, 128]), [], ['bsT'])
                dma(ws_tm, W['cm_ws'][l].rearrange("g t s -> t g s"), [], ['ct10'])
                for g in range(4):
                    tr(ps[4][:, 128 * g:128 * g + 128], ws_tm[:, g, :], 128, ['ct10'], ['ps4'])
                tt(wmT[:],
                   ps[4][:, :].rearrange("p (g t) -> p g t", g=4),
                   C('triUP').unsqueeze(1).to_broadcast([128, 4, 128]), ALU.mult, ['ps4', 'cst'], ['wmT'])
            else:
                for g in range(4):
                    h2 = g % 2
                    dma(bsT[64 * h2:64 * h2 + 64, g // 2, :].rearrange("p (s t) -> p s t", s=16),
                        W['cm_bs'][l, g:g + 1, 0:8].unsqueeze(1).to_broadcast([64, 16, 8]), [], ['bsT'])
                dma(ws8[:], W['cm_ws'][l, :, 0:8, 0:8].rearrange("g t s -> t g s"), [], ['ws8'])
                for g in range(4):
                    mm(ps[4][0:8, 128 * g:128 * g + 128], ws8[:, g, :], C('e8')[0:8, :], True, True, ['ws8', 'cst'], ['ps4'])
                cp(t18[:].rearrange("p g t -> p (g t)"), ps[4][0:8, :], ['ps4'], ['ct11'])
                for g in range(4):
                    mm(ps[5][:, 128 * g:128 * g + 128], C('e8')[0:8, :], t18[:, g, :], True, True, ['ct11', 'cst'], ['ps5'])
                tt(wmT[:], ps[5][:, :].rearrange("p (g t) -> p g t", g=4),
                   C('triUS').unsqueeze(1).to_broadcast([128, 4, 128]), ALU.mult, ['ps5', 'cst'], ['wmT'])
                dma(Sgla_s[:], st_gla[l].rearrange("s p v -> p s v"), [], ['Sgla_s'])
                for pair in range(2):
                    for h2 in range(2):
                        dma(Sgdn_s[64 * h2:64 * h2 + 64, :, pair, :], st_gdn[l, :, 2 * pair + h2].rearrange("s k v -> k s v"),
                            [], ['Sgdn_s'])
                dma(st_tm[:], st_gc[l], [], ['ct8', 'ct9'])
                for j in range(6):
                    tr(ps[4][:, 48 * j:48 * j + 48], st_tm[0:48, 128 * j:128 * j + 128], 48, ['ct8', 'ct9'], ['ps4'])
                cp(xp[:, :, 0:16 * 11].rearrange("p j (s t) -> p j s t", s=16)[:, :, :, 0:3],
                   ps[4][:, 0:288].rearrange("p (j s t) -> p j s t", j=6, s=16), ['ps4'], XPK)
                dma(st_tm[0:32, 0:256], st_sc[l], [], ['ct8', 'ct9'])
                for j in range(2):
                    tr(ps[4][:, 32 * j:32 * j + 32], st_tm[0:32, 128 * j:128 * j + 128], 32, ['ct8', 'ct9'], ['ps4'])
                cp(shp[:, :, 0:16 * 10].rearrange("p j (s t) -> p j s t", s=16)[:, :, :, 0:2],
                   ps[4][:, 0:64].rearrange("p (j s t) -> p j s t", j=2, s=16), ['ps4'], ['shp'])
            if mode == 'P':
                cp(xp[:, :, 0:3], halo_gc[:, l, :, :], ['halo_gc'], XPK)
                cp(shp[:, :, 0:2], halo_sc[:, l, :, :], ['halo_sc'], ['shp'])
            pe_src = (p_p[l, tile['t0']:tile['t0'] + ntok, :] if mode == 'P' else p_s[l])
            for s_ in range(nsub):
                dma(pe_tm[:], pe_src[128 * s_:128 * s_ + 128, :], [], ['pe_tm'])
                for j in range(2):
                    tr(ps[4][:, 128 * j:128 * j + 128], pe_tm[:, 128 * j:128 * j + 128], 128, ['pe_tm'], ['ps4'])
                cp(peT[:, :, 128 * s_:128 * s_ + 128], ps[4][:, 0:256].rearrange("p (j t) -> p j t", j=2), ['ps4'], ['peT'])

            xpv = xp[:, :, 0:nseq * (Ls + 3)].rearrange("p j (s t) -> p j s t", s=nseq)
            shv = shp[:, :, 0:nseq * (Ls + 2)].rearrange("p j (s t) -> p j s t", s=nseq)

            rmsnorm_to_xn(ntok, lambda kc: gcol('norm_mix', l, kc))

            wv, wk = slab('w_in', l, 0, D, 0, 512)
            proj_fm(ps[0], 'ps0', wv, wk, 0, 128, xn, 'xn', KC, ntok)
            cp(qT[:, 0:ntok], ps[0][:, 0:ntok], ['ps0'], ['qT'], eng='act')
            proj_fm(ps[1], 'ps1', wv, wk, 128, 128, xn, 'xn', KC, ntok)
            cp(kT[:, 0:ntok], ps[1][:, 0:ntok], ['ps1'], ['kT'])
            for s_ in range(nsub):
                pb = ps[2 + (s_ % 2)]
                pk = 'ps%d' % (2 + (s_ % 2))
                for kc in range(KC):
                    mm(pb[:, 0:256], xn[:, kc, 128 * s_:128 * s_ + 128], wv[:, kc, 256:512], kc == 0, kc == KC - 1,
                       [rk(wk, kc), ('xn', kc)], [pk])
                cp(gv_tm[:, s_, :], pb[:, 0:256], [pk], ['gv_tm'], eng='act')
            wv, wk = slab('w_in', l, 0, D, 512, 272)
            for j in range(2):
                pb, pk = ps[j], 'ps%d' % j
                proj_fm(pb, pk, wv, wk, 128 * j, 128, xn, 'xn', KC, ntok)
                act(grs[:, j, 0:ntok], pb[:, 0:ntok], AF.Silu, [pk], ['grs'])
            proj_fm(ps[2], 'ps2', wv, wk, 256, 16, xn, 'xn', KC, ntok)
            cp(gaT[0:16, 0:ntok], ps[2][0:16, 0:ntok], ['ps2'], ['gaT'])
            wv, wk = slab('w_in', l, 0, D, C_DQ, 512)
            for j in range(4):
                pb, pk = ps[j % 4], 'ps%d' % (j % 4)
                proj_fm(pb, pk, wv, wk, 128 * j, 128, xn, 'xn', KC, ntok)
                cp(xpv[:, j, :, 3:3 + Ls], v3(pb[:, 0:ntok]), [pk], [('xp', j)], eng=('act' if j % 2 else 'dve'))
            wv, wk = slab('w_in', l, 0, D, C_DV, 512)
            for j in range(4):
                pb, pk = ps[j % 4], 'ps%d' % (j % 4)
                proj_fm(pb, pk, wv, wk, 128 * j, 128, xn, 'xn', KC, ntok)
                if j < 2:
                    cp(xpv[:, 4 + j, :, 3:3 + Ls], v3(pb[:, 0:ntok]), [pk], [('xp', 4 + j)], eng=('act' if j % 2 else 'dve'))
                else:
                    act(dzs[:, j - 2, 0:ntok], pb[:, 0:ntok], AF.Silu, [pk], ['dzs'])
            wv, wk = slab('w_in', l, 0, D, C_DA, 8)
            for s_ in range(nsub):
                for kc in range(KC):
                    mm(ps[4][:, 8 * s_:8 * s_ + 8], xn[:, kc, 128 * s_:128 * s_ + 128], wv[:, kc, 0:8], kc == 0, kc == KC - 1,
                       [rk(wk, kc), ('xn', kc)], ['ps4'])
            cp(ab_tm[:, 0:nsub, :], ps[4][:, 0:8 * nsub].rearrange("p (s c) -> p s c", c=8), ['ps4'], ['ab_tm'])
            wv, wk = slab('w_in', l, 0, D, C_CU, 512)
            for j in range(2):
                pb, pk = ps[j], 'ps%d' % j
                proj_fm(pb, pk, wv, wk, 128 * j, 128, xn, 'xn', KC, ntok)
                gelu(cug[:, j, 0:ntok], 'cug', pb[:, 0:ntok], pk, ntok)
            for s_ in range(nsub):
                pb, pk = ps[2 + (s_ % 2)], 'ps%d' % (2 + (s_ % 2))
                for kc in range(KC):
                    mm(pb[:, 0:256], xn[:, kc, 128 * s_:128 * s_ + 128], wv[:, kc, 256:512], kc == 0, kc == KC - 1,
                       [rk(wk, kc), ('xn', kc)], [pk])
                gelu(cs[s_][:, :], 'cs%d' % s_, pb[:, 0:256], pk, 256)
            for s_ in range(nsub):
                gk_ = 'cs%d' % s_
                P.op('dve', lambda e, b=cs[s_]: e.reduce_sum(out=sm[0][:, 0:1], in_=b[:, :], axis=AX.X), [gk_], ['sm0'])
                ts(sm[0][:, 1:2], sm[0][:, 0:1], -1.0 / 256, None, ALU.mult, None, ['sm0'], ['sm0b'])
                ts(cs[4][:, :], cs[s_][:, :], sm[0][:, 1:2], None, ALU.add, None, [gk_, 'sm0b'], ['cs4'])
                tt(cs[5][:, :], cs[4][:, :], cs[4][:, :], ALU.mult, ['cs4'], ['cs5'])
                P.op('dve', lambda e: e.reduce_sum(out=sm[0][:, 2:3], in_=cs[5][:, :], axis=AX.X), ['cs5'], ['sm0c'])
                act(sm[0][:, 3:4], sm[0][:, 2:3], AF.Ln, ['sm0c'], ['sm0d'], bias=eps_col[:, 0:1], scale=1.0 / 256)
                act(sm[0][:, 4:5], sm[0][:, 3:4], AF.Exp, ['sm0d'], ['sm0e'], scale=-0.5)
                stt(cs[5][:, :], cs[4][:, :], sm[0][:, 4:5], lng_b[:, :], ALU.mult, ALU.mult, ['cs4', 'sm0e', 'lng'], ['cs5'])
                tt(vn_tm[:, s_, :], cs[5][:, :], lnb_b[:, :], ALU.add, ['cs5', 'lnb'], ['vn_tm'])
            if mode == 'S':
                dma(o_cv_s[l], vn_tm[:, 0, :], ['vn_tm'], [], is_out=True)
            wv, wk = slab('w_in', l, 0, D, C_SH, 512)
            for j in range(2):
                pb, pk = ps[j], 'ps%d' % j
                proj_fm(pb, pk, wv, wk, 128 * j, 128, xn, 'xn', KC, ntok)
                cp(tmpA[:, 0:ntok] if j == 0 else tmpB[:, 0:ntok], pb[:, 0:ntok], [pk], ['tmpA' if j == 0 else 'tmpB'],
                   eng='act')
            for j in range(2):
                pb, pk = ps[2 + j], 'ps%d' % (2 + j)
                proj_fm(pb, pk, wv, wk, 256 + 128 * j, 128, xn, 'xn', KC, ntok)
                cp(sbb[:, j, 0:ntok], pb[:, 0:ntok], [pk], ['sbb'], eng='act')
            wv, wk = slab('w_in', l, 0, D, C_SC, 256)
            for j in range(2):
                pb, pk = ps[j], 'ps%d' % j
                proj_fm(pb, pk, wv, wk, 128 * j, 128, xn, 'xn', KC, ntok)
                shsrc = tmpA if j == 0 else tmpB
                tt(shv[:, j, :, 2:2 + Ls], v3(pb[:, 0:ntok]), v3(shsrc[:, 0:ntok]), ALU.mult,
                   [pk, 'tmpA' if j == 0 else 'tmpB'], ['shp'])

            for j in range(2):
                yv = v3(tmpA[:, 0:ntok]) if j == 0 else v3(tmpB[:, 0:ntok])
                yk = 'tmpA' if j == 0 else 'tmpB'
                wcol = lambda jj: cvec2[:, 96 + l * 6 + jj * 2 + j:96 + l * 6 + jj * 2 + j + 1]
                ts(yv, shv[:, j, :, 0:Ls], wcol(0), None, ALU.mult, None, ['shp', 'cvec2'], [yk])
                for jj in (1, 2):
                    stt(yv, shv[:, j, :, jj:jj + Ls], wcol(jj), yv, ALU.mult, ALU.add, ['shp', 'cvec2', yk], [yk])
                tt(brT[:, 3, j, 0:ntok], sbb[:, j, 0:ntok], (tmpA if j == 0 else tmpB)[:, 0:ntok], ALU.mult,
                   ['sbb', yk], [('brT', 3)])
            sc_state_out(l, tile, shv, Ls)
            for j in range(6):
                wcol = lambda jj: cvec2[:, l * 24 + jj * 6 + j:l * 24 + jj * 6 + j + 1]
                cb, cbk = ((tmpC, 'tmpC'), (tmpA, 'tmpA'), (tmpB, 'tmpB'))[j % 3]
                yv = v3(cb[:, 0:ntok])
                ts(yv, xpv[:, j, :, 0:Ls], wcol(0), None, ALU.mult, None, [('xp', j), 'cvec2'], [cbk])
                for jj in (1, 2, 3):
                    stt(yv, xpv[:, j, :, jj:jj + Ls], wcol(jj), yv, ALU.mult, ALU.add, [('xp', j), 'cvec2', cbk], [cbk])
                act(qkv[:, j, 0:ntok], cb[:, 0:ntok], AF.Silu, [cbk], [('qkv', j)])
            gc_state_out(l, tile, xpv, Ls)
            l2s = ((lnt, 'lnt'), (rstd, 'rstd'), (sbb[:, 0, :], 'sbb'), (sbb[:, 1, :], 'sbb'))
            for j in range(4):
                tt(sqs[:, j, 0:ntok], qkv[:, j, 0:ntok], qkv[:, j, 0:ntok], ALU.mult, [('qkv', j)], [('sqs', j)])
            for j in range(4):
                mm(ps[j][:, 0:ntok], blk64b[:], sqs[:, j, 0:ntok], True, True, ['blk64b', ('sqs', j)], ['ps%d' % j])
            for j in range(4):
                b_, k_ = l2s[j]
                act(b_[:, 0:ntok], ps[j][:, 0:ntok], AF.Ln, ['ps%d' % j], [k_], bias=eps_col[:, 0:1], scale=1.0)
            for j in range(4):
                b_, k_ = l2s[j]
                act(b_[:, 0:ntok], b_[:, 0:ntok], AF.Exp, [k_], [k_], scale=-0.5)
            for j in range(4):
                b_, k_ = l2s[j]
                if j < 2:
                    stt(qkv[:, j, 0:ntok], qkv[:, j, 0:ntok], 0.125, b_[:, 0:ntok], ALU.mult, ALU.mult,
                        [('qkv', j), k_], [('qkv', j)])
                else:
                    tt(qkv[:, j, 0:ntok], qkv[:, j, 0:ntok], b_[:, 0:ntok], ALU.mult, [('qkv', j), k_], [('qkv', j)])
            for _ in gla_chunk(l, tile, 0):
                pass

            def side_work(c_):
                if c_ + 1 < nsub:
                    for _ in gla_chunk(l, tile, c_ + 1):
                        yield
                cm_chunk(l, tile, c_)
                yield

            for c in range(nsub):
                nxt = side_work(c)
                gdn_chunk(l, tile, c, tick=((lambda g_=nxt: next(g_, None)) if mode == 'P' else None))
                for _ in nxt:
                    pass
            if tile['last']:
                state_out(l, tile)
            P.barrier()

            merge(l, tile)
            ffn(l, tile)
            ple(l, tile)

        gelu_ctr = [0]

        def gelu(out, outkey, pin, pkey, n):
            gelu_ctr[0] += 1
            if gelu_ctr[0] % 2:
                xsb, xk, t2b, tk = tmpC, 'tmpC', lnt, 'lnt'
            else:
                xsb, xk, t2b, tk = tmpA, 'tmpA', tmpB, 'tmpB'
            xs = xsb[:, 0:n]
            cp(xs, pin, [pkey], [xk], eng='act')
            t2 = t2b[:, 0:n]
            tt(t2, xs, xs, ALU.mult, [xk], [tk])
            ts(t2, t2, 0.044715, 1.0, ALU.mult, ALU.add, [tk], [tk])
            tt(t2, t2, xs, ALU.mult, [tk, xk], [tk])
            act(t2, t2, AF.Sigmoid, [tk], [tk], scale=1.5957691216057308)
            tt(out, xs, t2, ALU.mult, [xk, tk], [outkey])

        def sc_state_out(l, tile, shv, Ls):
            mode = tile['mode']
            if mode == 'P':
                cp(halo_sc[:, l, :, :], shv[:, :, 0, Ls:Ls + 2], ['shp'], ['halo_sc'])
                if not tile['last']:
                    return
                for j in range(2):
                    tr(ps[4][0:2, 128 * j:128 * j + 128], shv[:, j, 0, Ls:Ls + 2], 128, ['shp'], ['ps4'])
                cp(st_tm[0:2, 0:256], ps[4][0:2, 0:256], ['ps4'], ['ct8', 'ct9'])
                dma(o_sc_p[l], st_tm[0:2, 0:256], ['ct8', 'ct9'], [], is_out=True)
            else:
                cp(cs[0][:, 0:64].rearrange("p (j s t) -> p j s t", j=2, s=16), shv[:, :, :, Ls:Ls + 2], ['shp'], ['cs0'])
                for j in range(2):
                    tr(ps[4][0:32, 128 * j:128 * j + 128], cs[0][:, 32 * j:32 * j + 32], 128, ['cs0'], ['ps4'])
                cp(st_tm[0:32, 0:256], ps[4][0:32, 0:256], ['ps4'], ['ct8', 'ct9'])
                dma(o_sc_s[l], st_tm[0:32, 0:256], ['ct8', 'ct9'], [], is_out=True)

        def gc_state_out(l, tile, xpv, Ls):
            mode = tile['mode']
            if mode == 'P':
                cp(halo_gc[:, l, :, :], xpv[:, :, 0, Ls:Ls + 3], XPK, ['halo_gc'])
                if not tile['last']:
                    return
                for j in range(6):
                    tr(ps[4 + j // 4][0:3, 128 * (j % 4):128 * (j % 4) + 128], xpv[:, j, 0, Ls:Ls + 3], 128, XPK,
                       ['ps%d' % (4 + j // 4)])
                cp(st_tm[0:3, 0:512], ps[4][0:3, 0:512], ['ps4'], ['ct8', 'ct9'])
                cp(st_tm[0:3, 512:768], ps[5][0:3, 0:256], ['ps5'], ['ct8', 'ct9'])
                dma(o_gc_p[l], st_tm[0:3, :], ['ct8', 'ct9'], [], is_out=True)
            else:
                cp(cs[1][:, 0:144].rearrange("p (j s t) -> p j s t", j=3, s=16), xpv[:, 0:3, :, Ls:Ls + 3], XPK, ['cs1'])
                cp(cs[2][:, 0:144].rearrange("p (j s t) -> p j s t", j=3, s=16), xpv[:, 3:6, :, Ls:Ls + 3], XPK, ['cs2'])
                for j in range(6):
                    srcb = cs[1] if j < 3 else cs[2]
                    srck = 'cs1' if j < 3 else 'cs2'
                    tr(ps[4 + j // 4][0:48, 128 * (j % 4):128 * (j % 4) + 128], srcb[:, 48 * (j % 3):48 * (j % 3) + 48], 128,
                       [srck], ['ps%d' % (4 + j // 4)])
                cp(st_tm[0:48, 0:512], ps[4][0:48, 0:512], ['ps4'], ['ct8', 'ct9'])
                cp(st_tm[0:48, 512:768], ps[5][0:48, 0:256], ['ps5'], ['ct8', 'ct9'])
                dma(o_gc_s[l], st_tm[0:48, :], ['ct8', 'ct9'], [], is_out=True)

        def state_out(l, tile):
            if tile['mode'] == 'P':
                dma(o_gla_p[l], Sgla_p[:, l, :], ['Sgla_p'], [], is_out=True)
                for pair in range(2):
                    for h2 in range(2):
                        dma(o_gdn_p[l, 2 * pair + h2], Sgdn_p[64 * h2:64 * h2 + 64, l, pair, :], ['Sgdn_p'], [], is_out=True)
            else:
                dma(o_gla_s[l].rearrange("s p v -> p s v"), Sgla_s[:], ['Sgla_s'], [], is_out=True)
                for pair in range(2):
                    for h2 in range(2):
                        dma(o_gdn_s[l, :, 2 * pair + h2].rearrange("s k v -> k s v"), Sgdn_s[64 * h2:64 * h2 + 64, :, pair, :],
                            ['Sgdn_s'], [], is_out=True)

        def gla_chunk(l, tile, c):
            mode = tile['mode']
            nseq = 1 if mode == 'P' else 16
            Lq = 128 // nseq
            tok = slice(128 * c, 128 * c + 128)
            triU = C('triU' + mode)

            def s3(ap):
                return ap.rearrange("p (s t) -> p s t", s=nseq)
            gcs = [xp[:, 3 + i // 2, 256 * (i % 2):256 * (i % 2) + 256] for i in range(6)]
            gct = [xp[:, i, 0:512] for i in range(3)]

            def gk(i, b):
                return ('xp', 3 + i // 2, 2 * (i % 2) + b)
            mm(ps[0][:, 0:128], gaT[:, tok], wa2_sb[:, 128 * l:128 * l + 128], True, True, ['gaT', 'gaT1', 'wa2', 'wa2b'], ['ps0'])
            e1 = gcs[0][:, 0:128]
            act(e1, ps[0][:, 0:128], AF.Exp, ['ps0'], [gk(0, 0)], scale=-1.0)
            sp_ = gcs[0][:, 128:256]
            act(sp_, e1, AF.Ln, [gk(0, 0)], [gk(0, 1)], bias=one_col[:, 0:1], scale=1.0)
            yield
            mm(ps[1][:, 0:128], sp_, triU, True, True, [gk(0, 1), 'cst'], ['ps1'])
            bT = gcs[1][:, 0:128]
            cp(bT, ps[1][:, 0:128], ['ps1'], [gk(1, 0)])
            eb = gcs[1][:, 128:256]
            act(eb, bT, AF.Exp, [gk(1, 0)], [gk(1, 1)], scale=-1.0 / 16)
            enb = gcs[2][:, 0:128]
            act(enb, bT, AF.Exp, [gk(1, 0)], [gk(2, 0)], scale=1.0 / 16)
            yield
            dif = gcs[2][:, 128:256]
            tt(s3(dif), s3(bT), s3(bT)[:, :, Lq - 1:Lq].to_broadcast([128, nseq, Lq]), ALU.subtract, [gk(1, 0)], [gk(2, 1)])
            act(dif, dif, AF.Exp, [gk(2, 1)], [gk(2, 1)], scale=1.0 / 16)
            dec = sm[1][:, 0:nseq]
            act(dec, s3(bT)[:, :, Lq - 1], AF.Exp, [gk(1, 0)], ['sm1'], scale=-1.0 / 16)
            yield
            qd = gcs[3][:, 0:128]
            stt(qd, qT[:, tok], float(32 ** -0.5), eb, ALU.mult, ALU.mult, ['qT', gk(1, 1)], [gk(3, 0)])
            kd = gcs[3][:, 128:256]
            tt(kd, kT[:, tok], enb, ALU.mult, ['kT', gk(2, 0)], [gk(3, 1)])
            ke = gcs[4][:, 0:128]
            tt(ke, kT[:, tok], dif, ALU.mult, ['kT', gk(2, 1)], [gk(4, 0)])
            qdm = gct[0]
            tt(qdm[:, :].rearrange("p (h t) -> p h t", h=4), qd.unsqueeze(1).to_broadcast([128, 4, 128]),
               C('headmask')[:, 0:4].unsqueeze(2).to_broadcast([128, 4, 128]), ALU.mult, [gk(3, 0), 'cst'], [('xp', 0)])
            yield
            for h in range(4):
                mm(ps[2][:, 128 * h:128 * h + 128], kd, qdm[:, 128 * h:128 * h + 128], True, True, [gk(3, 1), ('xp', 0)], ['ps2'])
            AT = gct[1]
            tt(AT[:, :].rearrange("p (h t) -> p h t", h=4), ps[2][:, :].rearrange("p (h t) -> p h t", h=4),
               triU.unsqueeze(1).to_broadcast([128, 4, 128]), ALU.mult, ['ps2', 'cst'], [('xp', 1)])
            yield
            tr(ps[3][:, 0:128], ke, 128, [gk(4, 0)], ['ps3'])
            kem = gct[2]
            tt(kem[:, :].rearrange("p (h t) -> p h t", h=4), ps[3][:, 0:128].unsqueeze(1).to_broadcast([128, 4, 128]),
               C('hmfree').rearrange("p (h t) -> p h t", h=4), ALU.mult, ['ps3', 'cst'], [('xp', 2)])
            yield
            Sk = 'Sgla_p' if mode == 'P' else 'Sgla_s'
            for h in range(4):
                h2, pair = h % 2, h // 2
                ob = ps[0][64 * h2:64 * h2 + 64, 128 * pair:128 * pair + 128]
                if mode == 'P':
                    mm(ob, gv_tm[:, c, 64 * h:64 * h + 64], AT[:, 128 * h:128 * h + 128], True, False, ['gv_tm', ('xp', 1)], ['ps0'])
                    mm(ob, Sgla_p[:, l, :], qdm[:, 128 * h:128 * h + 128], False, True, [Sk, ('xp', 0)], ['ps0'])
                else:
                    mm(ob, gv_tm[:, c, 64 * h:64 * h + 64], AT[:, 128 * h:128 * h + 128], True, False, ['gv_tm', ('xp', 1)], ['ps0'])
                    for s_ in range(16):
                        mm(ps[0][64 * h2:64 * h2 + 64, 128 * pair + 8 * s_:128 * pair + 8 * s_ + 8], Sgla_s[:, s_, :],
                           qdm[:, 128 * h + 8 * s_:128 * h + 8 * s_ + 8], False, s_ == 15, [Sk, ('xp', 0)], ['ps0'])
            yield
            if mode == 'P':
                for h in range(4):
                    mm(ps[1][:, 0:64], kem[:, 128 * h:128 * h + 128], gv_tm[:, c, 64 * h:64 * h + 64], h == 0, h == 3,
                       [('xp', 2), 'gv_tm'], ['ps1'])
                stt(Sgla_p[:, l, :], Sgla_p[:, l, :], dec[:, 0:1], ps[1][:, 0:64], ALU.mult, ALU.add,
                    [Sk, 'sm1', 'ps1'], [Sk])
            else:
                for half in range(2):
                    tt(um[:, :, :], gv_tm[:, c, :].unsqueeze(1).to_broadcast([128, 8, 256]),
                       C('seqmask')[:, 8 * half:8 * half + 8].unsqueeze(2).to_broadcast([128, 8, 256]), ALU.mult,
                       ['gv_tm', 'cst'], UMK)
                    for h in range(4):
                        mm(ps[1][:, :].rearrange("p (s v) -> p s v", s=8), kem[:, 128 * h:128 * h + 128],
                           um[:, :, 64 * h:64 * h + 64], h == 0, h == 3, [('xp', 2)] + UMK, ['ps1'])
                    sl = slice(8 * half, 8 * half + 8)
                    tt(Sgla_s[:, sl, :], Sgla_s[:, sl, :], dec[:, sl].unsqueeze(2).to_broadcast([128, 8, 64]), ALU.mult,
                       [Sk, 'sm1'], [Sk])
                    tt(Sgla_s[:, sl, :], Sgla_s[:, sl, :], ps[1][:, :].rearrange("p (s v) -> p s v", s=8), ALU.add,
                       [Sk, 'ps1'], [Sk])
            yield
            head_norm_gate(ps[0], 'ps0', cvec3[:, l:l + 1], grs, 'grs', 0, tok,
                           gcs[5], [gk(5, 0), gk(5, 1)], tmpA[:, 0:256], ['tmpA'], ps[3], 'ps3')
            yield

        one_col = sb("one_col", [128, 4])

        def head_norm_gate(pso, pskey, gaincol, gate, gatekey, bidx, tok, o_sb=None, ok=None, sq=None, sk=None,
                           psq=None, psqk=None):
            if o_sb is None:
                o_sb, ok, sq, sk, psq, psqk = cs[5], ['cs5'], cs[6], ['cs6'], ps[7], 'ps7'
            cp(o_sb[:, :], pso[:, 0:256], [pskey], ok, eng='act')
            tt(sq[:, :], o_sb[:, :], o_sb[:, :], ALU.mult, ok, sk)
            mm(psq[:, 0:256], C('blk64'), sq[:, :], True, True, ['cst'] + sk, [psqk])
            act(sq[:, :], psq[:, 0:256], AF.Ln, [psqk], sk, bias=eps_col[:, 0:1], scale=1.0 / 64)
            act(sq[:, :], sq[:, :], AF.Exp, sk, sk, scale=-0.5)
            stt(o_sb[:, :], o_sb[:, :], gaincol, sq[:, :], ALU.mult, ALU.mult, ok + sk + ['cvec3'], ok)
            tt(brT[:, bidx, :, tok], o_sb[:, :].rearrange("p (j t) -> p j t", j=2), gate[:, :, tok], ALU.mult,
               ok + [gatekey], [('brT', bidx)])

        def gdn_chunk(l, tile, c, tick=None):
            mode = tile['mode']
            nseq = 1 if mode == 'P' else 16
            Lq = 128 // nseq
            tok = slice(128 * c, 128 * c + 128)
            triU = C('triU' + mode)
            H4 = lambda ap: ap.rearrange("p (h t) -> p h t", h=4)
            for j in range(2):
                tr(ps[4][:, 128 * j:128 * j + 128], qkv[:, 2 + j, tok], 128, [('qkv', 2 + j)], ['ps4'])
                tr(ps[4][:, 256 + 128 * j:256 + 128 * j + 128], qkv[:, 4 + j, tok], 128, [('qkv', 4 + j)], ['ps4'])
            k_tm = cs[0]
            v_tm = cs[1]
            cp(k_tm[:, :], ps[4][:, 0:256], ['ps4'], ['cs0'], eng='act')
            cp(v_tm[:, :], ps[4][:, 256:512], ['ps4'], ['cs1'])
            g_ = sm[2]
            tt(g_[:, 0:4], ab_tm[:, c, 0:4], dtb_b[:, 4 * l:4 * l + 4], ALU.add, ['ab_tm', 'dtb'], ['sm2'])
            act(g_[:, 0:4], g_[:, 0:4], AF.Exp, ['sm2'], ['sm2'])
            act(g_[:, 0:4], g_[:, 0:4], AF.Ln, ['sm2'], ['sm2'], bias=one_col[:, 0:1], scale=1.0)
            tt(g_[:, 0:4], g_[:, 0:4], nega_b[:, 4 * l:4 * l + 4], ALU.mult, ['sm2', 'nega'], ['sm2'])
            be = sm[3]
            act(be[:, 0:4], ab_tm[:, c, 4:8], AF.Exp, ['ab_tm'], ['sm3'], scale=-1.0)
            ts(be[:, 0:4], be[:, 0:4], 1.0, None, ALU.add, None, ['sm3'], ['sm3'])
            recip(be[:, 0:4], be[:, 0:4], ['sm3'], ['sm3'])
            gbc = ct[0]
            cp(H4(gbc[:, :]), g_[:, 0:4].unsqueeze(2).to_broadcast([128, 4, 128]), ['sm2'], ['ct0'])
            for h in range(4):
                mm(ps[5][:, 128 * h:128 * h + 128], gbc[:, 128 * h:128 * h + 128], triU, True, True, ['ct0', 'cst'], ['ps5'])
            mm(ps[6][:, 0:4], triU, g_[:, 0:4], True, True, ['cst', 'sm2'], ['ps6'])
            mm(ps[6][:, 4:8], C('tot' + mode), g_[:, 0:4], True, True, ['cst', 'sm2'], ['ps6'])
            Gc = sm[4]
            cp(Gc[:, 0:8], ps[6][:, 0:8], ['ps6'], ['sm4'])
            Gb = ct[1]
            cp(Gb[:, :], ps[5][:, :], ['ps5'], ['ct1'], eng='act')
            d_ = ct[2]
            tt(H4(d_[:, :]), H4(Gb[:, :]), Gc[:, 0:4].unsqueeze(2).to_broadcast([128, 4, 128]), ALU.subtract, ['ct1', 'sm4'], ['ct2'])
            dL = ct[3]
            tt(H4(dL[:, :]), H4(d_[:, :]), C('bigL' + mode).unsqueeze(1).to_broadcast([128, 4, 128]), ALU.add, ['ct2', 'cst'], ['ct3'])
            act(dL[:, :], dL[:, :], AF.Exp, ['ct3'], ['ct3'], scale=-1.0)
            dU = ct[4]
            tt(H4(dU[:, :]), H4(d_[:, :]), C('bigU' + mode).unsqueeze(1).to_broadcast([128, 4, 128]), ALU.subtract, ['ct2', 'cst'], ['ct4'])
            act(dU[:, :], dU[:, :], AF.Exp, ['ct4'], ['ct4'])
            eG = sm[5]
            act(eG[:, 0:4], Gc[:, 0:4], AF.Exp, ['sm4'], ['sm5'])
            bw = sm[6]
            tt(bw[:, 0:4], eG[:, 0:4], be[:, 0:4], ALU.mult, ['sm5', 'sm3'], ['sm6'])
            ts(bw[:, 4:8], bw[:, 0:4], -1.0, None, ALU.mult, None, ['sm6'], ['sm6'])
            ek = sm[7]
            tt(ek[:, 0:4], Gc[:, 4:8], Gc[:, 0:4], ALU.subtract, ['sm4'], ['sm7'])
            act(ek[:, 0:4], ek[:, 0:4], AF.Exp, ['sm7'], ['sm7'])
            hm2 = C('blk64').rearrange("p (a b) -> p a b", a=2)[:, :, 0]
            kmask, qmask = ct[0], ct[2]
            for pair in range(2):
                tt(H4(kmask[:, :])[:, 2 * pair:2 * pair + 2, :], qkv[:, 2 + pair, tok].unsqueeze(1).to_broadcast([128, 2, 128]),
                   hm2.unsqueeze(2).to_broadcast([128, 2, 128]), ALU.mult, [('qkv', 2 + pair), 'cst'], ['ct0'])
                tt(H4(qmask[:, :])[:, 2 * pair:2 * pair + 2, :], qkv[:, pair, tok].unsqueeze(1).to_broadcast([128, 2, 128]),
                   hm2.unsqueeze(2).to_broadcast([128, 2, 128]), ALU.mult, [('qkv', pair), 'cst'], ['ct2'])
            for h in range(4):
                h2, pair = h % 2, h // 2
                mm(ps[6][:, 128 * h:128 * h + 128], qkv[:, 2 + pair, tok], kmask[:, 128 * h:128 * h + 128], True, True,
                   [('qkv', 2 + pair), 'ct0'], ['ps6'])
                mm(ps[7][:, 128 * h:128 * h + 128], qkv[:, 2 + pair, tok], qmask[:, 128 * h:128 * h + 128], True, True,
                   [('qkv', 2 + pair), 'ct2'], ['ps7'])
            Nm = ct[5]
            tt(H4(Nm[:, :]), H4(dL[:, :]), C('strictL' + mode).unsqueeze(1).to_broadcast([128, 4, 128]), ALU.mult, ['ct3', 'cst'], ['ct5'])
            tt(Nm[:, :], Nm[:, :], ps[6][:, :], ALU.mult, ['ct5', 'ps6'], ['ct5'])
            tt(H4(Nm[:, :]), H4(Nm[:, :]), be[:, 0:4].unsqueeze(2).to_broadcast([128, 4, 128]), ALU.mult, ['ct5', 'sm3'], ['ct5'])
            qkT = ct[6]
            tt(qkT[:, :], dU[:, :], ps[7][:, :], ALU.mult, ['ct4', 'ps7'], ['ct6'])
            RU = cs[2]
            tt(RU[:, :].rearrange("p (h v) -> p h v", h=4), v_tm[:, :].rearrange("p (h v) -> p h v", h=4),
               be[:, 0:4].unsqueeze(2).to_broadcast([128, 4, 64]), ALU.mult, ['cs1', 'sm3'], ['cs2'])
            RW = cs[3]
            tt(RW[:, :].rearrange("p (h v) -> p h v", h=4), k_tm[:, :].rearrange("p (h v) -> p h v", h=4),
               bw[:, 4:8].unsqueeze(2).to_broadcast([128, 4, 64]), ALU.mult, ['cs0', 'sm6'], ['cs3'])
            kend = cs[4]
            tt(kend[:, :].rearrange("p (h v) -> p h v", h=4), k_tm[:, :].rearrange("p (h v) -> p h v", h=4),
               ek[:, 0:4].unsqueeze(2).to_broadcast([128, 4, 64]), ALU.mult, ['cs0', 'sm7'], ['cs4'])
            for h in range(4):
                tr(ps[5][:, 128 * h:128 * h + 128], Nm[:, 128 * h:128 * h + 128], 128, ['ct5'], ['ps5'])
            NT = ct[7]
            cp(NT[:, :], ps[5][:, :], ['ps5'], ['ct7'], eng='act')
            PT = ct[8]
            ts(PT[:, :], NT[:, :], -1.0, None, ALU.mult, None, ['ct7'], ['ct8'])
            tt(H4(PT[:, :]), H4(PT[:, :]), C('ident').unsqueeze(1).to_broadcast([128, 4, 128]), ALU.add, ['ct8', 'cst'], ['ct8'])
            A_, AT_, B_, BT_ = Nm, NT, ct[9], ct[10]
            Ak, ATk, Bk, BTk = 'ct5', 'ct7', 'ct9', 'ct10'
            nlev = 6 if mode == 'P' else 2
            psets = ((ps[5], 'ps5', ps[6], 'ps6', ps[7], 'ps7'), (ps[5], 'ps5', ps[6], 'ps6', ps[7], 'ps7'))
            for lev in range(nlev):
                last = (lev == nlev - 1)
                for hf in range(2):
                    pB, pBk, pBT, pBTk, pP, pPk = psets[hf]
                    hc = slice(256 * hf, 256 * hf + 256)
                    for h in (2 * hf, 2 * hf + 1):
                        hs = slice(128 * h, 128 * h + 128)
                        mm(pB[:, hs], AT_[:, hs], A_[:, hs], True, True, [(Ak, hf), (ATk, hf)], [pBk])
                        if not last:
                            mm(pBT[:, hs], A_[:, hs], AT_[:, hs], True, True, [(Ak, hf), (ATk, hf)], [pBTk])
                    cp(B_[:, hc], pB[:, hc], [pBk], [(Bk, hf)], eng='act')
                    if not last:
                        cp(BT_[:, hc], pBT[:, hc], [pBTk], [(BTk, hf)])
                    for h in (2 * hf, 2 * hf + 1):
                        hs = slice(128 * h, 128 * h + 128)
                        mm(pP[:, hs], B_[:, hs], PT[:, hs], True, True, [(Bk, hf), ('ct8', hf)], [pPk])
                    tt(PT[:, hc], PT[:, hc], pP[:, hc], ALU.add, [('ct8', hf), pPk], [('ct8', hf)])
                    if tick is not None:
                        tick()
                A_, AT_, B_, BT_ = B_, BT_, A_, AT_
                Ak, ATk, Bk, BTk = Bk, BTk, Ak, ATk
            if tick is not None:
                tick()
            for h in range(4):
                h2, pair = h % 2, h // 2
                mm(ps[5][64 * h2:64 * h2 + 64, 128 * pair:128 * pair + 128], RW[:, 64 * h:64 * h + 64],
                   PT[:, 128 * h:128 * h + 128], True, True, ['cs3', 'ct8'], ['ps5'])
            nWT = cs[5]
            cp(nWT[:, :], ps[5][:, 0:256], ['ps5'], ['cs5'], eng='act')
            eGb = ct[9]
            act(eGb[:, :], Gb[:, :], AF.Exp, ['ct1'], ['ct9'])
            qg = cs[6]
            for pair in range(2):
                for h2 in range(2):
                    h = 2 * pair + h2
                    tt(qg[64 * h2:64 * h2 + 64, 128 * pair:128 * pair + 128], qkv[64 * h2:64 * h2 + 64, pair, tok],
                       eGb[64 * h2:64 * h2 + 64, 128 * h:128 * h + 128], ALU.mult, [('qkv', pair), 'ct9'], ['cs6'])
            nWTm, qgm = ct[5], ct[7]
            tt(nWTm[:, :].rearrange("p (a x) -> p a x", a=2), nWT[:, :].unsqueeze(1).to_broadcast([128, 2, 256]),
               hm2.unsqueeze(2).to_broadcast([128, 2, 256]), ALU.mult, ['cs5', 'cst'], ['ct5'])
            tt(qgm[:, :].rearrange("p (a x) -> p a x", a=2), qg[:, :].unsqueeze(1).to_broadcast([128, 2, 256]),
               hm2.unsqueeze(2).to_broadcast([128, 2, 256]), ALU.mult, ['cs6', 'cst'], ['ct7'])
            u_sb = cs[7]
            Sk = 'Sgdn_p' if mode == 'P' else 'Sgdn_s'
            if mode == 'P':
                for h in range(4):
                    h2, pair = h % 2, h // 2
                    hp = slice(64 * h2, 64 * h2 + 64)
                    mm(ps[6][:, 64 * h:64 * h + 64], PT[:, 128 * h:128 * h + 128], RU[:, 64 * h:64 * h + 64], True, False,
                       ['ct8', 'cs2'], ['ps6'])
                    mm(ps[6][:, 64 * h:64 * h + 64], nWTm[:, 256 * h2 + 128 * pair:256 * h2 + 128 * pair + 128], Sgdn_p[:, l, pair, :],
                       False, True, ['ct5', Sk], ['ps6'])
                cp(u_sb[:, :], ps[6][:, 0:256], ['ps6'], ['cs7'])
                if tick is not None:
                    tick()
                for h in range(4):
                    h2, pair = h % 2, h // 2
                    hp = slice(64 * h2, 64 * h2 + 64)
                    ob = ps[4][hp, 128 * pair:128 * pair + 128]
                    mm(ob, Sgdn_p[:, l, pair, :], qgm[:, 256 * h2 + 128 * pair:256 * h2 + 128 * pair + 128], True, False, [Sk, 'ct7'], ['ps4'])
                    mm(ob, u_sb[:, 64 * h:64 * h + 64], qkT[:, 128 * h:128 * h + 128], False, True, ['cs7', 'ct6'], ['ps4'])
                for h in range(4):
                    h2, pair = h % 2, h // 2
                    hp = slice(64 * h2, 64 * h2 + 64)
                    mm(ps[7][hp, 64 * pair:64 * pair + 64], kend[:, 64 * h:64 * h + 64], u_sb[:, 64 * h:64 * h + 64], True, True,
                       ['cs4', 'cs7'], ['ps7'])
                for pair in range(2):
                    for h2 in range(2):
                        h = 2 * pair + h2
                        hp = slice(64 * h2, 64 * h2 + 64)
                        stt(Sgdn_p[hp, l, pair, :], Sgdn_p[hp, l, pair, :], eGb[hp, 128 * h + 127:128 * h + 128],
                            ps[7][hp, 64 * pair:64 * pair + 64], ALU.mult, ALU.add, [Sk, 'ct9', 'ps7'], [Sk])
            else:
                for h in range(4):
                    h2, pair = h % 2, h // 2
                    hp = slice(64 * h2, 64 * h2 + 64)
                    for s_ in range(16):
                        o_ = 256 * h2 + 128 * pair + 8 * s_
                        mm(ps[6][hp, 128 * pair + 8 * s_:128 * pair + 8 * s_ + 8], Sgdn_s[:, s_, pair, :],
                           nWTm[:, o_:o_ + 8], True, True, [Sk, 'ct5'], ['ps6'])
                cp(ct[10][:, 0:256], ps[6][:, 0:256], ['ps6'], ['ct10'])
                for pair in range(2):
                    tr(ps[7][:, 128 * pair:128 * pair + 128], ct[10][:, 128 * pair:128 * pair + 128], 128, ['ct10'], ['ps7'])
                for h in range(4):
                    mm(ps[6][:, 256 + 64 * h:256 + 64 * h + 64], PT[:, 128 * h:128 * h + 128], RU[:, 64 * h:64 * h + 64], True, True,
                       ['ct8', 'cs2'], ['ps6'])
                cp(u_sb[:, :], ps[6][:, 256:512], ['ps6'], ['cs7'])
                tt(u_sb[:, :], u_sb[:, :], ps[7][:, 0:256], ALU.add, ['cs7', 'ps7'], ['cs7'])
                for h in range(4):
                    h2, pair = h % 2, h // 2
                    hp = slice(64 * h2, 64 * h2 + 64)
                    ob = ps[4][hp, 128 * pair:128 * pair + 128]
                    mm(ob, u_sb[:, 64 * h:64 * h + 64], qkT[:, 128 * h:128 * h + 128], True, False, ['cs7', 'ct6'], ['ps4'])
                    for s_ in range(16):
                        o_ = 256 * h2 + 128 * pair + 8 * s_
                        mm(ps[4][hp, 128 * pair + 8 * s_:128 * pair + 8 * s_ + 8], Sgdn_s[:, s_, pair, :],
                           qgm[:, o_:o_ + 8], False, s_ == 15, [Sk, 'ct7'], ['ps4'])
                for half in range(2):
                    sl = slice(8 * half, 8 * half + 8)
                    tt(um[:, :, :], u_sb[:, :].unsqueeze(1).to_broadcast([128, 8, 256]),
                       C('seqmask')[:, 8 * half:8 * half + 8].unsqueeze(2).to_broadcast([128, 8, 256]), ALU.mult,
                       ['cs7', 'cst'], UMK)
                    for pair in range(2):
                        pb, pk = (ps[5], 'ps5') if pair == 0 else (ps[7], 'ps7')
                        for h2 in range(2):
                            h = 2 * pair + h2
                            hp = slice(64 * h2, 64 * h2 + 64)
                            mm(pb[hp, :].rearrange("p (s v) -> p s v", s=8), kend[:, 64 * h:64 * h + 64],
                               um[:, :, 64 * h:64 * h + 64], True, True, ['cs4'] + UMK, [pk])
                        for h2 in range(2):
                            h = 2 * pair + h2
                            hp = slice(64 * h2, 64 * h2 + 64)
                            dnv = eGb[hp, 128 * h:128 * h + 128].rearrange("p (s t) -> p s t", s=16)[:, sl, 7:8]
                            tt(Sgdn_s[hp, sl, pair, :], Sgdn_s[hp, sl, pair, :], dnv.to_broadcast([64, 8, 64]), ALU.mult,
                               [Sk, 'ct9'], [Sk])
                            tt(Sgdn_s[hp, sl, pair, :], Sgdn_s[hp, sl, pair, :], pb[hp, :].rearrange("p (s v) -> p s v", s=8),
                               ALU.add, [Sk, pk], [Sk])
            head_norm_gate(ps[4], 'ps4', cvec3[:, 8 + l:8 + l + 1], dzs, 'dzs', 1, tok)

        def cm_chunk(l, tile, c):
            tok = slice(128 * c, 128 * c + 128)
            for g in range(4):
                h2, pair = g % 2, g // 2
                mm(ps[1][64 * h2:64 * h2 + 64, 128 * pair:128 * pair + 128], vn_tm[:, c, 64 * g:64 * g + 64], wmT[:, g, :], True, True,
                   ['vn_tm', 'wmT'], ['ps1'])
            s_sb = tmpB[:, 0:256]
            P.op('dve', lambda e: e.tensor_tensor(out=s_sb, in0=ps[1][:, 0:256],
                                                  in1=bsT[:, :, :].rearrange("p j t -> p (j t)"), op=ALU.add),
                 ['ps1', 'bsT'], ['tmpB'], strict=True)
            tt(brT[:, 2, :, tok], cug[:, :, tok], s_sb.rearrange("p (j t) -> p j t", j=2), ALU.mult, ['cug', 'tmpB'],
               [('brT', 2)])

        def merge(l, tile):
            ntok = tile['ntok']
            for g in range(4):
                wb_lo, wbk_lo = slab_wbranch(l, g, 0)
                wg0, wgk0 = slab('w_gate', l, 0, D, 1024 * g, 512)
                half_groups(l, g, 0, wb_lo, wbk_lo, wg0, wgk0, ntok)
                wb_hi, wbk_hi = slab_wbranch(l, g, 1)
                wg1, wgk1 = slab('w_gate', l, 0, D, 1024 * g + 512, 512)
                half_groups(l, g, 1, wb_hi, wbk_hi, wg1, wgk1, ntok)
            for og in range(KC):
                cp(sqs[:, og, 0:ntok], ct[og][:, 0:ntok], ['ct%d' % og], [('sqs', og)], eng=('act' if og % 2 else 'dve'))
            for half in range(2):
                wv, wk = slab('w_o', l, 0, D, 512 * half, 512)
                for j in range(4):
                    og = 4 * half + j
                    pb, pk = ps[og % 2], 'ps%d' % (og % 2)
                    proj_fm(pb, pk, wv, wk, 128 * j, 128, sqs, 'sqs', KC, ntok)
                    tt(hT[:, og, 0:ntok], hT[:, og, 0:ntok], pb[:, 0:ntok], ALU.add, [('h', og), pk], [('h', og)])
                    sq_h(og, ntok, 'xn')

        def slab_wbranch(l, g, half):
            spec_rows = 256
            v, k = slab_rows('w_branch', (l, g), spec_rows, 512 * half, 512)
            return v, k

        def slab_rows(wname, idx, nrows, c0, ncols):
            spec = (wname, idx, 0, nrows, c0, ncols)
            return slab(*spec)

        def half_groups(l, g, half, wb, wbk, wg, wgk, ntok):
            for j in range(4):
                og = 4 * half + j
                pg, pgk = ps[0 + 2 * (j % 2)], 'ps%d' % (0 + 2 * (j % 2))
                pbr, pbk = ps[1 + 2 * (j % 2)], 'ps%d' % (1 + 2 * (j % 2))
                proj_fm(pg, pgk, wg, wgk, 128 * j, 128, xn, 'xn', KC, ntok)
                for kc in range(2):
                    mm(pbr[:, 0:ntok], wb[:, kc, 128 * j:128 * j + 128], brT[:, g, kc, 0:ntok], kc == 0, kc == 1,
                       [rk(wbk, kc), ('brT', g)], [pbk])
                sg = tmpA if j % 2 == 0 else tmpB
                sgk = 'tmpA' if j % 2 == 0 else 'tmpB'
                act(sg[:, 0:ntok], pg[:, 0:ntok], AF.Sigmoid, [pgk], [sgk])
                if g == 0:
                    tt(ct[og][:, 0:ntok], sg[:, 0:ntok], pbr[:, 0:ntok], ALU.mult, [sgk, pbk], ['ct%d' % og])
                else:
                    tt(sg[:, 0:ntok], sg[:, 0:ntok], pbr[:, 0:ntok], ALU.mult, [sgk, pbk], [sgk])
                    tt(ct[og][:, 0:ntok], ct[og][:, 0:ntok], sg[:, 0:ntok], ALU.add, ['ct%d' % og, sgk], ['ct%d' % og])

        def ffn(l, tile):
            ntok = tile['ntok']
            rmsnorm_to_xn(ntok, lambda kc: gcol('norm_ffn', l, kc), 'xn')
            nslab = (DFF + 511) // 512
            for s_ in range(nslab):
                c0 = 512 * s_
                ncols = min(512, DFF - c0)
                wgv, wgk = slab('w_ffn_gate', l, 0, D, c0, ncols)
                wuv, wuk = slab('w_ffn_up', l, 0, D, c0, ncols)
                for j in range(ncols // 128):
                    fg = 4 * s_ + j
                    pg, pgk = ps[0 + 2 * (j % 2)], 'ps%d' % (0 + 2 * (j % 2))
                    pu, puk = ps[1 + 2 * (j % 2)], 'ps%d' % (1 + 2 * (j % 2))
                    proj_fm(pg, pgk, wgv, wgk, 128 * j, 128, xn, 'xn', KC, ntok)
                    proj_fm(pu, puk, wuv, wuk, 128 * j, 128, xn, 'xn', KC, ntok)
                    sg = tmpA if j % 2 == 0 else tmpB
                    sgk = 'tmpA' if j % 2 == 0 else 'tmpB'
                    act(sg[:, 0:ntok], pg[:, 0:ntok], AF.Silu, [pgk], [sgk])
                    tt(hid[:, fg, 0:ntok], sg[:, 0:ntok], pu[:, 0:ntok], ALU.mult, [sgk, puk], ['ct%d' % (fg // 2)])
            for og in range(KC):
                wv, wk = slab('w_ffn_down', l, 0, DFF, 128 * og, 128)
                pb, pk = ps[og % 2], 'ps%d' % (og % 2)
                for fc in range(FC):
                    mm(pb[:, 0:ntok], wv[:, fc, :], hid[:, fc, 0:ntok], fc == 0, fc == FC - 1, [rk(wk, fc), 'ct%d' % (fc // 2)], [pk])
                tt(hT[:, og, 0:ntok], hT[:, og, 0:ntok], pb[:, 0:ntok], ALU.add, [('h', og), pk], [('h', og)])
                sq_h(og, ntok, 'sqs')

        def ple(l, tile):
            ntok = tile['ntok']
            rmsnorm_to_xn(ntok, lambda kc: gcol('norm_ple', l, kc))
            for half in range(2):
                wgv, wgk = slab('w_ple_gate', l, 0, D, 512 * half, 512)
                wpv, wpk = slab('w_ple', l, 0, 256, 512 * half, 512)
                for j in range(4):
                    og = 4 * half + j
                    pg, pgk = ps[0 + 2 * (j % 2)], 'ps%d' % (0 + 2 * (j % 2))
                    pp, ppk = ps[1 + 2 * (j % 2)], 'ps%d' % (1 + 2 * (j % 2))
                    proj_fm(pg, pgk, wgv, wgk, 128 * j, 128, xn, 'xn', KC, ntok)
                    for kc in range(2):
                        mm(pp[:, 0:ntok], wpv[:, kc, 128 * j:128 * j + 128], peT[:, kc, 0:ntok], kc == 0, kc == 1,
                           [rk(wpk, kc), 'peT'], [ppk])
                    sg = tmpA if j % 2 == 0 else tmpB
                    sgk = 'tmpA' if j % 2 == 0 else 'tmpB'
                    act(sg[:, 0:ntok], pg[:, 0:ntok], AF.Sigmoid, [pgk], [sgk])
                    tt(sg[:, 0:ntok], sg[:, 0:ntok], pp[:, 0:ntok], ALU.mult, [sgk, ppk], [sgk])
                    tt(hT[:, og, 0:ntok], hT[:, og, 0:ntok], sg[:, 0:ntok], ALU.add, [('h', og), sgk], [('h', og)])
                    sq_h(og, ntok, 'sqs')

        def load_tile(tile):
            ntok = tile['ntok']
            src = x_p[tile['t0']:tile['t0'] + ntok, :] if tile['mode'] == 'P' else x_s
            for s_ in range(ntok // 128):
                dma(x_tm[:], src[128 * s_:128 * s_ + 128, :], [], ['tmpA', 'tmpB'])
                for half in range(2):
                    pb, pk = ps[4 + half], 'ps%d' % (4 + half)
                    for j in range(4):
                        kc = 4 * half + j
                        tr(pb[:, 128 * j:128 * j + 128], x_tm[:, 128 * kc:128 * kc + 128], 128, ['tmpA', 'tmpB'], [pk])
                    cp(hT[:, 4 * half:4 * half + 4, 128 * s_:128 * s_ + 128], pb[:, :].rearrange("p (j t) -> p j t", j=4),
                       [pk], [('h', 4 * half + j) for j in range(4)], eng=('act' if half else 'dve'))
            for kc in range(KC):
                sq_h(kc, ntok, 'sqs')

        def store_tile(tile):
            ntok = tile['ntok']
            dst = y_p[tile['t0']:tile['t0'] + ntok, :] if tile['mode'] == 'P' else y_s
            for kc in range(KC):
                mm(ps[0][:, 0:ntok], onesb[:], sqs[:, kc, 0:ntok], kc == 0, kc == KC - 1, ['onesb', ('sqs', kc)], ['ps0'])
            act(lnt[:, 0:ntok], ps[0][:, 0:ntok], AF.Ln, ['ps0'], ['lnt'], bias=eps_col[:, 0:1], scale=1.0 / D)
            act(rstd[:, 0:ntok], lnt[:, 0:ntok], AF.Exp, ['lnt'], ['rstd'], scale=-0.5)
            for kc in range(KC):
                stt(hT[:, kc, 0:ntok], hT[:, kc, 0:ntok], cvec[:, 96 + kc:96 + kc + 1], rstd[:, 0:ntok], ALU.mult, ALU.mult,
                    [('h', kc), 'rstd', 'cvec'], [('h', kc)])
            for s_ in range(ntok // 128):
                for half in range(2):
                    pb, pk = ps[4 + half], 'ps%d' % (4 + half)
                    for j in range(4):
                        kc = 4 * half + j
                        tr(pb[:, 128 * j:128 * j + 128], hT[:, kc, 128 * s_:128 * s_ + 128], 128, [('h', kc)], [pk])
                    cp(x_tm[:, 512 * half:512 * half + 512], pb[:, :], [pk], ['tmpA', 'tmpB'], eng=('act' if half else 'dve'))
                dma(dst[128 * s_:128 * s_ + 128, :], x_tm[:], ['tmpA', 'tmpB'], [], is_out=True)

        tiles = []
        for t in range(NT):
            tiles.append(dict(mode='P', ntok=TT, nseq=1, t0=t * TT, last=(t == NT - 1)))
        if do_sample:
            tiles.append(dict(mode='S', ntok=128, nseq=16, t0=0, last=True))

        def program():
            for tile in tiles:
                load_tile(tile)
                for l in range(L):
                    layer(l, tile)
                store_tile(tile)

        P.planning = True
        program()
        P.planning = False
        memset(eps_col[:], EPS, ['eps_col'])
        memset(one_col[:], 1.0, ['one_col'])
        try:
            setup()
            program()
        except StopBuild:
            pass
        P.finish()
        block = es.enter_context(nc.Block())
        P.emit(block)
    return nc


_NC_CACHE = {}


def make_in_maps(inputs, TP, DEPTH, ncores):
    f = lambda a: np.ascontiguousarray(np.asarray(a, dtype=np.float32))
    wnames = ['norm_mix', 'w_in', 'gla_wa2', 'gla_ba', 'gla_norm', 'gdn_conv_w', 'gdn_a_log', 'gdn_dt_bias', 'gdn_norm',
              'cm_ln_g', 'cm_ln_b', 'cm_ws', 'cm_bs', 'sc_conv_w', 'w_gate', 'w_branch', 'w_o', 'norm_ffn', 'w_ffn_gate',
              'w_ffn_up', 'w_ffn_down', 'norm_ple', 'w_ple_gate', 'w_ple', 'norm_final']
    wd = {k: f(inputs[k]) for k in wnames}
    maps = []
    for b in range(ncores):
        sl = slice(16 * b, 16 * b + 16)
        m = dict(wd)
        m['x_p'] = f(inputs['x_prompt'][b])
        m['x_s'] = f(inputs['x_sample'][sl]).reshape(128, D)
        m['st_gla'] = f(inputs['state_gla'][:, sl]).reshape(DEPTH, 16, 128, 64)
        m['st_gdn'] = f(inputs['state_gdn'][:, sl])
        m['st_gc'] = f(inputs['state_gdn_conv'][:, sl]).reshape(DEPTH, 48, 768)
        m['st_sc'] = f(inputs['state_sconv'][:, sl]).reshape(DEPTH, 32, 256)
        m['p_p'] = f(inputs['p_prompt'][:, b])
        m['p_s'] = f(inputs['p_sample'][:, sl]).reshape(DEPTH, 128, 256)
        m['consts'] = CONST_ARR
        maps.append(m)
    return maps


def gather(results, TP, DEPTH, ncores):
    R = results
    cat = lambda k, ax: np.concatenate([np.asarray(r[k], dtype=np.float32) for r in R], axis=ax)
    y_prompt = np.stack([np.asarray(r['y_p'], np.float32) for r in R], 0)
    y_sample = cat('y_s', 0).reshape(ncores * 16, 8, D)
    gla_p = np.stack([np.asarray(r['o_gla_p'], np.float32).reshape(DEPTH, 4, 32, 64) for r in R], 1)
    gla_s = np.concatenate([np.asarray(r['o_gla_s'], np.float32).reshape(DEPTH, 16, 4, 32, 64) for r in R], 1)
    gdn_p = np.stack([np.asarray(r['o_gdn_p'], np.float32) for r in R], 1)
    gdn_s = np.concatenate([np.asarray(r['o_gdn_s'], np.float32) for r in R], 1)
    gc_p = np.stack([np.asarray(r['o_gc_p'], np.float32) for r in R], 1)
    gc_s = np.concatenate([np.asarray(r['o_gc_s'], np.float32).reshape(DEPTH, 16, 3, 768) for r in R], 1)
    sc_p = np.stack([np.asarray(r['o_sc_p'], np.float32) for r in R], 1)
    sc_s = np.concatenate([np.asarray(r['o_sc_s'], np.float32).reshape(DEPTH, 16, 2, 256) for r in R], 1)
    cv_s = np.concatenate([np.asarray(r['o_cv_s'], np.float32).reshape(DEPTH, 16, 8, 256) for r in R], 1)
    return (y_prompt, y_sample, gla_p, gla_s, gdn_p, gdn_s, gc_p, gc_s, sc_p, sc_s, cv_s)


def kernel(**inputs):
    TP = int(np.asarray(inputs['x_prompt']).shape[1])
    DEPTH = int(np.asarray(inputs['w_in']).shape[0])
    ncores = int(np.asarray(inputs['x_prompt']).shape[0])
    key = (TP, DEPTH)
    if key not in _NC_CACHE:
        _NC_CACHE[key] = build(TP=TP, TT=min(512, TP), DEPTH=DEPTH)
    nc = _NC_CACHE[key]
    maps = make_in_maps(inputs, TP, DEPTH, ncores)
    res = run_bass_kernel_spmd(nc, maps, core_ids=list(range(ncores)))
    return gather(res.results, TP, DEPTH, ncores)
```
